# Optimizing a Trainium2 kernel written in Bass

```python
import math
import jax, jax.numpy as jnp
from jax import lax
import numpy as np

D_MODEL = 1024
BATCH = 16
SEQ = 2048
DEPTH = 4

N_MIXERS = 2
RMS_EPS = 1e-6
FFN_HIDDEN = 2816
Q_BLOCK = 128
NEG = -1e30
MLA_HEADS = 16
MLA_Q_LORA = 384
MLA_KV_LORA = 256
MLA_NOPE = 64
MLA_ROPE = 32
MLA_V = 64
ROPE_THETA = 10000.0
NSA_HEADS = 16
NSA_GROUPS = 4
NSA_QK = 64
NSA_V = 64
CMP_BLOCK = 32
CMP_STRIDE = 16
CMP_HIDDEN = 128
SEL_BLOCK = 64
SEL_TOP_N = 16
WINDOW = 512
SEL_Q_CHUNK = 16
REL_BUCKETS = 32
REL_MAX_DIST = 128

kernel_name = "hybrid_mla_nsa_macaron"


def rmsnorm(x, g):
    x32 = x.astype(jnp.float32)
    y = x32 * lax.rsqrt(jnp.mean(x32 * x32, axis=-1, keepdims=True) + RMS_EPS)
    return y.astype(x.dtype) * g


def swiglu(x, w_gate, w_up, w_down):
    return (jax.nn.silu(x @ w_gate) * (x @ w_up)) @ w_down


def rope_angles(S):
    half = MLA_ROPE // 2
    inv = ROPE_THETA ** (-jnp.arange(half, dtype=jnp.float32) * 2.0 / MLA_ROPE)
    ang = jnp.arange(S, dtype=jnp.float32)[:, None] * inv[None, :]
    return jnp.cos(ang), jnp.sin(ang)


def apply_rotary(x, cos, sin):
    half = x.shape[-1] // 2
    cos = cos.astype(x.dtype)
    sin = sin.astype(x.dtype)
    x1, x2 = x[..., :half], x[..., half:]
    return jnp.concatenate([x1 * cos - x2 * sin, x1 * sin + x2 * cos], axis=-1)


def t5_bucket(dist):
    n = jnp.maximum(dist, 0)
    max_exact = REL_BUCKETS // 2
    nf = jnp.maximum(n, 1).astype(jnp.float32)
    large = max_exact + (jnp.log(nf / max_exact) / math.log(REL_MAX_DIST / max_exact)
                         * (REL_BUCKETS - max_exact)).astype(jnp.int32)
    large = jnp.minimum(large, REL_BUCKETS - 1)
    return jnp.where(n < max_exact, n, large)


def mla_mixer(h, w_in, q_norm, kv_norm, w_uq, w_ukv, w_o):
    B, S, _ = h.shape
    H = MLA_HEADS
    proj = h @ w_in
    c_q = proj[..., :MLA_Q_LORA]
    c_kv = proj[..., MLA_Q_LORA:MLA_Q_LORA + MLA_KV_LORA]
    k_rope = proj[..., MLA_Q_LORA + MLA_KV_LORA:]
    q = (rmsnorm(c_q, q_norm) @ w_uq).reshape(B, S, H, MLA_NOPE + MLA_ROPE)
    kv = (rmsnorm(c_kv, kv_norm) @ w_ukv).reshape(B, S, H, MLA_NOPE + MLA_V)
    cos, sin = rope_angles(S)
    q_nope = q[..., :MLA_NOPE]
    q_rope = apply_rotary(q[..., MLA_NOPE:], cos[:, None, :], sin[:, None, :])
    k_rope = apply_rotary(k_rope, cos, sin)
    k_nope = kv[..., :MLA_NOPE]
    v = kv[..., MLA_NOPE:]
    scale = (MLA_NOPE + MLA_ROPE) ** -0.5
    n_blk = S // Q_BLOCK
    qn_b = q_nope.reshape(B, n_blk, Q_BLOCK, H, MLA_NOPE).swapaxes(0, 1)
    qr_b = q_rope.reshape(B, n_blk, Q_BLOCK, H, MLA_ROPE).swapaxes(0, 1)
    starts = jnp.arange(n_blk) * Q_BLOCK
    kpos = jnp.arange(S)

    def block(args):
        qn, qr, s0 = args
        s = (jnp.einsum('bqhd,bkhd->bhqk', qn, k_nope)
             + jnp.einsum('bqhd,bkd->bhqk', qr, k_rope)).astype(jnp.float32) * scale
        qpos = s0 + jnp.arange(Q_BLOCK)
        p = jax.nn.softmax(jnp.where(kpos[None, :] <= qpos[:, None], s, NEG), axis=-1)
        return jnp.einsum('bhqk,bkhd->bqhd', p.astype(v.dtype), v)

    o = lax.map(block, (qn_b, qr_b, starts)).swapaxes(0, 1).reshape(B, S, H * MLA_V)
    return o @ w_o


def compress_blocks(x, blk_idx, pos_emb, w1, w2):
    B = x.shape[0]
    n_cmp, L = blk_idx.shape
    blocks = x[:, blk_idx] + pos_emb[None, None, :, None, :]
    flat = jnp.moveaxis(blocks, 3, 2).reshape(B, n_cmp, x.shape[2], L * x.shape[3])
    return jax.nn.gelu(flat @ w1) @ w2


def selection_overlap(n_cmp, n_sel):
    cs = np.arange(n_cmp) * CMP_STRIDE
    ce = cs + CMP_BLOCK
    ss = np.arange(n_sel) * SEL_BLOCK
    se = ss + SEL_BLOCK
    ov = np.minimum(ce[:, None], se[None, :]) - np.maximum(cs[:, None], ss[None, :])
    return (np.clip(ov, 0, None) / CMP_BLOCK).astype(np.float32)


def selected_attention(q, k, v, sel_idx, tbl, scale):
    B, S, G, R, dk = q.shape
    dv = v.shape[-1]
    n_sel = S // SEL_BLOCK
    n_top = sel_idx.shape[-1]
    kb = k.reshape(B, n_sel, SEL_BLOCK, G, dk).transpose(0, 3, 1, 2, 4)
    vb = v.reshape(B, n_sel, SEL_BLOCK, G, dv).transpose(0, 3, 1, 2, 4)
    C = SEL_Q_CHUNK
    n_ch = S // C
    q_ch = q.reshape(B, n_ch, C, G, R, dk).swapaxes(0, 1)
    idx_ch = sel_idx.reshape(B, G, n_ch, C, n_top).transpose(2, 0, 1, 3, 4)
    starts = jnp.arange(n_ch) * C
    gather = jax.vmap(jax.vmap(lambda blocks, ix: blocks[ix]))
    g_ar = jnp.arange(G)[None, :, None, None]
    K = n_top * SEL_BLOCK

    def chunk(args):
        q_i, ix, s0 = args
        kg = gather(kb, ix).reshape(B, G, C, K, dk)
        vg = gather(vb, ix).reshape(B, G, C, K, dv)
        kpos = (ix[..., None] * SEL_BLOCK + jnp.arange(SEL_BLOCK)).reshape(B, G, C, K)
        qpos = s0 + jnp.arange(C)
        dist = qpos[None, None, :, None] - kpos
        bias = tbl[t5_bucket(dist), g_ar]
        s = jnp.einsum('bcgrd,bgckd->bgcrk', q_i, kg).astype(jnp.float32) * scale \
            + jnp.moveaxis(bias, -1, 3).astype(jnp.float32)
        p = jax.nn.softmax(jnp.where((dist >= 0)[:, :, :, None, :], s, NEG), axis=-1)
        return jnp.einsum('bgcrk,bgckd->bcgrd', p.astype(vg.dtype), vg)

    out = lax.map(chunk, (q_ch, idx_ch, starts))
    return out.swapaxes(0, 1).reshape(B, S, G, R, dv)


def window_attention(q, k, v, tbl, scale):
    B, S, G, R, dk = q.shape
    n_blk = S // Q_BLOCK
    span = WINDOW + Q_BLOCK
    k_pad = jnp.pad(k, ((0, 0), (WINDOW, 0), (0, 0), (0, 0)))
    v_pad = jnp.pad(v, ((0, 0), (WINDOW, 0), (0, 0), (0, 0)))
    q_b = q.reshape(B, n_blk, Q_BLOCK, G, R, dk).swapaxes(0, 1)
    starts = jnp.arange(n_blk) * Q_BLOCK

    def block(args):
        q_i, s0 = args
        k_i = lax.dynamic_slice_in_dim(k_pad, s0, span, axis=1)
        v_i = lax.dynamic_slice_in_dim(v_pad, s0, span, axis=1)
        qpos = s0 + jnp.arange(Q_BLOCK)
        kpos = s0 - WINDOW + jnp.arange(span)
        dist = qpos[:, None] - kpos[None, :]
        mask = (dist >= 0) & (dist < WINDOW) & (kpos >= 0)[None, :]
        bias = tbl[t5_bucket(dist)].transpose(2, 3, 0, 1)
        s = jnp.einsum('bqgrd,bkgd->bgrqk', q_i, k_i).astype(jnp.float32) * scale + bias.astype(jnp.float32)
        p = jax.nn.softmax(jnp.where(mask, s, NEG), axis=-1)
        return jnp.einsum('bgrqk,bkgd->bqgrd', p.astype(v_i.dtype), v_i)

    out = lax.map(block, (q_b, starts))
    return out.swapaxes(0, 1).reshape(B, S, G, R, v.shape[-1])


def nsa_mixer(h, rel_bias, w_in, pos_k, w1_k, w2_k, pos_v, w1_v, w2_v, w_o):
    B, S, _ = h.shape
    H, G = NSA_HEADS, NSA_GROUPS
    R = H // G
    sizes = [H * NSA_QK, G * NSA_QK, G * NSA_V, G * NSA_QK, G * NSA_V, G * NSA_QK, G * NSA_V, H * 3]
    cuts = [int(c) for c in np.cumsum(sizes[:-1])]
    q, k_c, v_c, k_s, v_s, k_w, v_w, g = jnp.split(h @ w_in, cuts, axis=-1)
    q = q.reshape(B, S, G, R, NSA_QK)
    k_c, k_s, k_w = (t.reshape(B, S, G, NSA_QK) for t in (k_c, k_s, k_w))
    v_c, v_s, v_w = (t.reshape(B, S, G, NSA_V) for t in (v_c, v_s, v_w))
    gates = jax.nn.sigmoid(g.astype(jnp.float32)).astype(h.dtype).reshape(B, S, G, R, 3)
    tbl = rel_bias.reshape(REL_BUCKETS, G, R)
    scale = NSA_QK ** -0.5
    pos = jnp.arange(S)

    n_cmp = (S - CMP_BLOCK) // CMP_STRIDE + 1
    blk_idx = jnp.arange(n_cmp)[:, None] * CMP_STRIDE + jnp.arange(CMP_BLOCK)[None, :]
    kc = compress_blocks(k_c, blk_idx, pos_k, w1_k, w2_k)
    vc = compress_blocks(v_c, blk_idx, pos_v, w1_v, w2_v)
    cmp_end = jnp.arange(n_cmp) * CMP_STRIDE + CMP_BLOCK - 1
    dist = pos[:, None] - cmp_end[None, :]
    valid = dist >= 0
    bias = tbl[t5_bucket(dist)].transpose(2, 3, 0, 1)
    s = jnp.einsum('bsgrd,bngd->bgrsn', q, kc).astype(jnp.float32) * scale + bias.astype(jnp.float32)
    p_cmp = jax.nn.softmax(jnp.where(valid, s, NEG), axis=-1) * valid
    o_cmp = jnp.einsum('bgrsn,bngd->bsgrd', p_cmp.astype(vc.dtype), vc)

    n_sel = S // SEL_BLOCK
    overlap = jnp.asarray(selection_overlap(n_cmp, n_sel))
    imp = jnp.einsum('bgrsn,nj->bgsj', p_cmp, overlap)
    blk_t = (pos // SEL_BLOCK)[:, None]
    j = jnp.arange(n_sel)[None, :]
    forced = (j == 0) | (j == blk_t) | (j == blk_t - 1)
    score = jnp.where(forced, 1e6, jnp.where(j <= blk_t, imp, -1e6))
    n_top = min(SEL_TOP_N, n_sel)
    _, sel_idx = lax.top_k(score, n_top)
    o_sel = selected_attention(q, k_s, v_s, sel_idx, tbl, scale)

    o_win = window_attention(q, k_w, v_w, tbl, scale)

    o = gates[..., 0:1] * o_cmp + gates[..., 1:2] * o_sel + gates[..., 2:3] * o_win
    return o.reshape(B, S, H * NSA_V) @ w_o


def setup_inputs(seed: int = 0) -> dict:
    key = jax.random.key(seed)
    ks = iter(jax.random.split(key, 32))
    n_mla = len(range(0, DEPTH, N_MIXERS))
    n_nsa = DEPTH - n_mla

    def dense(shape, fan_in):
        return jax.random.normal(next(ks), shape, jnp.float32) * fan_in ** -0.5

    def gain(shape):
        return 1.0 + 0.01 * jax.random.normal(next(ks), shape, jnp.float32)

    def small(shape, s):
        return s * jax.random.normal(next(ks), shape, jnp.float32)

    mla_in = MLA_Q_LORA + MLA_KV_LORA + MLA_ROPE
    nsa_in = NSA_HEADS * NSA_QK + 3 * NSA_GROUPS * (NSA_QK + NSA_V) + 3 * NSA_HEADS
    return {
        "x": jax.random.normal(next(ks), (BATCH, SEQ, D_MODEL), jnp.float32),
        "ffn_norm_a": gain((DEPTH, D_MODEL)),
        "ffn_a_w_gate": dense((DEPTH, D_MODEL, FFN_HIDDEN), D_MODEL),
        "ffn_a_w_up": dense((DEPTH, D_MODEL, FFN_HIDDEN), D_MODEL),
        "ffn_a_w_down": dense((DEPTH, FFN_HIDDEN, D_MODEL), FFN_HIDDEN),
        "mix_norm": gain((DEPTH, D_MODEL)),
        "ffn_norm_b": gain((DEPTH, D_MODEL)),
        "ffn_b_w_gate": dense((DEPTH, D_MODEL, FFN_HIDDEN), D_MODEL),
        "ffn_b_w_up": dense((DEPTH, D_MODEL, FFN_HIDDEN), D_MODEL),
        "ffn_b_w_down": dense((DEPTH, FFN_HIDDEN, D_MODEL), FFN_HIDDEN),
        "final_norm": gain((D_MODEL,)),
        "rel_bias": small((REL_BUCKETS, NSA_HEADS), 0.5),
        "mla_w_in": dense((n_mla, D_MODEL, mla_in), D_MODEL),
        "mla_q_norm": gain((n_mla, MLA_Q_LORA)),
        "mla_kv_norm": gain((n_mla, MLA_KV_LORA)),
        "mla_w_uq": dense((n_mla, MLA_Q_LORA, MLA_HEADS * (MLA_NOPE + MLA_ROPE)), MLA_Q_LORA),
        "mla_w_ukv": dense((n_mla, MLA_KV_LORA, MLA_HEADS * (MLA_NOPE + MLA_V)), MLA_KV_LORA),
        "mla_w_o": dense((n_mla, MLA_HEADS * MLA_V, D_MODEL), MLA_HEADS * MLA_V),
        "nsa_w_in": dense((n_nsa, D_MODEL, nsa_in), D_MODEL),
        "nsa_cmp_pos_k": small((n_nsa, CMP_BLOCK, NSA_QK), 0.1),
        "nsa_cmp_w1_k": dense((n_nsa, CMP_BLOCK * NSA_QK, CMP_HIDDEN), CMP_BLOCK * NSA_QK),
        "nsa_cmp_w2_k": dense((n_nsa, CMP_HIDDEN, NSA_QK), CMP_HIDDEN),
        "nsa_cmp_pos_v": small((n_nsa, CMP_BLOCK, NSA_V), 0.1),
        "nsa_cmp_w1_v": dense((n_nsa, CMP_BLOCK * NSA_V, CMP_HIDDEN), CMP_BLOCK * NSA_V),
        "nsa_cmp_w2_v": dense((n_nsa, CMP_HIDDEN, NSA_V), CMP_HIDDEN),
        "nsa_w_o": dense((n_nsa, NSA_HEADS * NSA_V, D_MODEL), NSA_HEADS * NSA_V),
    }


def reference(x, ffn_norm_a, ffn_a_w_gate, ffn_a_w_up, ffn_a_w_down, mix_norm, ffn_norm_b,
              ffn_b_w_gate, ffn_b_w_up, ffn_b_w_down, final_norm, rel_bias,
              mla_w_in, mla_q_norm, mla_kv_norm, mla_w_uq, mla_w_ukv, mla_w_o,
              nsa_w_in, nsa_cmp_pos_k, nsa_cmp_w1_k, nsa_cmp_w2_k,
              nsa_cmp_pos_v, nsa_cmp_w1_v, nsa_cmp_w2_v, nsa_w_o):
    h = x
    for i in range(DEPTH):
        h = h + 0.5 * swiglu(rmsnorm(h, ffn_norm_a[i]), ffn_a_w_gate[i], ffn_a_w_up[i], ffn_a_w_down[i])
        m = rmsnorm(h, mix_norm[i])
        j = i // N_MIXERS
        if i % N_MIXERS == 0:
            h = h + mla_mixer(m, mla_w_in[j], mla_q_norm[j], mla_kv_norm[j],
                              mla_w_uq[j], mla_w_ukv[j], mla_w_o[j])
        else:
            h = h + nsa_mixer(m, rel_bias, nsa_w_in[j], nsa_cmp_pos_k[j], nsa_cmp_w1_k[j], nsa_cmp_w2_k[j],
                              nsa_cmp_pos_v[j], nsa_cmp_w1_v[j], nsa_cmp_w2_v[j], nsa_w_o[j])
        h = h + 0.5 * swiglu(rmsnorm(h, ffn_norm_b[i]), ffn_b_w_gate[i], ffn_b_w_up[i], ffn_b_w_down[i])
    return rmsnorm(h, final_norm)
```

```python
import math
from contextlib import ExitStack

import numpy as np
import ml_dtypes

import concourse.bass as bass
import concourse.mybir as mybir
from concourse.bass_utils import run_bass_kernel_spmd

F32 = mybir.dt.float32
BF16 = mybir.dt.bfloat16
AF = mybir.ActivationFunctionType
ALU = mybir.AluOpType

N_CORES = 8
D = 1024
S = 2048
NSEQ = 2
T = NSEQ * S
FF = 2816
NHC = FF // 128
DEPTH = 4
EPS = 1e-6
NEGM = -16384.0

ENGS = ["pe", "act", "dve", "pool", "sp"]
SELF_SYNC = {"pe": False, "act": True, "dve": True, "pool": True, "sp": False}


class Buf:
    __slots__ = ("name", "base", "w", "r")

    def __init__(self, name, base):
        self.name = name
        self.base = base
        self.w = []
        self.r = []


class Prog:
    def __init__(self, nc, es):
        self.nc = nc
        self.es = es
        self.q = {e: [] for e in ENGS}
        self.cnt = {e: 0 for e in ENGS}
        self.sems = {e: es.enter_context(nc.semaphore("s_" + e)) for e in ENGS if e != "sp"}
        self.dcnt = {}
        self.nbuf = 0
        self.ninst = 0

    def buf(self, name=None):
        self.nbuf += 1
        return Buf("%s_%d" % (name or "b", self.nbuf), name or "b")

    def bufs(self, n, name="b"):
        return [self.buf("%s%d" % (name, i)) for i in range(n)]

    def _deps(self, reads, writes):
        waits = []
        for b in reads:
            waits += b.w
        for b in writes:
            waits += b.w
            waits += b.r
        return waits

    def _commit(self, ev, reads, writes):
        for b in reads:
            b.r.append(ev)
        for b in writes:
            b.w = [ev]
            b.r = []

    def op(self, eng, fn, reads=(), writes=(), inc=True):
        waits = self._deps(reads, writes)
        idx = self.cnt[eng] + 1
        if inc:
            self.cnt[eng] = idx
        self._commit((eng, idx), reads, writes)
        self.q[eng].append((fn, waits, (eng, 1) if inc else None))
        self.ninst += 1

    def dma(self, eng, pairs, reads=(), writes=(), key=None, **kw):
        key = "d_" + (key or writes[0].base)
        if key not in self.sems:
            self.sems[key] = self.es.enter_context(self.nc.semaphore(key))
            self.dcnt[key] = 0
        waits = self._deps(reads, writes)
        self.dcnt[key] += 16 * len(pairs)
        self._commit((key, self.dcnt[key]), reads, writes)
        for i, (o, a) in enumerate(pairs):
            self.q[eng].append(((lambda e, o=o, a=a: e.dma_start(out=o, in_=a, **kw)),
                                waits if i == 0 else [], (key, 16)))
            self.ninst += 1

    def barrier(self):
        evs = [(e, self.cnt[e]) for e in ENGS if e != "sp" and self.cnt[e] > 0]
        evs += [(k, v) for k, v in self.dcnt.items() if v > 0]
        for e in ENGS:
            self.q[e].append((None, list(evs), None))

    def replay(self, eng, e):
        waited = {}
        for fn, waits, inc in self.q[eng]:
            need = {}
            for k, v in waits:
                if k == eng and not SELF_SYNC[eng]:
                    continue
                if v > need.get(k, 0):
                    need[k] = v
            for k, v in need.items():
                if waited.get(k, 0) < v:
                    e.wait_ge(self.sems[k], v)
                    waited[k] = v
            if fn is not None:
                ins = fn(e)
                if inc is not None:
                    ins.then_inc(self.sems[inc[0]], inc[1])

    def run_block(self):
        for e in ENGS:
            assert self.cnt[e] < 65000, (e, self.cnt[e])
        with self.nc.Block() as block:
            @block.sync
            def _(e):
                self.replay("sp", e)

            @block.tensor
            def _(e):
                self.replay("pe", e)

            @block.scalar
            def _(e):
                self.replay("act", e)

            @block.vector
            def _(e):
                self.replay("dve", e)

            @block.gpsimd
            def _(e):
                self.replay("pool", e)


class Arena:
    def __init__(self, big, words):
        self.big = big
        self.words = words
        self.top = 0
        self.marks = []
        self.peak = 0

    def alloc(self, shape_free, dtype, parts=128):
        n = int(np.prod(shape_free))
        nb = 2 if dtype == BF16 else 4
        w = (n * nb + 3) // 4
        w = (w + 7) // 8 * 8
        assert self.top + w <= self.words, ("SBUF arena overflow", self.top, w, self.words)
        ap = self.big[0:parts, self.top:self.top + w]
        self.top += w
        self.peak = max(self.peak, self.top)
        if dtype != F32:
            ap = ap.bitcast(dtype)
        ap = ap[:, 0:n]
        if len(shape_free) > 1:
            names = " ".join("d%d" % i for i in range(len(shape_free)))
            kw = {"d%d" % i: int(s) for i, s in enumerate(shape_free)}
            ap = ap.rearrange("p (%s) -> p %s" % (names, names), **kw)
        return ap

    def mark(self):
        self.marks.append(self.top)

    def release(self):
        self.top = self.marks.pop()


class Ctx:
    dbg_stop = None


class StopBuild(Exception):
    pass


def dbg(cx, level):
    if cx.dbg_stop is not None and cx.dbg_stop == level:
        cx.P.barrier()
        raise StopBuild()


def rms_rstd(cx, x_ap, bx, n, ss, bss, col, junk, bjunk):
    P = cx.P
    c = ss[:, col:col + 1]
    P.op("act", lambda e: e.activation(out=junk[:, 0:n], in_=x_ap, func=AF.Square, accum_out=c),
         reads=[bx], writes=[bjunk, bss])
    P.op("dve", lambda e: e.tensor_scalar(out=c, in0=c, scalar1=1.0 / n, scalar2=EPS, op0=ALU.mult, op1=ALU.add),
         reads=[bss], writes=[bss])
    P.op("act", lambda e: e.activation(out=c, in_=c, func=AF.Ln), reads=[bss], writes=[bss])
    P.op("act", lambda e: e.activation(out=c, in_=c, func=AF.Exp, scale=-0.5), reads=[bss], writes=[bss])


def load_gain(cx, vec_ap, n, name="gain"):
    P, A = cx.P, cx.A
    g = A.alloc([n], F32)
    bg = P.buf(name)
    P.dma("sp", [(g, vec_ap.partition_broadcast(128))], writes=[bg])
    return g, bg


def ffn_phase(cx, li, which, src, dst):
    P, A, ps, pb = cx.P, cx.A, cx.ps, cx.pb
    A.mark()
    Wg_d = cx.inp["ffn_%s_w_gate" % which][li]
    Wu_d = cx.inp["ffn_%s_w_up" % which][li]
    Wd_d = cx.inp["ffn_%s_w_down" % which][li]
    gn_d = cx.inp["ffn_norm_%s" % which][li]
    wg = A.alloc([8, FF], BF16)
    wu = A.alloc([8, FF], BF16)
    wd = A.alloc([NHC, D], BF16)
    CG = [(0, 768), (768, 1536), (1536, 2304), (2304, 2816)]
    bwg, bwu, bwd = P.bufs(4, "wg"), P.bufs(4, "wu"), P.bufs(4, "wd")
    for gi, (c0, c1) in enumerate(CG):
        P.dma("pool", [(wg[:, :, c0:c1], Wg_d[:, c0:c1].rearrange("(kc p) c -> p kc c", p=128))], writes=[bwg[gi]],
              key="wg%d" % gi)
        P.dma("pool", [(wu[:, :, c0:c1], Wu_d[:, c0:c1].rearrange("(kc p) c -> p kc c", p=128))], writes=[bwu[gi]],
              key="wu%d" % gi)
    for gi, (c0, c1) in enumerate(CG):
        P.dma("pool", [(wd[:, c0 // 128:c1 // 128, :], Wd_d[c0:c1, :].rearrange("(hc p) c -> p hc c", p=128))],
              writes=[bwd[gi]], key="wd%d" % gi)
    gain, bgain = load_gain(cx, gn_d, D)
    hn = [A.alloc([D], F32) for _ in range(2)]
    bhn = P.bufs(2, "hn")
    junk = A.alloc([D], BF16)
    bjunk = P.buf("junk")
    ss = A.alloc([8], F32)
    bss = P.bufs(8, "ss")
    xn = A.alloc([4, D], BF16)
    bxn = P.bufs(4, "xn")
    xT = A.alloc([8, 512], BF16)
    bxT = P.buf("xT")
    sg = [A.alloc([512], BF16) for _ in range(2)]
    bsg = P.bufs(2, "sg")
    hT = A.alloc([NHC, 512], BF16)
    bhT = P.bufs(NHC, "hT")
    hr = [A.alloc([D], F32) for _ in range(2)]
    bhr = P.bufs(2, "hr")
    psA, psB, psD, psT = ps[0:2], ps[2:4], ps[4:6], ps[6]
    bA, bB, bD, bT = pb[0:2], pb[2:4], pb[4:6], pb[6]
    ident = cx.ident
    cnt = {"n": 0, "g": 0, "d": 0}

    def norm_a(tt):
        for j in range(4):
            n = cnt["n"]
            cnt["n"] += 1
            sl = n % 2
            r0 = tt * 512 + j * 128
            P.dma("sp", [(hn[sl], src[r0:r0 + 128, :])], writes=[bhn[sl]], key="hn%d" % sl)
            c = n % 8
            rms_rstd(cx, hn[sl], bhn[sl], D, ss, bss[c], c, junk, bjunk)
            P.op("dve", lambda e, sl=sl, c=c, j=j: e.scalar_tensor_tensor(
                out=xn[:, j, :], in0=hn[sl], scalar=ss[:, c:c + 1], in1=gain, op0=ALU.mult, op1=ALU.mult),
                 reads=[bhn[sl], bss[c], bgain], writes=[bxn[j]])

    def norm_b(tt):
        pst = psT.bitcast(BF16)
        for j in range(4):
            for kc in range(8):
                P.op("pe", lambda e, j=j, kc=kc: e.transpose(out=pst[:, kc * 128:(kc + 1) * 128],
                                                               in_=xn[:, j, kc * 128:(kc + 1) * 128], identity=ident),
                     reads=[bxn[j], cx.bident], writes=[bT], inc=(kc == 7))
            P.op("act", lambda e, j=j: e.copy(out=xT[:, :, j * 128:(j + 1) * 128],
                                              in_=pst.rearrange("p (k t) -> p k t", k=8)),
                 reads=[bT], writes=[bxT])

    def gateup(tt):
        for hc in range(NHC):
            g = cnt["g"]
            cnt["g"] += 1
            s2 = g % 2
            gi = min(hc // 6, 3)
            for kc in range(8):
                P.op("pe", lambda e, hc=hc, kc=kc, s2=s2: e.matmul(psA[s2], lhsT=wg[:, kc, hc * 128:(hc + 1) * 128],
                                                                    rhs=xT[:, kc, :], start=(kc == 0), stop=(kc == 7)),
                     reads=[bxT, bwg[gi]], writes=[bA[s2]], inc=(kc == 7))
            for kc in range(8):
                P.op("pe", lambda e, hc=hc, kc=kc, s2=s2: e.matmul(psB[s2], lhsT=wu[:, kc, hc * 128:(hc + 1) * 128],
                                                                    rhs=xT[:, kc, :], start=(kc == 0), stop=(kc == 7)),
                     reads=[bxT, bwu[gi]], writes=[bB[s2]], inc=(kc == 7))
            P.op("act", lambda e, s2=s2: e.activation(out=sg[s2], in_=psA[s2], func=AF.Silu),
                 reads=[bA[s2]], writes=[bsg[s2]])
            P.op("dve", lambda e, s2=s2, hc=hc: e.tensor_tensor(out=hT[:, hc, :], in0=sg[s2], in1=psB[s2], op=ALU.mult),
                 reads=[bsg[s2], bB[s2]], writes=[bhT[hc]])

    def down(tt):
        for j in range(4):
            r0 = tt * 512 + j * 128
            sl = (tt * 4 + j) % 2
            P.dma("sp", [(hr[sl], src[r0:r0 + 128, :])], writes=[bhr[sl]], key="hr%d" % sl)
            for half in range(2):
                d = cnt["d"]
                cnt["d"] += 1
                s2 = d % 2
                for hc in range(NHC):
                    gi = min(hc // 6, 3)
                    P.op("pe", lambda e, hc=hc, j=j, half=half, s2=s2: e.matmul(
                        psD[s2], lhsT=hT[:, hc, j * 128:(j + 1) * 128], rhs=wd[:, hc, half * 512:(half + 1) * 512],
                        start=(hc == 0), stop=(hc == NHC - 1)),
                         reads=[bhT[hc], bwd[gi]], writes=[bD[s2]], inc=(hc == NHC - 1))
                P.op("dve", lambda e, sl=sl, half=half, s2=s2: e.scalar_tensor_tensor(
                    out=hr[sl][:, half * 512:(half + 1) * 512], in0=psD[s2], scalar=0.5,
                    in1=hr[sl][:, half * 512:(half + 1) * 512], op0=ALU.mult, op1=ALU.add),
                     reads=[bD[s2], bhr[sl]], writes=[bhr[sl]])
            P.dma("sp", [(dst[r0:r0 + 128, :], hr[sl])], reads=[bhr[sl]], key="hrst%d" % sl)

    NT = T // 512
    norm_a(0)
    norm_b(0)
    for tt in range(NT):
        gateup(tt)
        if tt + 1 < NT:
            norm_a(tt + 1)
        down(tt)
        if tt + 1 < NT:
            norm_b(tt + 1)
    P.barrier()
    A.release()


def norm_T_tile(cx, src, r0, gain, bgain, hn, bhn, sl, ss, bss, c, junk, bjunk, xn, bxn, psT, bT, dstT, bdstT, col0):
    P = cx.P
    P.dma("sp", [(hn[sl], src[r0:r0 + 128, :])], writes=[bhn[sl]], key="mhn%d" % sl)
    rms_rstd(cx, hn[sl], bhn[sl], D, ss, bss[c], c, junk, bjunk)
    P.op("dve", lambda e: e.scalar_tensor_tensor(out=xn[sl], in0=hn[sl], scalar=ss[:, c:c + 1], in1=gain,
                                                 op0=ALU.mult, op1=ALU.mult),
         reads=[bhn[sl], bss[c], bgain], writes=[bxn[sl]])
    pst = psT.bitcast(BF16)
    for kc in range(8):
        P.op("pe", lambda e, kc=kc: e.transpose(out=pst[:, kc * 128:(kc + 1) * 128], in_=xn[sl][:, kc * 128:(kc + 1) * 128],
                                                identity=cx.ident),
             reads=[bxn[sl], cx.bident], writes=[bT], inc=(kc == 7))
    P.op("act", lambda e: e.copy(out=dstT[:, 0:8, col0:col0 + 128], in_=pst.rearrange("p (k t) -> p k t", k=8)),
         reads=[bT], writes=[bdstT])


def out_proj_tile(cx, o_tile, bo, wo, bwo, src, dst, r0, oT, boT, hr, bhr, sl, psT, bT, psW, bW):
    P = cx.P
    pst = psT.bitcast(BF16)
    for kc in range(8):
        P.op("pe", lambda e, kc=kc: e.transpose(out=pst[:, kc * 128:(kc + 1) * 128], in_=o_tile[:, kc * 128:(kc + 1) * 128],
                                                identity=cx.ident),
             reads=[bo, cx.bident], writes=[bT], inc=(kc == 7))
    P.op("act", lambda e: e.copy(out=oT[sl], in_=pst.rearrange("p (k t) -> p k t", k=8)), reads=[bT], writes=[boT[sl]])
    P.dma("sp", [(hr[sl], src[r0:r0 + 128, :])], writes=[bhr[sl]], key="mhr%d" % sl)
    for half in range(2):
        for kc in range(8):
            P.op("pe", lambda e, kc=kc, half=half: e.matmul(psW[half], lhsT=oT[sl][:, kc, :],
                                                            rhs=wo[:, kc, half * 512:(half + 1) * 512],
                                                            start=(kc == 0), stop=(kc == 7)),
                 reads=[boT[sl], bwo], writes=[bW[half]], inc=(kc == 7))
        P.op("dve", lambda e, half=half: e.tensor_tensor(out=hr[sl][:, half * 512:(half + 1) * 512], in0=psW[half],
                                                         in1=hr[sl][:, half * 512:(half + 1) * 512], op=ALU.add),
             reads=[bW[half], bhr[sl]], writes=[bhr[sl]])
    P.dma("sp", [(dst[r0:r0 + 128, :], hr[sl])], reads=[bhr[sl]], key="mhrst%d" % sl)


def mla_phase(cx, j, li, src, dst):
    P, A, ps, pb = cx.P, cx.A, cx.ps, cx.pb
    A.mark()
    scale = 96.0 ** -0.5
    Win_d, Wuq_d, Wukv_d, Wo_d = cx.inp["mla_w_in"][j], cx.inp["mla_w_uq"][j], cx.inp["mla_w_ukv"][j], cx.inp["mla_w_o"][j]
    win = A.alloc([8, 672], BF16)
    wkr = A.alloc([8, 96], BF16)
    wq = A.alloc([3, 1536], BF16)
    wqs = A.alloc([3, 16, 96], BF16)
    wkn = A.alloc([2, 16, 64], BF16)
    wv = A.alloc([2, 16, 64], BF16)
    wo = A.alloc([8, 1024], BF16)
    bwin, bwkr, bwq, bwqs, bwkn, bwv, bwo = (P.buf(n) for n in ["win", "wkr", "wq", "wqs", "wkn", "wv", "wo"])
    P.dma("pool", [(win, Win_d.rearrange("(kc p) c -> p kc c", p=128))], writes=[bwin])
    w_in_r = Win_d.rearrange("(kc p) c -> p kc c", p=128)
    P.op("dve", lambda e: e.memset(wkr, 0.0), writes=[bwkr])
    P.dma("pool", [(wkr[:, :, 64:80], w_in_r[:, :, 656:672]), (wkr[:, :, 80:96], w_in_r[:, :, 640:656])], writes=[bwkr])
    P.op("act", lambda e: e.mul(out=wkr[:, :, 64:80], in_=wkr[:, :, 64:80], mul=-1.0), reads=[bwkr], writes=[bwkr])
    wuq_r = Wuq_d.rearrange("(kc p) (h d) -> p kc h d", p=128, d=96)
    P.dma("pool", [(wq, Wuq_d.rearrange("(kc p) c -> p kc c", p=128))], writes=[bwq])
    P.op("dve", lambda e: e.memset(wqs, 0.0), writes=[bwqs])
    for kc in range(3):
        P.dma("pool", [(wqs[:, kc, :, 64:80], wuq_r[:, kc, :, 80:96]), (wqs[:, kc, :, 80:96], wuq_r[:, kc, :, 64:80])],
              writes=[bwqs], key="wqs")
    P.op("act", lambda e: e.mul(out=wqs[:, :, :, 64:80], in_=wqs[:, :, :, 64:80], mul=-1.0), reads=[bwqs], writes=[bwqs])
    wukv_r = Wukv_d.rearrange("(kc p) (h d) -> p kc h d", p=128, d=128)
    for kc in range(2):
        P.dma("pool", [(wkn[:, kc, :, :], wukv_r[:, kc, :, 0:64])], writes=[bwkn], key="wkn")
        P.dma("pool", [(wv[:, kc, :, :], wukv_r[:, kc, :, 64:128])], writes=[bwv], key="wv")
    P.dma("pool", [(wo, Wo_d.rearrange("(kc p) c -> p kc c", p=128))], writes=[bwo])
    gain, bgain = load_gain(cx, cx.inp["mix_norm"][li], D)
    gq, bgq = load_gain(cx, cx.inp["mla_q_norm"][j], 384, "gainq")
    gkv, bgkv = load_gain(cx, cx.inp["mla_kv_norm"][j], 256, "gainkv")
    CC = A.alloc([S], F32)
    SS = A.alloc([S], F32)
    bcs = P.buf("cs")
    P.dma("sp", [(CC[64:96, :], cx.cd["c_rope_cos"]), (SS[64:96, :], cx.cd["c_rope_sin"])], writes=[bcs])
    mdiag = A.alloc([128], F32)
    bmd = P.buf("mdiag")
    P.dma("sp", [(mdiag, cx.cd["c_mdiag"])], writes=[bmd])
    junk = A.alloc([D], BF16)
    bjunk = P.buf("junk")
    ss = A.alloc([8], F32)
    bss = P.bufs(8, "ss")
    ssq = A.alloc([8], F32)
    bssq = P.bufs(8, "ssq")
    cqnT = A.alloc([3, S], BF16)
    ckvnT = A.alloc([2, S], BF16)
    bcqnT, bckvnT = P.buf("cqnT"), P.buf("ckvnT")
    kT = [A.alloc([S], BF16) for _ in range(2)]
    bkT = P.bufs(2, "kT")
    o_all = A.alloc([16, D], BF16)
    bo_all = P.bufs(16, "o_all")
    psT, bT = ps[6], pb[6]

    for sq in range(NSEQ):
        tok0 = sq * S
        A.mark()
        hn = [A.alloc([D], F32) for _ in range(2)]
        bhn = P.bufs(2, "hn")
        xn = [A.alloc([D], BF16) for _ in range(2)]
        bxn = P.bufs(2, "xn")
        mT = A.alloc([8, 512], BF16)
        bmT = P.buf("mT")
        cqn = [A.alloc([384], BF16) for _ in range(2)]
        ckvn = [A.alloc([256], BF16) for _ in range(2)]
        bcqn, bckvn = P.bufs(2, "cqn"), P.bufs(2, "ckvn")
        tmpa = A.alloc([512], F32)
        tmpb = A.alloc([512], F32)
        btmpa, btmpb = P.buf("tmpa"), P.buf("tmpb")
        n = 0
        for c in range(4):
            for jj in range(4):
                norm_T_tile(cx, src, tok0 + c * 512 + jj * 128, gain, bgain, hn, bhn, n % 2, ss, bss, n % 8, junk, bjunk,
                            xn, bxn, psT, bT, mT, bmT, jj * 128)
                n += 1
            for kc in range(8):
                P.op("pe", lambda e, kc=kc: e.matmul(ps[4][0:96, :], lhsT=win[:, kc, 576:672], rhs=mT[:, kc, :],
                                                     start=(kc == 0), stop=(kc == 7)),
                     reads=[bwin, bmT], writes=[pb[4]], inc=(kc == 7))
            for kc in range(8):
                P.op("pe", lambda e, kc=kc: e.matmul(ps[5][0:96, :], lhsT=wkr[:, kc, :], rhs=mT[:, kc, :],
                                                     start=(kc == 0), stop=(kc == 7)),
                     reads=[bwkr, bmT], writes=[pb[5]], inc=(kc == 7))
            cs = slice(c * 512, (c + 1) * 512)
            P.op("dve", lambda e, cs=cs: e.tensor_tensor(out=tmpa[64:96, :], in0=ps[4][64:96, :], in1=CC[64:96, cs], op=ALU.mult),
                 reads=[pb[4], bcs], writes=[btmpa])
            P.op("dve", lambda e, cs=cs: e.tensor_tensor(out=tmpb[64:96, :], in0=ps[5][64:96, :], in1=SS[64:96, cs], op=ALU.mult),
                 reads=[pb[5], bcs], writes=[btmpb])
            for b2 in range(2):
                P.op("dve", lambda e, cs=cs, b2=b2: e.tensor_tensor(out=kT[b2][64:96, cs], in0=tmpa[64:96, :], in1=tmpb[64:96, :],
                                                                    op=ALU.add),
                     reads=[btmpa, btmpb], writes=[bkT[b2]])
            for jj in range(4):
                tsl = slice(jj * 128, (jj + 1) * 128)
                s2 = jj % 2
                for kc in range(8):
                    P.op("pe", lambda e, kc=kc, tsl=tsl, s2=s2: e.matmul(ps[s2][:, 0:384], lhsT=mT[:, kc, tsl],
                                                                        rhs=win[:, kc, 0:384], start=(kc == 0), stop=(kc == 7)),
                         reads=[bwin, bmT], writes=[pb[s2]], inc=(kc == 7))
                for kc in range(8):
                    P.op("pe", lambda e, kc=kc, tsl=tsl, s2=s2: e.matmul(ps[2 + s2][:, 0:256], lhsT=mT[:, kc, tsl],
                                                                        rhs=win[:, kc, 384:640], start=(kc == 0), stop=(kc == 7)),
                         reads=[bwin, bmT], writes=[pb[2 + s2]], inc=(kc == 7))
                cq = (c * 4 + jj) % 8
                rms_rstd(cx, ps[s2][:, 0:384], pb[s2], 384, ssq, bssq[cq], cq, junk, bjunk)
                P.op("dve", lambda e, s2=s2, cq=cq: e.scalar_tensor_tensor(out=cqn[s2], in0=ps[s2][:, 0:384],
                                                                           scalar=ssq[:, cq:cq + 1], in1=gq,
                                                                           op0=ALU.mult, op1=ALU.mult),
                     reads=[pb[s2], bssq[cq], bgq], writes=[bcqn[s2]])
                rms_rstd(cx, ps[2 + s2][:, 0:256], pb[2 + s2], 256, ss, bss[cq], cq, junk, bjunk)
                P.op("dve", lambda e, s2=s2, cq=cq: e.scalar_tensor_tensor(out=ckvn[s2], in0=ps[2 + s2][:, 0:256],
                                                                           scalar=ss[:, cq:cq + 1], in1=gkv,
                                                                           op0=ALU.mult, op1=ALU.mult),
                     reads=[pb[2 + s2], bss[cq], bgkv], writes=[bckvn[s2]])
                pst = psT.bitcast(BF16)
                for kc in range(3):
                    P.op("pe", lambda e, kc=kc, s2=s2: e.transpose(out=pst[:, kc * 128:(kc + 1) * 128],
                                                                   in_=cqn[s2][:, kc * 128:(kc + 1) * 128], identity=cx.ident),
                         reads=[bcqn[s2], cx.bident], writes=[bT], inc=False)
                for kc in range(2):
                    P.op("pe", lambda e, kc=kc, s2=s2: e.transpose(out=pst[:, (3 + kc) * 128:(4 + kc) * 128],
                                                                   in_=ckvn[s2][:, kc * 128:(kc + 1) * 128], identity=cx.ident),
                         reads=[bckvn[s2], cx.bident], writes=[bT], inc=(kc == 1))
                g0 = c * 512 + jj * 128
                P.op("act", lambda e, g0=g0: e.copy(out=cqnT[:, :, g0:g0 + 128],
                                                    in_=pst[:, 0:384].rearrange("p (k t) -> p k t", k=3)),
                     reads=[bT], writes=[bcqnT])
                P.op("act", lambda e, g0=g0: e.copy(out=ckvnT[:, :, g0:g0 + 128],
                                                    in_=pst[:, 384:640].rearrange("p (k t) -> p k t", k=2)),
                     reads=[bT], writes=[bckvnT])
        P.barrier()
        A.release()
        A.mark()
        qT = [A.alloc([S], BF16) for _ in range(2)]
        bqT = P.bufs(2, "qT")
        vaug = [A.alloc([16, 65], BF16) for _ in range(2)]
        bva = P.bufs(2, "vaug")
        for b2 in range(2):
            P.op("dve", lambda e, b2=b2: e.memset(vaug[b2][:, :, 64:65], 1.0), writes=[bva[b2]])
        E = [A.alloc([512], BF16) for _ in range(4)]
        bE = P.bufs(4, "E")
        tmpd = [A.alloc([128], F32) for _ in range(2)]
        btd = P.bufs(2, "tmpd")
        tq1 = A.alloc([512], F32)
        tq2 = A.alloc([512], F32)
        btq1, btq2 = P.buf("tq1"), P.buf("tq2")
        rec = A.alloc([8], F32)
        brec = P.bufs(2, "rec")
        nS = 0
        nE = 0
        nD = 0
        for h in range(16):
            hb = h % 2
            for c in range(4):
                cs = slice(c * 512, (c + 1) * 512)
                for kc in range(3):
                    P.op("pe", lambda e, kc=kc, cs=cs, h=h: e.matmul(ps[2][0:96, :], lhsT=wq[:, kc, h * 96:(h + 1) * 96], rhs=cqnT[:, kc, cs],
                                                                    start=(kc == 0), stop=(kc == 2)),
                         reads=[bwq, bcqnT], writes=[pb[2]], inc=(kc == 2))
                for kc in range(3):
                    P.op("pe", lambda e, kc=kc, cs=cs, h=h: e.matmul(ps[3][0:96, :], lhsT=wqs[:, kc, h, :], rhs=cqnT[:, kc, cs],
                                                                    start=(kc == 0), stop=(kc == 2)),
                         reads=[bwqs, bcqnT], writes=[pb[3]], inc=(kc == 2))
                for kc in range(2):
                    P.op("pe", lambda e, kc=kc, cs=cs, h=h: e.matmul(ps[4][0:64, :], lhsT=wkn[:, kc, h, :], rhs=ckvnT[:, kc, cs],
                                                                    start=(kc == 0), stop=(kc == 1)),
                         reads=[bwkn, bckvnT], writes=[pb[4]], inc=(kc == 1))
                P.op("act", lambda e, cs=cs, hb=hb: e.copy(out=qT[hb][0:64, cs], in_=ps[2][0:64, :]),
                     reads=[pb[2]], writes=[bqT[hb]])
                P.op("dve", lambda e, cs=cs: e.tensor_tensor(out=tq1[64:96, :], in0=ps[2][64:96, :], in1=CC[64:96, cs], op=ALU.mult),
                     reads=[pb[2], bcs], writes=[btq1])
                P.op("dve", lambda e, cs=cs: e.tensor_tensor(out=tq2[64:96, :], in0=ps[3][64:96, :], in1=SS[64:96, cs], op=ALU.mult),
                     reads=[pb[3], bcs], writes=[btq2])
                P.op("dve", lambda e, cs=cs, hb=hb: e.tensor_tensor(out=qT[hb][64:96, cs], in0=tq1[64:96, :], in1=tq2[64:96, :],
                                                                    op=ALU.add),
                     reads=[btq1, btq2], writes=[bqT[hb]])
                P.op("act", lambda e, cs=cs, hb=hb: e.copy(out=kT[hb][0:64, cs], in_=ps[4][0:64, :]),
                     reads=[pb[4]], writes=[bkT[hb]])
            for g8 in range(2):
                for kk in range(8):
                    kb = g8 * 8 + kk
                    for kc in range(2):
                        P.op("pe", lambda e, kc=kc, kb=kb, kk=kk, h=h: e.matmul(
                            ps[5][:, kk * 64:(kk + 1) * 64], lhsT=ckvnT[:, kc, kb * 128:(kb + 1) * 128], rhs=wv[:, kc, h, :],
                            start=(kc == 0), stop=(kc == 1)),
                             reads=[bwv, bckvnT], writes=[pb[5]], inc=(kc == 1 and kk == 7))
                P.op("act", lambda e, g8=g8, hb=hb: e.copy(out=vaug[hb][:, g8 * 8:(g8 + 1) * 8, 0:64],
                                                           in_=ps[5].rearrange("p (k d) -> p k d", k=8)),
                     reads=[pb[5]], writes=[bva[hb]])
            for c in range(4):
                ob = c % 2
                psO, bO = ps[6 + ob], pb[6 + ob]
                first = True
                for kb in range(4 * c + 4):
                    qlo = max(kb, 4 * c)
                    ncol = (4 * c + 4 - qlo) * 128
                    sb = nS % 2
                    nS += 1
                    P.op("pe", lambda e, kb=kb, qlo=qlo, ncol=ncol, sb=sb, hb=hb: e.matmul(
                        ps[sb][:, 0:ncol], lhsT=kT[hb][0:96, kb * 128:(kb + 1) * 128],
                        rhs=qT[hb][0:96, qlo * 128:qlo * 128 + ncol], start=True, stop=True),
                         reads=[bkT[hb], bqT[hb]], writes=[pb[sb]])
                    eb = nE % 4
                    nE += 1
                    c0 = 0
                    if kb >= 4 * c:
                        db = nD % 2
                        nD += 1
                        P.op("dve", lambda e, sb=sb, db=db: e.tensor_tensor(out=tmpd[db], in0=ps[sb][:, 0:128], in1=mdiag, op=ALU.add),
                             reads=[pb[sb], bmd], writes=[btd[db]])
                        P.op("act", lambda e, eb=eb, db=db: e.activation(out=E[eb][:, 0:128], in_=tmpd[db], func=AF.Exp, scale=scale),
                             reads=[btd[db]], writes=[bE[eb]])
                        c0 = 128
                    if ncol > c0:
                        P.op("act", lambda e, eb=eb, sb=sb, c0=c0, ncol=ncol: e.activation(
                            out=E[eb][:, c0:ncol], in_=ps[sb][:, c0:ncol], func=AF.Exp, scale=scale),
                             reads=[pb[sb]], writes=[bE[eb]])
                    nq = 4 * c + 4 - qlo
                    for qi in range(nq):
                        qb = qlo + qi
                        oi = qb - 4 * c
                        P.op("pe", lambda e, eb=eb, qi=qi, oi=oi, kb=kb, hb=hb, first=first, psO=psO: e.matmul(
                            psO[:, oi * 65:(oi + 1) * 65], lhsT=E[eb][:, qi * 128:(qi + 1) * 128], rhs=vaug[hb][:, kb, :],
                            start=first, stop=False, skip_group_check=True),
                             reads=[bE[eb], bva[hb]], writes=[bO], inc=(qi == nq - 1))
                        first = False
                rb = brec[ob]
                P.op("dve", lambda e, psO=psO, ob=ob: e.reciprocal(out=rec[:, ob * 4:(ob + 1) * 4],
                                                                   in_=psO[:, 0:260].rearrange("p (q d) -> p q d", d=65)[:, :, 64]),
                     reads=[bO], writes=[rb])
                for oi in range(4):
                    qb = 4 * c + oi
                    P.op("dve", lambda e, psO=psO, ob=ob, oi=oi, qb=qb, h=h: e.tensor_scalar(
                        out=o_all[:, qb, h * 64:(h + 1) * 64], in0=psO[:, oi * 65:oi * 65 + 64],
                        scalar1=rec[:, ob * 4 + oi:ob * 4 + oi + 1], scalar2=None, op0=ALU.mult),
                         reads=[bO, rb], writes=[bo_all[qb]])
        P.barrier()
        A.release()
        A.mark()
        oT = [A.alloc([8, 128], BF16) for _ in range(2)]
        boT = P.bufs(2, "oT")
        hr = [A.alloc([D], F32) for _ in range(2)]
        bhr = P.bufs(2, "hr")
        for qb in range(16):
            out_proj_tile(cx, o_all[:, qb, :], bo_all[qb], wo, bwo, src, dst, tok0 + qb * 128, oT, boT, hr, bhr, qb % 2,
                          psT, bT, ps[0:2], pb[0:2])
        P.barrier()
        A.release()
    A.release()


def rev_cols(ap, start, n):
    a = [list(x) for x in ap.ap]
    assert len(a) == 2 and a[1][0] == 1, a
    return bass.AP(ap.tensor, ap.offset + start, [a[0], [-1, n]])


def nsa_setup(cx):
    P, A, ps, pb = cx.P, cx.A, cx.ps, cx.pb
    nc = cx.nc
    cx.rtab_t = nc.dram_tensor("rtab", [16, 4096], F32)
    rtab = cx.rtab_t.ap()
    A.mark()
    tbl = A.alloc([16], F32)
    btbl = P.buf("tbl")
    P.op("dve", lambda e: e.memset(tbl[0:64, :], NEGM), writes=[btbl])
    P.dma("sp", [(tbl[0:32, :], cx.inp["rel_bias"])], writes=[btbl])
    oh = A.alloc([4096], F32)
    boh = P.buf("oh")
    P.dma("sp", [(oh[0:33, :], cx.cd["c_oh"])], writes=[boh])
    rt = A.alloc([4096], F32)
    brt = P.buf("rt")
    for ch in range(8):
        b = ch % 2
        P.op("pe", lambda e, ch=ch, b=b: e.matmul(ps[b][0:16, :], lhsT=tbl[0:33, :], rhs=oh[0:33, ch * 512:(ch + 1) * 512],
                                                  start=True, stop=True),
             reads=[btbl, boh], writes=[pb[b]])
        P.op("act", lambda e, ch=ch, b=b: e.copy(out=rt[0:16, ch * 512:(ch + 1) * 512], in_=ps[b][0:16, :]),
             reads=[pb[b]], writes=[brt])
    P.dma("sp", [(rtab, rt[0:16, :])], reads=[brt], key="rtab_st")
    P.barrier()
    A.release()
    dbg(cx, 0)


def nsa_phase(cx, j, li, src, dst):
    P, A, ps, pb = cx.P, cx.A, cx.ps, cx.pb
    A.mark()
    scale = 0.125
    Win_d = cx.inp["nsa_w_in"][j]
    w_in_r = Win_d.rearrange("(kc p) c -> p kc c", p=128)
    rt = cx.rtab_t
    W1 = [A.alloc([32, 128], BF16) for _ in range(2)]
    bW1 = P.bufs(2, "W1")
    w2 = [A.alloc([64], BF16) for _ in range(2)]
    bw2 = P.bufs(2, "w2")
    for kv, nm in enumerate(["k", "v"]):
        P.dma("pool", [(W1[kv][0:64], cx.inp["nsa_cmp_w1_%s" % nm][j].rearrange("(l d) c -> d l c", d=64))], writes=[bW1[kv]])
        P.dma("pool", [(w2[kv], cx.inp["nsa_cmp_w2_%s" % nm][j])], writes=[bw2[kv]])
    posf = A.alloc([2, 32], F32)
    posb = A.alloc([2, 32], BF16)
    bposf, bposb = P.buf("posf"), P.buf("posb")
    P.dma("sp", [(posf[0:64, 0, :], cx.inp["nsa_cmp_pos_k"][j].rearrange("l d -> d l")),
                 (posf[0:64, 1, :], cx.inp["nsa_cmp_pos_v"][j].rearrange("l d -> d l"))], writes=[bposf],
          allow_slow_non_contiguous=True)
    P.op("act", lambda e: e.copy(out=posb[0:64], in_=posf[0:64]), reads=[bposf], writes=[bposb])
    cpos = A.alloc([2], F32)
    bcpos = P.buf("cpos")
    for kv in range(2):
        for l in range(32):
            P.op("pe", lambda e, kv=kv, l=l: e.matmul(ps[4 + kv][:, 0:1], lhsT=W1[kv][0:64, l, :], rhs=posb[0:64, kv, l:l + 1],
                                                      start=(l == 0), stop=(l == 31)),
                 reads=[bW1[kv], bposb], writes=[pb[4 + kv]], inc=(l == 31))
        P.op("act", lambda e, kv=kv: e.copy(out=cpos[:, kv:kv + 1], in_=ps[4 + kv][:, 0:1]), reads=[pb[4 + kv]], writes=[bcpos])
    gain, bgain = load_gain(cx, cx.inp["mix_norm"][li], D)
    c31 = A.alloc([16], F32)
    bc31 = P.buf("c31")
    P.dma("sp", [(c31, cx.inp["rel_bias"][31].partition_broadcast(128))], writes=[bc31])
    m4 = A.alloc([128], F32)
    bm4 = P.buf("m4")
    P.dma("sp", [(m4, cx.cd["c_m4"])], writes=[bm4])
    selc = A.alloc([2, 16, 32], F32)
    bselc = P.buf("selc")
    P.dma("sp", [(selc[:, 0], cx.cd["c_cm"]), (selc[:, 1], cx.cd["c_add"])], writes=[bselc])
    junk = A.alloc([D], BF16)
    bjunk = P.buf("junk")
    ss = A.alloc([8], F32)
    bss = P.bufs(8, "ss")
    psT, bT = ps[6], pb[6]
    dbg(cx, 1)

    for sq in range(NSEQ):
        tok0 = sq * S
        mT = A.alloc([8, S], BF16) if sq == 0 else mT
        gates = A.alloc([16, 48], F32) if sq == 0 else gates
        o_all = A.alloc([16, D], BF16) if sq == 0 else o_all
        if sq == 0:
            bmT, bgates = P.buf("mT"), P.bufs(16, "gates")
            bo_all = P.bufs(16, "o_all")
        A.mark()
        wgt = A.alloc([8, 48], BF16)
        bwgt = P.buf("wgt")
        P.dma("pool", [(wgt, w_in_r[:, :, 2560:2608])], writes=[bwgt])
        hn = [A.alloc([D], F32) for _ in range(2)]
        bhn = P.bufs(2, "hn")
        xn = [A.alloc([D], BF16) for _ in range(2)]
        bxn = P.bufs(2, "xn")
        for qb in range(16):
            norm_T_tile(cx, src, tok0 + qb * 128, gain, bgain, hn, bhn, qb % 2, ss, bss, qb % 8, junk, bjunk,
                        xn, bxn, psT, bT, mT, bmT, qb * 128)
            b = qb % 2
            for kc in range(8):
                P.op("pe", lambda e, kc=kc, qb=qb, b=b: e.matmul(ps[b][:, 0:48], lhsT=mT[:, kc, qb * 128:(qb + 1) * 128],
                                                                rhs=wgt[:, kc, :], start=(kc == 0), stop=(kc == 7)),
                     reads=[bmT, bwgt], writes=[pb[b]], inc=(kc == 7))
            P.op("act", lambda e, qb=qb, b=b: e.activation(out=gates[:, qb, :], in_=ps[b][:, 0:48], func=AF.Sigmoid),
                 reads=[pb[b]], writes=[bgates[qb]])
        P.barrier()
        A.release()
        dbg(cx, 2)
        for g in range(4):
            A.mark()
            wg_ = A.alloc([8, 640], BF16)
            bwg_ = P.buf("wing")
            prs = [(wg_[:, :, 0:256], w_in_r[:, :, g * 256:(g + 1) * 256])]
            for i in range(6):
                prs.append((wg_[:, :, 256 + i * 64:320 + i * 64], w_in_r[:, :, 1024 + i * 256 + g * 64:1024 + i * 256 + (g + 1) * 64]))
            P.dma("pool", prs, writes=[bwg_])
            x01 = A.alloc([4, 256], F32)
            bx01 = P.buf("x01")
            P.dma("sp", [(x01, bass.AP(rt, (4 * g) * 4096 + 1792, [[1, 128], [4096, 4], [1, 256]]))], writes=[bx01])
            qa = [A.alloc([S], BF16) for _ in range(4)]
            bqa = P.bufs(4, "qa")
            ka = A.alloc([S], BF16)
            bka = P.buf("ka")
            P.dma("sp", [(ka[64:96, :], cx.cd["c_bexp"])], writes=[bka])
            kw = A.alloc([S], BF16)
            kcf = A.alloc([S], BF16)
            vcf = A.alloc([S], BF16)
            bkw, bkcf, bvcf = P.buf("kw"), P.buf("kcf"), P.buf("vcf")
            vs = A.alloc([16, 65], BF16)
            vw = A.alloc([16, 65], BF16)
            bvs, bvw = P.buf("vs"), P.buf("vw")
            P.op("dve", lambda e: e.memset(vs[:, :, 64:65], 1.0), writes=[bvs])
            P.op("dve", lambda e: e.memset(vw[:, :, 64:65], 1.0), writes=[bvw])
            kcT = A.alloc([128], BF16)
            bkcT = P.buf("kcT")
            vcx = A.alloc([97], BF16)
            bvcx = P.buf("vcx")
            P.op("dve", lambda e: e.memset(vcx[:, 64:65], 1.0), writes=[bvcx])
            P.dma("sp", [(vcx[0:127, 65:97], cx.cd["c_ov"])], writes=[bvcx])
            ocmp = A.alloc([16, 4, 64], BF16)
            bocmp = P.bufs(16, "ocmp")
            imp = A.alloc([16, 32], F32)
            bimp = P.bufs(16, "imp")
            nst = A.alloc([S], BF16)
            bnst = P.buf("nst")
            pi = 0
            fm = [(qa[0], bqa[0], 0), (qa[1], bqa[1], 64), (qa[2], bqa[2], 128), (qa[3], bqa[3], 192),
                  (kcf, bkcf, 256), (vcf, bvcf, 320), (ka, bka, 384), (kw, bkw, 512)]
            for (dt_, bdt, c0) in fm:
                for c in range(4):
                    b = 4 + pi % 2
                    pi += 1
                    cs = slice(c * 512, (c + 1) * 512)
                    for kc in range(8):
                        P.op("pe", lambda e, kc=kc, cs=cs, c0=c0, b=b: e.matmul(ps[b][0:64, :], lhsT=wg_[:, kc, c0:c0 + 64],
                                                                               rhs=mT[:, kc, cs], start=(kc == 0), stop=(kc == 7)),
                             reads=[bwg_, bmT], writes=[pb[b]], inc=(kc == 7))
                    P.op("act", lambda e, dt_=dt_, cs=cs, b=b: e.copy(out=dt_[0:64, cs], in_=ps[b][0:64, :]),
                         reads=[pb[b]], writes=[bdt])
            for (vt, bvt, c0) in [(vs, bvs, 448), (vw, bvw, 576)]:
                for g8 in range(2):
                    b = 4 + pi % 2
                    pi += 1
                    for kk in range(8):
                        kb = g8 * 8 + kk
                        for kc in range(8):
                            P.op("pe", lambda e, kc=kc, kb=kb, kk=kk, c0=c0, b=b: e.matmul(
                                ps[b][:, kk * 64:(kk + 1) * 64], lhsT=mT[:, kc, kb * 128:(kb + 1) * 128], rhs=wg_[:, kc, c0:c0 + 64],
                                start=(kc == 0), stop=(kc == 7)),
                                 reads=[bwg_, bmT], writes=[pb[b]], inc=(kc == 7 and kk == 7))
                    P.op("act", lambda e, vt=vt, g8=g8, b=b: e.copy(out=vt[:, g8 * 8:(g8 + 1) * 8, 0:64],
                                                                  in_=ps[b].rearrange("p (k d) -> p k d", k=8)),
                         reads=[pb[b]], writes=[bvt])
            dbg(cx, 3)
            xh = A.alloc([128], F32)
            x2 = A.alloc([128], F32)
            sgm = A.alloc([128], F32)
            gh = A.alloc([128], BF16)
            bxh, bx2, bsgm, bgh = P.buf("xh"), P.buf("x2"), P.buf("sgm"), P.buf("gh")
            for kv, (cf, bcf) in enumerate([(kcf, bkcf), (vcf, bvcf)]):
                b = 4 + kv
                for l in range(32):
                    P.op("pe", lambda e, kv=kv, l=l, cf=cf, b=b: e.matmul(ps[b][:, 0:127], lhsT=W1[kv][0:64, l, :],
                                                                         rhs=cf[0:64, l:l + 16 * 126 + 1:16],
                                                                         start=(l == 0), stop=(l == 31)),
                         reads=[bW1[kv], bcf], writes=[pb[b]], inc=(l == 31))
                P.op("act", lambda e, kv=kv, b=b: e.activation(out=xh[:, 0:127], in_=ps[b][:, 0:127], func=AF.Identity,
                                                               bias=cpos[:, kv:kv + 1], scale=1.0),
                     reads=[pb[b], bcpos], writes=[bxh])
                P.op("dve", lambda e: e.tensor_tensor(out=x2[:, 0:127], in0=xh[:, 0:127], in1=xh[:, 0:127], op=ALU.mult),
                     reads=[bxh], writes=[bx2])
                P.op("dve", lambda e: e.tensor_scalar(out=x2[:, 0:127], in0=x2[:, 0:127], scalar1=0.044715, scalar2=1.0,
                                                      op0=ALU.mult, op1=ALU.add), reads=[bx2], writes=[bx2])
                P.op("dve", lambda e: e.tensor_tensor(out=x2[:, 0:127], in0=x2[:, 0:127], in1=xh[:, 0:127], op=ALU.mult),
                     reads=[bx2, bxh], writes=[bx2])
                P.op("act", lambda e: e.activation(out=sgm[:, 0:127], in_=x2[:, 0:127], func=AF.Sigmoid, scale=1.5957691216057308),
                     reads=[bx2], writes=[bsgm])
                P.op("dve", lambda e: e.tensor_tensor(out=gh[:, 0:127], in0=xh[:, 0:127], in1=sgm[:, 0:127], op=ALU.mult),
                     reads=[bxh, bsgm], writes=[bgh])
                if kv == 0:
                    P.op("pe", lambda e: e.matmul(ps[6][0:64, 0:127], lhsT=w2[0], rhs=gh[:, 0:127], start=True, stop=True),
                         reads=[bw2[0], bgh], writes=[pb[6]])
                    P.op("act", lambda e: e.copy(out=kcT[0:64, 0:127], in_=ps[6][0:64, 0:127]), reads=[pb[6]], writes=[bkcT])
                else:
                    P.op("pe", lambda e: e.matmul(ps[6][0:127, 0:64], lhsT=gh[:, 0:127], rhs=w2[1], start=True, stop=True),
                         reads=[bw2[1], bgh], writes=[pb[6]])
                    P.op("act", lambda e: e.copy(out=vcx[0:127, 0:64], in_=ps[6][0:127, 0:64]), reads=[pb[6]], writes=[bvcx])
            dbg(cx, 4)
            xb = A.alloc([S], F32)
            bxb = P.buf("xb")
            tmpc = A.alloc([512], F32)
            btmpc = P.buf("tmpc")
            Ec = [A.alloc([512], BF16) for _ in range(2)]
            bEc = P.bufs(2, "Ec")
            rc = A.alloc([8], F32)
            brc = P.bufs(2, "rc")
            rg = A.alloc([8], F32)
            brg = P.bufs(2, "rg")
            nc_ = 0
            for r in range(4):
                h = 4 * g + r
                P.dma("sp", [(xb[0:127, :], bass.AP(rt, h * 4096 + 31, [[16, 127], [1, 2048]]))], writes=[bxb], key="xb")
                for c in range(4):
                    sb = nc_ % 2
                    ob = nc_ % 2
                    nc_ += 1
                    cs = slice(c * 512, (c + 1) * 512)
                    P.op("pe", lambda e, cs=cs, sb=sb, r=r: e.matmul(ps[sb][0:127, :], lhsT=kcT[0:64, 0:127], rhs=qa[r][0:64, cs],
                                                                    start=True, stop=True),
                         reads=[bkcT, bqa[r]], writes=[pb[sb]])
                    P.op("dve", lambda e, sb=sb, c=c: e.scalar_tensor_tensor(
                        out=tmpc[0:127, :], in0=ps[sb][0:127, :], scalar=scale, in1=rev_cols(xb[0:127, :], 2047 - c * 512, 512),
                        op0=ALU.mult, op1=ALU.add), reads=[pb[sb], bxb], writes=[btmpc])
                    P.op("act", lambda e, sb=sb: e.activation(out=Ec[sb][0:127, :], in_=tmpc[0:127, :], func=AF.Exp),
                         reads=[btmpc], writes=[bEc[sb]])
                    psC, bC = ps[2 + ob], pb[2 + ob]
                    for qi in range(4):
                        P.op("pe", lambda e, sb=sb, qi=qi, psC=psC: e.matmul(
                            psC[:, qi * 97:(qi + 1) * 97], lhsT=Ec[sb][0:127, qi * 128:(qi + 1) * 128], rhs=vcx[0:127, :],
                            start=(qi == 0), stop=False, skip_group_check=True),
                             reads=[bEc[sb], bvcx], writes=[bC], inc=(qi == 3))
                    pc3 = psC[:, 0:388].rearrange("p (q d) -> p q d", d=97)
                    P.op("dve", lambda e, pc3=pc3, ob=ob: e.tensor_scalar(out=rc[:, ob * 4:(ob + 1) * 4], in0=pc3[:, :, 64],
                                                                          scalar1=1e-30, scalar2=None, op0=ALU.max),
                         reads=[bC], writes=[brc[ob]])
                    P.op("dve", lambda e, ob=ob: e.reciprocal(out=rc[:, ob * 4:(ob + 1) * 4], in_=rc[:, ob * 4:(ob + 1) * 4]),
                         reads=[brc[ob]], writes=[brc[ob]])
                    P.op("dve", lambda e, ob=ob, c=c, h=h: e.tensor_tensor(out=rg[:, ob * 4:(ob + 1) * 4], in0=rc[:, ob * 4:(ob + 1) * 4],
                                                                          in1=gates[:, 4 * c:4 * c + 4, 3 * h], op=ALU.mult),
                         reads=[brc[ob]] + bgates[4 * c:4 * c + 4], writes=[brg[ob]])
                    for qi in range(4):
                        qb = 4 * c + qi
                        P.op("dve", lambda e, psC=psC, qi=qi, qb=qb, r=r, ob=ob: e.tensor_scalar(
                            out=ocmp[:, qb, r, :], in0=psC[:, qi * 97:qi * 97 + 64], scalar1=rg[:, ob * 4 + qi:ob * 4 + qi + 1],
                            scalar2=None, op0=ALU.mult), reads=[bC, brg[ob]], writes=[bocmp[qb]])
                        if r == 0:
                            P.op("dve", lambda e, psC=psC, qi=qi, qb=qb, ob=ob: e.tensor_scalar(
                                out=imp[:, qb, :], in0=psC[:, qi * 97 + 65:qi * 97 + 97], scalar1=rc[:, ob * 4 + qi:ob * 4 + qi + 1],
                                scalar2=None, op0=ALU.mult), reads=[bC, brc[ob]], writes=[bimp[qb]])
                        else:
                            P.op("dve", lambda e, psC=psC, qi=qi, qb=qb, ob=ob: e.scalar_tensor_tensor(
                                out=imp[:, qb, :], in0=psC[:, qi * 97 + 65:qi * 97 + 97], scalar=rc[:, ob * 4 + qi:ob * 4 + qi + 1],
                                in1=imp[:, qb, :], op0=ALU.mult, op1=ALU.add), reads=[bC, brc[ob], bimp[qb]], writes=[bimp[qb]])
            dbg(cx, 5)
            sc = A.alloc([32], F32)
            wk = A.alloc([32], F32)
            m8 = A.alloc([16], F32)
            stg = A.alloc([96], BF16)
            bsc, bwk, bm8, bstg = P.buf("sc"), P.buf("wk"), P.buf("m8"), P.buf("stg")
            P.op("dve", lambda e: e.memset(stg, 0.0), writes=[bstg])
            for qb in range(16):
                P.op("dve", lambda e, qb=qb: e.tensor_tensor(out=sc, in0=imp[:, qb, :], in1=selc[:, 0, qb, :], op=ALU.mult),
                     reads=[bimp[qb], bselc], writes=[bsc])
                P.op("dve", lambda e, qb=qb: e.tensor_tensor(out=sc, in0=sc, in1=selc[:, 1, qb, :], op=ALU.add),
                     reads=[bsc, bselc], writes=[bsc])
                P.op("dve", lambda e: e.max(out=m8[:, 0:8], in_=sc), reads=[bsc], writes=[bm8])
                P.op("dve", lambda e: e.match_replace(out=wk, in_to_replace=m8[:, 0:8], in_values=sc, imm_value=-3.0e38),
                     reads=[bsc, bm8], writes=[bwk])
                P.op("dve", lambda e: e.max(out=m8[:, 8:16], in_=wk), reads=[bwk], writes=[bm8])
                P.op("dve", lambda e: e.tensor_scalar(out=wk, in0=sc, scalar1=m8[:, 15:16], scalar2=None, op0=ALU.is_ge),
                     reads=[bsc, bm8], writes=[bwk])
                P.op("dve", lambda e: e.tensor_scalar(out=stg[:, 64:96], in0=wk, scalar1=-NEGM, scalar2=NEGM, op0=ALU.mult, op1=ALU.add),
                     reads=[bwk], writes=[bstg])
                pst = psT.bitcast(BF16)
                P.op("pe", lambda e, pst=pst: e.transpose(out=pst[0:96, 0:128], in_=stg, identity=cx.ident),
                     reads=[bstg, cx.bident], writes=[bT])
                P.op("act", lambda e, qb=qb, pst=pst: e.copy(out=nst[64:96, qb * 128:(qb + 1) * 128], in_=pst[64:96, 0:128]),
                     reads=[bT], writes=[bnst])
            for r in range(4):
                P.op("act", lambda e, r=r: e.copy(out=qa[r][64:96, :], in_=nst[64:96, :]), reads=[bnst], writes=[bqa[r]])
            dbg(cx, 6)
            E = [A.alloc([512], BF16) for _ in range(4)]
            bE = P.bufs(4, "E")
            tmpd = [A.alloc([256], F32) for _ in range(2)]
            btd = P.bufs(2, "tmpd")
            rsw = A.alloc([16], F32)
            brsw = P.bufs(2, "rsw")
            tmpo = [A.alloc([64], F32) for _ in range(2)]
            btmpo = P.bufs(2, "tmpo")
            st = {"S": 0, "E": 0, "D": 0, "O": 0}
            zE = A.alloc([128], BF16)
            bzE = P.buf("zE")
            P.op("dve", lambda e: e.memset(zE, 0.0), writes=[bzE])

            def branch(kind, r, h, c, psO, bO):
                vt, bvt = (vs, bvs) if kind == 0 else (vw, bvw)
                for oi in range(4):
                    P.op("pe", lambda e, oi=oi, psO=psO, vt=vt: e.matmul(psO[:, oi * 65:(oi + 1) * 65], lhsT=zE, rhs=vt[:, 0, :],
                                                                      start=(oi == 0), stop=False, skip_group_check=True),
                         reads=[bzE, bvt], writes=[bO], inc=(oi == 3))
                first = False
                kb_lo = 0 if kind == 0 else max(0, 4 * c - 4)
                for kb in range(kb_lo, 4 * c + 4):
                    qlo = max(kb, 4 * c)
                    qhi = 4 * c + 3 if kind == 0 else min(kb + 4, 4 * c + 3)
                    nq = qhi - qlo + 1
                    ncol = nq * 128
                    sb = st["S"] % 2
                    st["S"] += 1
                    if kind == 0:
                        P.op("pe", lambda e, kb=kb, qlo=qlo, ncol=ncol, sb=sb, r=r: e.matmul(
                            ps[sb][:, 0:ncol], lhsT=ka[0:96, kb * 128:(kb + 1) * 128], rhs=qa[r][0:96, qlo * 128:qlo * 128 + ncol],
                            start=True, stop=True), reads=[bka, bqa[r]], writes=[pb[sb]])
                    else:
                        P.op("pe", lambda e, kb=kb, qlo=qlo, ncol=ncol, sb=sb, r=r: e.matmul(
                            ps[sb][:, 0:ncol], lhsT=kw[0:64, kb * 128:(kb + 1) * 128], rhs=qa[r][0:64, qlo * 128:qlo * 128 + ncol],
                            start=True, stop=True), reads=[bkw, bqa[r]], writes=[pb[sb]])
                    eb = st["E"] % 4
                    st["E"] += 1
                    d0 = qlo - kb
                    n01 = max(0, min(2, d0 + nq) - d0) if d0 < 2 else 0
                    n4 = 1 if (kind == 1 and qhi - kb == 4) else 0
                    ncst = nq - n01 - n4
                    col = 0
                    if n01:
                        w = n01 * 128
                        db = st["D"] % 2
                        st["D"] += 1
                        P.op("dve", lambda e, sb=sb, db=db, w=w, d0=d0, r=r: e.scalar_tensor_tensor(
                            out=tmpd[db][:, 0:w], in0=ps[sb][:, 0:w], scalar=scale, in1=rev_cols(x01[:, r, :], 255 - d0 * 128, w),
                            op0=ALU.mult, op1=ALU.add),
                             reads=[pb[sb], bx01], writes=[btd[db]])
                        P.op("act", lambda e, eb=eb, db=db, w=w: e.activation(out=E[eb][:, 0:w], in_=tmpd[db][:, 0:w], func=AF.Exp),
                             reads=[btd[db]], writes=[bE[eb]])
                        col = w
                    col4 = (n01 + ncst) * 128
                    if n4:
                        db = st["D"] % 2
                        st["D"] += 1
                        P.op("dve", lambda e, sb=sb, db=db, col4=col4: e.tensor_tensor(
                            out=tmpd[db][:, 0:128], in0=ps[sb][:, col4:col4 + 128], in1=m4, op=ALU.add),
                             reads=[pb[sb], bm4], writes=[btd[db]])
                        P.op("act", lambda e, eb=eb, db=db, col4=col4, h=h: e.activation(
                            out=E[eb][:, col4:col4 + 128], in_=tmpd[db][:, 0:128], func=AF.Exp, scale=scale, bias=c31[:, h:h + 1]),
                             reads=[btd[db], bc31], writes=[bE[eb]])
                    if ncst:
                        w = ncst * 128
                        P.op("act", lambda e, eb=eb, sb=sb, col=col, w=w, h=h: e.activation(
                            out=E[eb][:, col:col + w], in_=ps[sb][:, col:col + w], func=AF.Exp, scale=scale, bias=c31[:, h:h + 1]),
                             reads=[pb[sb], bc31], writes=[bE[eb]])
                    vt, bvt = (vs, bvs) if kind == 0 else (vw, bvw)
                    for qi in range(nq):
                        oi = qlo + qi - 4 * c
                        P.op("pe", lambda e, eb=eb, qi=qi, oi=oi, kb=kb, first=first, psO=psO, vt=vt: e.matmul(
                            psO[:, oi * 65:(oi + 1) * 65], lhsT=E[eb][:, qi * 128:(qi + 1) * 128], rhs=vt[:, kb, :],
                            start=first, stop=False, skip_group_check=True),
                             reads=[bE[eb], bvt], writes=[bO], inc=(qi == nq - 1))
                        first = False
                    cx.nbr = getattr(cx, "nbr", 0) + 1
                    dbg(cx, 1000 + cx.nbr)

            for r in range(4):
                h = 4 * g + r
                for c in range(4):
                    ob = st["O"] % 2
                    st["O"] += 1
                    psOs, bOs = ps[2 + ob], pb[2 + ob]
                    psOw, bOw = ps[4 + ob], pb[4 + ob]
                    branch(0, r, h, c, psOs, bOs)
                    dbg(cx, 8)
                    branch(1, r, h, c, psOw, bOw)
                    dbg(cx, 9)
                    o3s = psOs[:, 0:260].rearrange("p (q d) -> p q d", d=65)
                    o3w = psOw[:, 0:260].rearrange("p (q d) -> p q d", d=65)
                    rs_ = rsw[:, ob * 8:ob * 8 + 4]
                    rw_ = rsw[:, ob * 8 + 4:ob * 8 + 8]
                    P.op("dve", lambda e, o3s=o3s, rs_=rs_: e.reciprocal(out=rs_, in_=o3s[:, :, 64]), reads=[bOs], writes=[brsw[ob]])
                    P.op("dve", lambda e, o3w=o3w, rw_=rw_: e.reciprocal(out=rw_, in_=o3w[:, :, 64]), reads=[bOw], writes=[brsw[ob]])
                    P.op("dve", lambda e, rs_=rs_, c=c, h=h: e.tensor_tensor(out=rs_, in0=rs_, in1=gates[:, 4 * c:4 * c + 4, 3 * h + 1],
                                                                            op=ALU.mult),
                         reads=[brsw[ob]] + bgates[4 * c:4 * c + 4], writes=[brsw[ob]])
                    P.op("dve", lambda e, rw_=rw_, c=c, h=h: e.tensor_tensor(out=rw_, in0=rw_, in1=gates[:, 4 * c:4 * c + 4, 3 * h + 2],
                                                                            op=ALU.mult),
                         reads=[brsw[ob]] + bgates[4 * c:4 * c + 4], writes=[brsw[ob]])
                    for oi in range(4):
                        qb = 4 * c + oi
                        tb = oi % 2
                        P.op("dve", lambda e, psOs=psOs, oi=oi, qb=qb, r=r, ob=ob, tb=tb: e.scalar_tensor_tensor(
                            out=tmpo[tb], in0=psOs[:, oi * 65:oi * 65 + 64], scalar=rsw[:, ob * 8 + oi:ob * 8 + oi + 1],
                            in1=ocmp[:, qb, r, :], op0=ALU.mult, op1=ALU.add),
                             reads=[bOs, brsw[ob], bocmp[qb]], writes=[btmpo[tb]])
                        P.op("dve", lambda e, psOw=psOw, oi=oi, qb=qb, h=h, ob=ob, tb=tb: e.scalar_tensor_tensor(
                            out=o_all[:, qb, h * 64:(h + 1) * 64], in0=psOw[:, oi * 65:oi * 65 + 64],
                            scalar=rsw[:, ob * 8 + 4 + oi:ob * 8 + 4 + oi + 1], in1=tmpo[tb], op0=ALU.mult, op1=ALU.add),
                             reads=[bOw, brsw[ob], btmpo[tb]], writes=[bo_all[qb]])
                    dbg(cx, 10)
                    cx.ncomb = getattr(cx, "ncomb", 0) + 1
                    dbg(cx, 100 + cx.ncomb)
            P.barrier()
            A.release()
            dbg(cx, 7)
        A.mark()
        wo = A.alloc([8, 1024], BF16)
        bwo = P.buf("wo")
        P.dma("pool", [(wo, cx.inp["nsa_w_o"][j].rearrange("(kc p) c -> p kc c", p=128))], writes=[bwo])
        oT = [A.alloc([8, 128], BF16) for _ in range(2)]
        boT = P.bufs(2, "oT")
        hr = [A.alloc([D], F32) for _ in range(2)]
        bhr = P.bufs(2, "hr")
        for qb in range(16):
            out_proj_tile(cx, o_all[:, qb, :], bo_all[qb], wo, bwo, src, dst, tok0 + qb * 128, oT, boT, hr, bhr, qb % 2,
                          psT, bT, ps[0:2], pb[0:2])
        P.barrier()
        A.release()
    A.release()


def final_norm_phase(cx, src, dst):
    P, A = cx.P, cx.A
    A.mark()
    gain, bgain = load_gain(cx, cx.inp["final_norm"], D)
    hn = [A.alloc([D], F32) for _ in range(2)]
    bhn = P.bufs(2, "fhn")
    yo = [A.alloc([D], F32) for _ in range(2)]
    byo = P.bufs(2, "fyo")
    junk = A.alloc([D], BF16)
    bjunk = P.buf("junk")
    ss = A.alloc([8], F32)
    bss = P.bufs(8, "ss")
    for i in range(T // 128):
        sl = i % 2
        c = i % 8
        P.dma("sp", [(hn[sl], src[i * 128:(i + 1) * 128, :])], writes=[bhn[sl]], key="fhn%d" % sl)
        rms_rstd(cx, hn[sl], bhn[sl], D, ss, bss[c], c, junk, bjunk)
        P.op("dve", lambda e, sl=sl, c=c: e.scalar_tensor_tensor(out=yo[sl], in0=hn[sl], scalar=ss[:, c:c + 1], in1=gain,
                                                                 op0=ALU.mult, op1=ALU.mult),
             reads=[bhn[sl], bss[c], bgain], writes=[byo[sl]])
        P.dma("sp", [(dst[i * 128:(i + 1) * 128, :], yo[sl])], reads=[byo[sl]], key="fyo%d" % sl)
    P.barrier()
    A.release()


INPUT_SHAPES = {
    "ffn_norm_a": (DEPTH, D), "ffn_a_w_gate": (DEPTH, D, FF), "ffn_a_w_up": (DEPTH, D, FF), "ffn_a_w_down": (DEPTH, FF, D),
    "mix_norm": (DEPTH, D), "ffn_norm_b": (DEPTH, D), "ffn_b_w_gate": (DEPTH, D, FF), "ffn_b_w_up": (DEPTH, D, FF),
    "ffn_b_w_down": (DEPTH, FF, D), "final_norm": (D,), "rel_bias": (32, 16),
    "mla_w_in": (2, D, 672), "mla_q_norm": (2, 384), "mla_kv_norm": (2, 256), "mla_w_uq": (2, 384, 1536),
    "mla_w_ukv": (2, 256, 2048), "mla_w_o": (2, 1024, 1024),
    "nsa_w_in": (2, D, 2608), "nsa_cmp_pos_k": (2, 32, 64), "nsa_cmp_w1_k": (2, 2048, 128), "nsa_cmp_w2_k": (2, 128, 64),
    "nsa_cmp_pos_v": (2, 32, 64), "nsa_cmp_w1_v": (2, 2048, 128), "nsa_cmp_w2_v": (2, 128, 64), "nsa_w_o": (2, 1024, 1024),
}
ARENA_WORDS = 50944


def host_consts():
    c = {}
    c["c_ident"] = np.eye(128, dtype=np.float32).astype(ml_dtypes.bfloat16)
    half = 16
    inv = (np.float32(10000.0) ** (-np.arange(half, dtype=np.float32) * np.float32(2.0) / np.float32(32))).astype(np.float32)
    ang = np.arange(S, dtype=np.float32)[:, None] * inv[None, :]
    cos, sin = np.cos(ang).astype(np.float32), np.sin(ang).astype(np.float32)
    c["c_rope_cos"] = np.ascontiguousarray(np.concatenate([cos.T, cos.T], axis=0))
    c["c_rope_sin"] = np.ascontiguousarray(np.concatenate([sin.T, sin.T], axis=0))
    k = np.arange(128)[:, None]
    t = np.arange(128)[None, :]
    c["c_mdiag"] = np.where(t >= k, 0.0, NEGM).astype(np.float32)
    c["c_m4"] = np.where(t < k, 0.0, NEGM).astype(np.float32)
    def bucket(dist):
        n = np.maximum(dist, 0)
        nf = np.maximum(n, 1).astype(np.float32)
        large = 16 + (np.log(nf / np.float32(16)) / np.float32(math.log(8.0)) * np.float32(16)).astype(np.int32)
        return np.where(n < 16, n, np.minimum(large, 31))
    i = np.arange(4096)
    dist = 2047 - i
    oh = np.zeros((33, 4096), np.float32)
    bk = bucket(dist)
    oh[bk[dist >= 0], i[dist >= 0]] = 1.0
    oh[32, i[dist < 0]] = 1.0
    c["c_oh"] = oh
    kk = np.arange(S)
    c["c_bexp"] = (kk[None, :] // 64 == np.arange(32)[:, None]).astype(np.float32).astype(ml_dtypes.bfloat16)
    cs_ = np.arange(127) * 16
    ce_ = cs_ + 32
    ss_ = np.arange(32) * 64
    se_ = ss_ + 64
    ov = np.minimum(ce_[:, None], se_[None, :]) - np.maximum(cs_[:, None], ss_[None, :])
    c["c_ov"] = (np.clip(ov, 0, None) / 32.0).astype(np.float32).astype(ml_dtypes.bfloat16)
    tl = np.arange(128)[:, None, None]
    qb_ = np.arange(16)[None, :, None]
    jj = np.arange(32)[None, None, :]
    blk = (qb_ * 128 + tl) // 64
    forced = (jj == 0) | (jj == blk) | (jj == blk - 1)
    causal = jj <= blk
    c["c_cm"] = (causal & ~forced).astype(np.float32)
    c["c_add"] = np.where(forced, 1e6, np.where(causal, 0.0, -1e6)).astype(np.float32)
    return c


def default_phases():
    ph = []
    for i in range(DEPTH):
        ph.append(("ffn", i, "a"))
        ph.append(("mla", i // 2) if i % 2 == 0 else ("nsa", i // 2))
        ph.append(("ffn", i, "b"))
    ph.append(("final",))
    return ph


def build_program(phases, dbg_stop=None):
    nc = bass.Bass("TRN2", target_bir_lowering=False)
    cx = Ctx()
    cx.dbg_stop = dbg_stop
    cx.nc = nc
    cx.inp = {}
    cx.cd = {}
    x_in = nc.dram_tensor("x", [T, D], F32, kind="ExternalInput").ap()
    for name, shp in INPUT_SHAPES.items():
        cx.inp[name] = nc.dram_tensor(name, list(shp), F32, kind="ExternalInput").ap()
    consts = host_consts()
    cd = cx.cd
    for name, arr in consts.items():
        dt = BF16 if arr.dtype == ml_dtypes.bfloat16 else F32
        cd[name] = nc.dram_tensor(name, list(arr.shape), dt, kind="ExternalInput").ap()
    y = nc.dram_tensor("y", [T, D], F32, kind="ExternalOutput").ap()
    hbuf = nc.dram_tensor("hbuf", [T, D], F32).ap()
    with ExitStack() as es:
        big = es.enter_context(nc.sbuf_tensor("big", [128, ARENA_WORDS], F32))
        cx.ps = [es.enter_context(nc.psum_tensor("ps%d" % i, [128, 512], F32))[:, :] for i in range(8)]
        P = Prog(nc, es)
        cx.P = P
        cx.pb = P.bufs(8, "psb")
        cx.A = Arena(big, ARENA_WORDS)
        cx.ident = cx.A.alloc([128], BF16)
        cx.bident = P.buf("ident")
        P.dma("sp", [(cx.ident, cd["c_ident"])], writes=[cx.bident])
        cur = x_in
        try:
          for ph in phases:
            if ph[0] == "ffn":
                ffn_phase(cx, ph[1], ph[2], cur, hbuf)
                cur = hbuf
            elif ph[0] == "mla":
                mla_phase(cx, ph[1], 2 * ph[1], cur, hbuf)
                cur = hbuf
            elif ph[0] == "nsa":
                if not getattr(cx, "nsa_ready", False):
                    nsa_setup(cx)
                    cx.nsa_ready = True
                nsa_phase(cx, ph[1], 2 * ph[1] + 1, cur, hbuf)
                cur = hbuf
            elif ph[0] == "final":
                final_norm_phase(cx, cur, y)
                cur = y
            else:
                raise NotImplementedError(ph)
        except StopBuild:
            cur = x_in
        if cur is not y:
            P.dma("sp", [(y, cur)], key="ycopy")
            P.barrier()
        P.run_block()
    cx.consts = consts
    return nc, cx


_CACHE = {}


def kernel(**inputs):
    phases = default_phases()
    key = "full"
    if key not in _CACHE:
        _CACHE[key] = build_program(phases)
    nc, cx = _CACHE[key]
    x = np.ascontiguousarray(np.asarray(inputs["x"], dtype=np.float32)).reshape(N_CORES, T, D)
    shared = {k: np.ascontiguousarray(np.asarray(inputs[k], dtype=np.float32)) for k in INPUT_SHAPES}
    shared.update(cx.consts)
    in_maps = []
    for c in range(N_CORES):
        m = dict(shared)
        m["x"] = x[c]
        in_maps.append(m)
    res = run_bass_kernel_spmd(nc, in_maps, core_ids=list(range(N_CORES)))
    out = np.stack([np.asarray(r["y"], dtype=np.float32) for r in res.results], axis=0)
    return out.reshape(16, S, D)
```

```python
import math
from contextlib import ExitStack

import numpy as np
import ml_dtypes

import concourse.bass as bass
import concourse.mybir as mybir
from concourse.bass_utils import run_bass_kernel_spmd

F32 = mybir.dt.float32
BF16 = mybir.dt.bfloat16
AF = mybir.ActivationFunctionType
ALU = mybir.AluOpType

N_CORES = 8
D = 1024
S = 2048
NSEQ = 2
T = NSEQ * S
FF = 2816
NHC = FF // 128
DEPTH = 4
EPS = 1e-6
NEGM = -16384.0

ENGS = ["pe", "act", "dve", "pool", "sp"]
SELF_SYNC = {"pe": False, "act": True, "dve": True, "pool": True, "sp": False}


class Buf:
    __slots__ = ("name", "base", "w", "r")

    def __init__(self, name, base):
        self.name = name
        self.base = base
        self.w = []
        self.r = []


class Prog:
    def __init__(self, nc, es):
        self.nc = nc
        self.es = es
        self.q = {e: [] for e in ENGS}
        self.cnt = {e: 0 for e in ENGS}
        self.sems = {e: es.enter_context(nc.semaphore("s_" + e)) for e in ENGS if e != "sp"}
        self.dcnt = {}
        self.nbuf = 0
        self.ninst = 0

    def buf(self, name=None):
        self.nbuf += 1
        return Buf("%s_%d" % (name or "b", self.nbuf), name or "b")

    def bufs(self, n, name="b"):
        return [self.buf("%s%d" % (name, i)) for i in range(n)]

    def _deps(self, reads, writes):
        waits = []
        for b in reads:
            waits += b.w
        for b in writes:
            waits += b.w
            waits += b.r
        return waits

    def _commit(self, ev, reads, writes):
        for b in reads:
            b.r.append(ev)
        for b in writes:
            b.w = [ev]
            b.r = []

    def op(self, eng, fn, reads=(), writes=(), inc=True):
        waits = self._deps(reads, writes)
        idx = self.cnt[eng] + 1
        if inc:
            self.cnt[eng] = idx
        self._commit((eng, idx), reads, writes)
        self.q[eng].append((fn, waits, (eng, 1) if inc else None))
        self.ninst += 1

    def dma(self, eng, pairs, reads=(), writes=(), key=None, **kw):
        key = "d_" + (key or writes[0].base)
        if key not in self.sems:
            self.sems[key] = self.es.enter_context(self.nc.semaphore(key))
            self.dcnt[key] = 0
        waits = self._deps(reads, writes)
        self.dcnt[key] += 16 * len(pairs)
        self._commit((key, self.dcnt[key]), reads, writes)
        for i, (o, a) in enumerate(pairs):
            self.q[eng].append(((lambda e, o=o, a=a: e.dma_start(out=o, in_=a, **kw)),
                                waits if i == 0 else [], (key, 16)))
            self.ninst += 1

    def barrier(self):
        evs = [(e, self.cnt[e]) for e in ENGS if e != "sp" and self.cnt[e] > 0]
        evs += [(k, v) for k, v in self.dcnt.items() if v > 0]
        for e in ENGS:
            self.q[e].append((None, list(evs), None))

    def replay(self, eng, e):
        waited = {}
        for fn, waits, inc in self.q[eng]:
            need = {}
            for k, v in waits:
                if k == eng and not SELF_SYNC[eng]:
                    continue
                if v > need.get(k, 0):
                    need[k] = v
            for k, v in need.items():
                if waited.get(k, 0) < v:
                    e.wait_ge(self.sems[k], v)
                    waited[k] = v
            if fn is not None:
                ins = fn(e)
                if inc is not None:
                    ins.then_inc(self.sems[inc[0]], inc[1])

    def run_block(self):
        for e in ENGS:
            assert self.cnt[e] < 65000, (e, self.cnt[e])
        with self.nc.Block() as block:
            @block.sync
            def _(e):
                self.replay("sp", e)

            @block.tensor
            def _(e):
                self.replay("pe", e)

            @block.scalar
            def _(e):
                self.replay("act", e)

            @block.vector
            def _(e):
                self.replay("dve", e)

            @block.gpsimd
            def _(e):
                self.replay("pool", e)


class Arena:
    def __init__(self, big, words):
        self.big = big
        self.words = words
        self.top = 0
        self.marks = []
        self.peak = 0

    def alloc(self, shape_free, dtype, parts=128):
        n = int(np.prod(shape_free))
        nb = 2 if dtype == BF16 else 4
        w = (n * nb + 3) // 4
        w = (w + 7) // 8 * 8
        assert self.top + w <= self.words, ("SBUF arena overflow", self.top, w, self.words)
        ap = self.big[0:parts, self.top:self.top + w]
        self.top += w
        self.peak = max(self.peak, self.top)
        if dtype != F32:
            ap = ap.bitcast(dtype)
        ap = ap[:, 0:n]
        if len(shape_free) > 1:
            names = " ".join("d%d" % i for i in range(len(shape_free)))
            kw = {"d%d" % i: int(s) for i, s in enumerate(shape_free)}
            ap = ap.rearrange("p (%s) -> p %s" % (names, names), **kw)
        return ap

    def mark(self):
        self.marks.append(self.top)

    def release(self):
        self.top = self.marks.pop()


class Ctx:
    dbg_stop = None


class StopBuild(Exception):
    pass


def dbg(cx, level):
    if cx.dbg_stop is not None and cx.dbg_stop == level:
        cx.P.barrier()
        raise StopBuild()


def rms_rstd(cx, x_ap, bx, n, ss, bss, col, junk, bjunk):
    P = cx.P
    c = ss[:, col:col + 1]
    P.op("act", lambda e: e.activation(out=junk[:, 0:n], in_=x_ap, func=AF.Square, accum_out=c),
         reads=[bx], writes=[bjunk, bss])
    P.op("dve", lambda e: e.tensor_scalar(out=c, in0=c, scalar1=1.0 / n, scalar2=EPS, op0=ALU.mult, op1=ALU.add),
         reads=[bss], writes=[bss])
    P.op("act", lambda e: e.activation(out=c, in_=c, func=AF.Ln), reads=[bss], writes=[bss])
    P.op("act", lambda e: e.activation(out=c, in_=c, func=AF.Exp, scale=-0.5), reads=[bss], writes=[bss])


def load_gain(cx, vec_ap, n, name="gain"):
    P, A = cx.P, cx.A
    g = A.alloc([n], F32)
    bg = P.buf(name)
    P.dma("sp", [(g, vec_ap.partition_broadcast(128))], writes=[bg])
    return g, bg


def ffn_phase(cx, li, which, src, dst):
    P, A, ps, pb = cx.P, cx.A, cx.ps, cx.pb
    A.mark()
    Wg_d = cx.inp["ffn_%s_w_gate" % which][li]
    Wu_d = cx.inp["ffn_%s_w_up" % which][li]
    Wd_d = cx.inp["ffn_%s_w_down" % which][li]
    gn_d = cx.inp["ffn_norm_%s" % which][li]
    wg = A.alloc([8, FF], BF16)
    wu = A.alloc([8, FF], BF16)
    wd = A.alloc([NHC, D], BF16)
    CG = [(0, 768), (768, 1536), (1536, 2304), (2304, 2816)]
    bwg, bwu, bwd = P.bufs(4, "wg"), P.bufs(4, "wu"), P.bufs(4, "wd")
    for gi, (c0, c1) in enumerate(CG):
        P.dma("pool", [(wg[:, :, c0:c1], Wg_d[:, c0:c1].rearrange("(kc p) c -> p kc c", p=128))], writes=[bwg[gi]],
              key="wg%d" % gi)
        P.dma("pool", [(wu[:, :, c0:c1], Wu_d[:, c0:c1].rearrange("(kc p) c -> p kc c", p=128))], writes=[bwu[gi]],
              key="wu%d" % gi)
    for gi, (c0, c1) in enumerate(CG):
        P.dma("pool", [(wd[:, c0 // 128:c1 // 128, :], Wd_d[c0:c1, :].rearrange("(hc p) c -> p hc c", p=128))],
              writes=[bwd[gi]], key="wd%d" % gi)
    gain, bgain = load_gain(cx, gn_d, D)
    hn = [A.alloc([D], F32) for _ in range(2)]
    bhn = P.bufs(2, "hn")
    junk = A.alloc([D], BF16)
    bjunk = P.buf("junk")
    ss = A.alloc([8], F32)
    bss = P.bufs(8, "ss")
    xn = A.alloc([4, D], BF16)
    bxn = P.bufs(4, "xn")
    xT = A.alloc([8, 512], BF16)
    bxT = P.buf("xT")
    sg = [A.alloc([512], BF16) for _ in range(2)]
    bsg = P.bufs(2, "sg")
    hT = A.alloc([NHC, 512], BF16)
    bhT = P.bufs(NHC, "hT")
    hr = [A.alloc([D], F32) for _ in range(2)]
    bhr = P.bufs(2, "hr")
    psA, psB, psD, psT = ps[0:2], ps[2:4], ps[4:6], ps[6]
    bA, bB, bD, bT = pb[0:2], pb[2:4], pb[4:6], pb[6]
    ident = cx.ident
    cnt = {"n": 0, "g": 0, "d": 0}

    def norm_a(tt):
        for j in range(4):
            n = cnt["n"]
            cnt["n"] += 1
            sl = n % 2
            r0 = tt * 512 + j * 128
            P.dma("sp", [(hn[sl], src[r0:r0 + 128, :])], writes=[bhn[sl]], key="hn%d" % sl)
            c = n % 8
            rms_rstd(cx, hn[sl], bhn[sl], D, ss, bss[c], c, junk, bjunk)
            P.op("dve", lambda e, sl=sl, c=c, j=j: e.scalar_tensor_tensor(
                out=xn[:, j, :], in0=hn[sl], scalar=ss[:, c:c + 1], in1=gain, op0=ALU.mult, op1=ALU.mult),
                 reads=[bhn[sl], bss[c], bgain], writes=[bxn[j]])

    def norm_b(tt):
        pst = psT.bitcast(BF16)
        for j in range(4):
            for kc in range(8):
                P.op("pe", lambda e, j=j, kc=kc: e.transpose(out=pst[:, kc * 128:(kc + 1) * 128],
                                                               in_=xn[:, j, kc * 128:(kc + 1) * 128], identity=ident),
                     reads=[bxn[j], cx.bident], writes=[bT], inc=(kc == 7))
            P.op("act", lambda e, j=j: e.copy(out=xT[:, :, j * 128:(j + 1) * 128],
                                              in_=pst.rearrange("p (k t) -> p k t", k=8)),
                 reads=[bT], writes=[bxT])

    def gateup(tt):
        for hc in range(NHC):
            g = cnt["g"]
            cnt["g"] += 1
            s2 = g % 2
            gi = min(hc // 6, 3)
            for kc in range(8):
                P.op("pe", lambda e, hc=hc, kc=kc, s2=s2: e.matmul(psA[s2], lhsT=wg[:, kc, hc * 128:(hc + 1) * 128],
                                                                    rhs=xT[:, kc, :], start=(kc == 0), stop=(kc == 7)),
                     reads=[bxT, bwg[gi]], writes=[bA[s2]], inc=(kc == 7))
            for kc in range(8):
                P.op("pe", lambda e, hc=hc, kc=kc, s2=s2: e.matmul(psB[s2], lhsT=wu[:, kc, hc * 128:(hc + 1) * 128],
                                                                    rhs=xT[:, kc, :], start=(kc == 0), stop=(kc == 7)),
                     reads=[bxT, bwu[gi]], writes=[bB[s2]], inc=(kc == 7))
            P.op("act", lambda e, s2=s2: e.activation(out=sg[s2], in_=psA[s2], func=AF.Silu),
                 reads=[bA[s2]], writes=[bsg[s2]])
            P.op("dve", lambda e, s2=s2, hc=hc: e.tensor_tensor(out=hT[:, hc, :], in0=sg[s2], in1=psB[s2], op=ALU.mult),
                 reads=[bsg[s2], bB[s2]], writes=[bhT[hc]])

    def down(tt):
        for j in range(4):
            r0 = tt * 512 + j * 128
            sl = (tt * 4 + j) % 2
            P.dma("sp", [(hr[sl], src[r0:r0 + 128, :])], writes=[bhr[sl]], key="hr%d" % sl)
            for half in range(2):
                d = cnt["d"]
                cnt["d"] += 1
                s2 = d % 2
                for hc in range(NHC):
                    gi = min(hc // 6, 3)
                    P.op("pe", lambda e, hc=hc, j=j, half=half, s2=s2: e.matmul(
                        psD[s2], lhsT=hT[:, hc, j * 128:(j + 1) * 128], rhs=wd[:, hc, half * 512:(half + 1) * 512],
                        start=(hc == 0), stop=(hc == NHC - 1)),
                         reads=[bhT[hc], bwd[gi]], writes=[bD[s2]], inc=(hc == NHC - 1))
                P.op("dve", lambda e, sl=sl, half=half, s2=s2: e.scalar_tensor_tensor(
                    out=hr[sl][:, half * 512:(half + 1) * 512], in0=psD[s2], scalar=0.5,
                    in1=hr[sl][:, half * 512:(half + 1) * 512], op0=ALU.mult, op1=ALU.add),
                     reads=[bD[s2], bhr[sl]], writes=[bhr[sl]])
            P.dma("sp", [(dst[r0:r0 + 128, :], hr[sl])], reads=[bhr[sl]], key="hrst%d" % sl)

    NT = T // 512
    norm_a(0)
    norm_b(0)
    for tt in range(NT):
        gateup(tt)
        if tt + 1 < NT:
            norm_a(tt + 1)
        down(tt)
        if tt + 1 < NT:
            norm_b(tt + 1)
    P.barrier()
    A.release()


def run_pipeline(stages, L):
    n = len(stages)
    for i in range(n + L):
        if i < n:
            stages[i][0]()
        if i >= L:
            stages[i - L][1]()


def norm_T_tile(cx, src, r0, gain, bgain, hn, bhn, sl, ss, bss, c, junk, bjunk, xn, bxn, psT, bT, dstT, bdstT, col0):
    P = cx.P
    P.dma("sp", [(hn[sl], src[r0:r0 + 128, :])], writes=[bhn[sl]], key="mhn%d" % sl)
    rms_rstd(cx, hn[sl], bhn[sl], D, ss, bss[c], c, junk, bjunk)
    P.op("dve", lambda e: e.scalar_tensor_tensor(out=xn[sl], in0=hn[sl], scalar=ss[:, c:c + 1], in1=gain,
                                                 op0=ALU.mult, op1=ALU.mult),
         reads=[bhn[sl], bss[c], bgain], writes=[bxn[sl]])
    pst = psT.bitcast(BF16)
    for kc in range(8):
        P.op("pe", lambda e, kc=kc: e.transpose(out=pst[:, kc * 128:(kc + 1) * 128], in_=xn[sl][:, kc * 128:(kc + 1) * 128],
                                                identity=cx.ident),
             reads=[bxn[sl], cx.bident], writes=[bT], inc=(kc == 7))
    P.op("act", lambda e: e.copy(out=dstT[:, 0:8, col0:col0 + 128], in_=pst.rearrange("p (k t) -> p k t", k=8)),
         reads=[bT], writes=[bdstT])


def out_proj_tile(cx, o_tile, bo, wo, bwo, src, dst, r0, oT, boT, hr, bhr, sl, psT, bT, psW, bW):
    P = cx.P
    pst = psT.bitcast(BF16)
    for kc in range(8):
        P.op("pe", lambda e, kc=kc: e.transpose(out=pst[:, kc * 128:(kc + 1) * 128], in_=o_tile[:, kc * 128:(kc + 1) * 128],
                                                identity=cx.ident),
             reads=[bo, cx.bident], writes=[bT], inc=(kc == 7))
    P.op("act", lambda e: e.copy(out=oT[sl], in_=pst.rearrange("p (k t) -> p k t", k=8)), reads=[bT], writes=[boT[sl]])
    P.dma("sp", [(hr[sl], src[r0:r0 + 128, :])], writes=[bhr[sl]], key="mhr%d" % sl)
    for half in range(2):
        for kc in range(8):
            P.op("pe", lambda e, kc=kc, half=half: e.matmul(psW[half], lhsT=oT[sl][:, kc, :],
                                                            rhs=wo[:, kc, half * 512:(half + 1) * 512],
                                                            start=(kc == 0), stop=(kc == 7)),
                 reads=[boT[sl], bwo], writes=[bW[half]], inc=(kc == 7))
        P.op("dve", lambda e, half=half: e.tensor_tensor(out=hr[sl][:, half * 512:(half + 1) * 512], in0=psW[half],
                                                         in1=hr[sl][:, half * 512:(half + 1) * 512], op=ALU.add),
             reads=[bW[half], bhr[sl]], writes=[bhr[sl]])
    P.dma("sp", [(dst[r0:r0 + 128, :], hr[sl])], reads=[bhr[sl]], key="mhrst%d" % sl)


def mla_phase(cx, j, li, src, dst):
    P, A, ps, pb = cx.P, cx.A, cx.ps, cx.pb
    A.mark()
    scale = 96.0 ** -0.5
    Win_d, Wuq_d, Wukv_d, Wo_d = cx.inp["mla_w_in"][j], cx.inp["mla_w_uq"][j], cx.inp["mla_w_ukv"][j], cx.inp["mla_w_o"][j]
    win = A.alloc([8, 672], BF16)
    wkr = A.alloc([8, 96], BF16)
    wq = A.alloc([3, 1536], BF16)
    wqs = A.alloc([3, 16, 96], BF16)
    wkn = A.alloc([2, 16, 64], BF16)
    wv = A.alloc([2, 16, 64], BF16)
    wo = A.alloc([8, 1024], BF16)
    bwin, bwkr, bwq, bwqs, bwkn, bwv, bwo = (P.buf(n) for n in ["win", "wkr", "wq", "wqs", "wkn", "wv", "wo"])
    P.dma("pool", [(win, Win_d.rearrange("(kc p) c -> p kc c", p=128))], writes=[bwin])
    w_in_r = Win_d.rearrange("(kc p) c -> p kc c", p=128)
    P.op("dve", lambda e: e.memset(wkr, 0.0), writes=[bwkr])
    P.dma("pool", [(wkr[:, :, 64:80], w_in_r[:, :, 656:672]), (wkr[:, :, 80:96], w_in_r[:, :, 640:656])], writes=[bwkr])
    P.op("act", lambda e: e.mul(out=wkr[:, :, 64:80], in_=wkr[:, :, 64:80], mul=-1.0), reads=[bwkr], writes=[bwkr])
    wuq_r = Wuq_d.rearrange("(kc p) (h d) -> p kc h d", p=128, d=96)
    P.dma("pool", [(wq, Wuq_d.rearrange("(kc p) c -> p kc c", p=128))], writes=[bwq])
    P.op("dve", lambda e: e.memset(wqs, 0.0), writes=[bwqs])
    for kc in range(3):
        P.dma("pool", [(wqs[:, kc, :, 64:80], wuq_r[:, kc, :, 80:96]), (wqs[:, kc, :, 80:96], wuq_r[:, kc, :, 64:80])],
              writes=[bwqs], key="wqs")
    P.op("act", lambda e: e.mul(out=wqs[:, :, :, 64:80], in_=wqs[:, :, :, 64:80], mul=-1.0), reads=[bwqs], writes=[bwqs])
    wukv_r = Wukv_d.rearrange("(kc p) (h d) -> p kc h d", p=128, d=128)
    for kc in range(2):
        P.dma("pool", [(wkn[:, kc, :, :], wukv_r[:, kc, :, 0:64])], writes=[bwkn], key="wkn")
        P.dma("pool", [(wv[:, kc, :, :], wukv_r[:, kc, :, 64:128])], writes=[bwv], key="wv")
    P.dma("pool", [(wo, Wo_d.rearrange("(kc p) c -> p kc c", p=128))], writes=[bwo])
    gain, bgain = load_gain(cx, cx.inp["mix_norm"][li], D)
    gq, bgq = load_gain(cx, cx.inp["mla_q_norm"][j], 384, "gainq")
    gkv, bgkv = load_gain(cx, cx.inp["mla_kv_norm"][j], 256, "gainkv")
    CC = A.alloc([S], F32)
    SS = A.alloc([S], F32)
    bcs = P.buf("cs")
    P.dma("sp", [(CC[64:96, :], cx.cd["c_rope_cos"]), (SS[64:96, :], cx.cd["c_rope_sin"])], writes=[bcs])
    mdiag = A.alloc([128], F32)
    bmd = P.buf("mdiag")
    P.dma("sp", [(mdiag, cx.cd["c_mdiag"])], writes=[bmd])
    junk = A.alloc([D], BF16)
    bjunk = P.buf("junk")
    ss = A.alloc([8], F32)
    bss = P.bufs(8, "ss")
    ssq = A.alloc([8], F32)
    bssq = P.bufs(8, "ssq")
    cqnT = A.alloc([3, S], BF16)
    ckvnT = A.alloc([2, S], BF16)
    bcqnT, bckvnT = P.buf("cqnT"), P.buf("ckvnT")
    kT = [A.alloc([S], BF16) for _ in range(2)]
    bkT = P.bufs(2, "kT")
    o_all = A.alloc([16, D], BF16)
    bo_all = P.bufs(16, "o_all")
    psT, bT = ps[6], pb[6]

    for sq in range(NSEQ):
        tok0 = sq * S
        A.mark()
        hn = [A.alloc([D], F32) for _ in range(2)]
        bhn = P.bufs(2, "hn")
        xn = [A.alloc([D], BF16) for _ in range(2)]
        bxn = P.bufs(2, "xn")
        mT = A.alloc([8, 512], BF16)
        bmT = P.buf("mT")
        cqn = [A.alloc([384], BF16) for _ in range(2)]
        ckvn = [A.alloc([256], BF16) for _ in range(2)]
        bcqn, bckvn = P.bufs(2, "cqn"), P.bufs(2, "ckvn")
        tmpa = A.alloc([512], F32)
        tmpb = A.alloc([512], F32)
        btmpa, btmpb = P.buf("tmpa"), P.buf("tmpb")
        n = 0
        for c in range(4):
            for jj in range(4):
                norm_T_tile(cx, src, tok0 + c * 512 + jj * 128, gain, bgain, hn, bhn, n % 2, ss, bss, n % 8, junk, bjunk,
                            xn, bxn, psT, bT, mT, bmT, jj * 128)
                n += 1
            for kc in range(8):
                P.op("pe", lambda e, kc=kc: e.matmul(ps[4][0:96, :], lhsT=win[:, kc, 576:672], rhs=mT[:, kc, :],
                                                     start=(kc == 0), stop=(kc == 7)),
                     reads=[bwin, bmT], writes=[pb[4]], inc=(kc == 7))
            for kc in range(8):
                P.op("pe", lambda e, kc=kc: e.matmul(ps[5][0:96, :], lhsT=wkr[:, kc, :], rhs=mT[:, kc, :],
                                                     start=(kc == 0), stop=(kc == 7)),
                     reads=[bwkr, bmT], writes=[pb[5]], inc=(kc == 7))
            cs = slice(c * 512, (c + 1) * 512)
            P.op("dve", lambda e, cs=cs: e.tensor_tensor(out=tmpa[64:96, :], in0=ps[4][64:96, :], in1=CC[64:96, cs], op=ALU.mult),
                 reads=[pb[4], bcs], writes=[btmpa])
            P.op("dve", lambda e, cs=cs: e.tensor_tensor(out=tmpb[64:96, :], in0=ps[5][64:96, :], in1=SS[64:96, cs], op=ALU.mult),
                 reads=[pb[5], bcs], writes=[btmpb])
            for b2 in range(2):
                P.op("dve", lambda e, cs=cs, b2=b2: e.tensor_tensor(out=kT[b2][64:96, cs], in0=tmpa[64:96, :], in1=tmpb[64:96, :],
                                                                    op=ALU.add),
                     reads=[btmpa, btmpb], writes=[bkT[b2]])
            for jj in range(4):
                tsl = slice(jj * 128, (jj + 1) * 128)
                s2 = jj % 2
                for kc in range(8):
                    P.op("pe", lambda e, kc=kc, tsl=tsl, s2=s2: e.matmul(ps[s2][:, 0:384], lhsT=mT[:, kc, tsl],
                                                                        rhs=win[:, kc, 0:384], start=(kc == 0), stop=(kc == 7)),
                         reads=[bwin, bmT], writes=[pb[s2]], inc=(kc == 7))
                for kc in range(8):
                    P.op("pe", lambda e, kc=kc, tsl=tsl, s2=s2: e.matmul(ps[2 + s2][:, 0:256], lhsT=mT[:, kc, tsl],
                                                                        rhs=win[:, kc, 384:640], start=(kc == 0), stop=(kc == 7)),
                         reads=[bwin, bmT], writes=[pb[2 + s2]], inc=(kc == 7))
                cq = (c * 4 + jj) % 8
                rms_rstd(cx, ps[s2][:, 0:384], pb[s2], 384, ssq, bssq[cq], cq, junk, bjunk)
                P.op("dve", lambda e, s2=s2, cq=cq: e.scalar_tensor_tensor(out=cqn[s2], in0=ps[s2][:, 0:384],
                                                                           scalar=ssq[:, cq:cq + 1], in1=gq,
                                                                           op0=ALU.mult, op1=ALU.mult),
                     reads=[pb[s2], bssq[cq], bgq], writes=[bcqn[s2]])
                rms_rstd(cx, ps[2 + s2][:, 0:256], pb[2 + s2], 256, ss, bss[cq], cq, junk, bjunk)
                P.op("dve", lambda e, s2=s2, cq=cq: e.scalar_tensor_tensor(out=ckvn[s2], in0=ps[2 + s2][:, 0:256],
                                                                           scalar=ss[:, cq:cq + 1], in1=gkv,
                                                                           op0=ALU.mult, op1=ALU.mult),
                     reads=[pb[2 + s2], bss[cq], bgkv], writes=[bckvn[s2]])
                pst = psT.bitcast(BF16)
                for kc in range(3):
                    P.op("pe", lambda e, kc=kc, s2=s2: e.transpose(out=pst[:, kc * 128:(kc + 1) * 128],
                                                                   in_=cqn[s2][:, kc * 128:(kc + 1) * 128], identity=cx.ident),
                         reads=[bcqn[s2], cx.bident], writes=[bT], inc=False)
                for kc in range(2):
                    P.op("pe", lambda e, kc=kc, s2=s2: e.transpose(out=pst[:, (3 + kc) * 128:(4 + kc) * 128],
                                                                   in_=ckvn[s2][:, kc * 128:(kc + 1) * 128], identity=cx.ident),
                         reads=[bckvn[s2], cx.bident], writes=[bT], inc=(kc == 1))
                g0 = c * 512 + jj * 128
                P.op("act", lambda e, g0=g0: e.copy(out=cqnT[:, :, g0:g0 + 128],
                                                    in_=pst[:, 0:384].rearrange("p (k t) -> p k t", k=3)),
                     reads=[bT], writes=[bcqnT])
                P.op("act", lambda e, g0=g0: e.copy(out=ckvnT[:, :, g0:g0 + 128],
                                                    in_=pst[:, 384:640].rearrange("p (k t) -> p k t", k=2)),
                     reads=[bT], writes=[bckvnT])
        P.barrier()
        A.release()
        A.mark()
        qT = [A.alloc([S], BF16) for _ in range(2)]
        bqT = P.bufs(2, "qT")
        vaug = [A.alloc([16, 65], BF16) for _ in range(2)]
        bva = P.bufs(2, "vaug")
        for b2 in range(2):
            P.op("dve", lambda e, b2=b2: e.memset(vaug[b2][:, :, 64:65], 1.0), writes=[bva[b2]])
        E = [A.alloc([512], BF16) for _ in range(6)]
        bE = P.bufs(6, "E")
        tmpd = [A.alloc([128], F32) for _ in range(2)]
        btd = P.bufs(2, "tmpd")
        tq1 = A.alloc([512], F32)
        tq2 = A.alloc([512], F32)
        btq1, btq2 = P.buf("tq1"), P.buf("tq2")
        rec = A.alloc([8], F32)
        brec = P.bufs(2, "rec")
        nS = 0
        nE = 0
        nD = 0
        for h in range(16):
            hb = h % 2
            for c in range(4):
                cs = slice(c * 512, (c + 1) * 512)
                for kc in range(3):
                    P.op("pe", lambda e, kc=kc, cs=cs, h=h: e.matmul(ps[2][0:96, :], lhsT=wq[:, kc, h * 96:(h + 1) * 96], rhs=cqnT[:, kc, cs],
                                                                    start=(kc == 0), stop=(kc == 2)),
                         reads=[bwq, bcqnT], writes=[pb[2]], inc=(kc == 2))
                for kc in range(3):
                    P.op("pe", lambda e, kc=kc, cs=cs, h=h: e.matmul(ps[3][0:96, :], lhsT=wqs[:, kc, h, :], rhs=cqnT[:, kc, cs],
                                                                    start=(kc == 0), stop=(kc == 2)),
                         reads=[bwqs, bcqnT], writes=[pb[3]], inc=(kc == 2))
                for kc in range(2):
                    P.op("pe", lambda e, kc=kc, cs=cs, h=h: e.matmul(ps[4][0:64, :], lhsT=wkn[:, kc, h, :], rhs=ckvnT[:, kc, cs],
                                                                    start=(kc == 0), stop=(kc == 1)),
                         reads=[bwkn, bckvnT], writes=[pb[4]], inc=(kc == 1))
                P.op("act", lambda e, cs=cs, hb=hb: e.copy(out=qT[hb][0:64, cs], in_=ps[2][0:64, :]),
                     reads=[pb[2]], writes=[bqT[hb]])
                P.op("dve", lambda e, cs=cs: e.tensor_tensor(out=tq1[64:96, :], in0=ps[2][64:96, :], in1=CC[64:96, cs], op=ALU.mult),
                     reads=[pb[2], bcs], writes=[btq1])
                P.op("dve", lambda e, cs=cs: e.tensor_tensor(out=tq2[64:96, :], in0=ps[3][64:96, :], in1=SS[64:96, cs], op=ALU.mult),
                     reads=[pb[3], bcs], writes=[btq2])
                P.op("dve", lambda e, cs=cs, hb=hb: e.tensor_tensor(out=qT[hb][64:96, cs], in0=tq1[64:96, :], in1=tq2[64:96, :],
                                                                    op=ALU.add),
                     reads=[btq1, btq2], writes=[bqT[hb]])
                P.op("act", lambda e, cs=cs, hb=hb: e.copy(out=kT[hb][0:64, cs], in_=ps[4][0:64, :]),
                     reads=[pb[4]], writes=[bkT[hb]])
            for g8 in range(2):
                for kk in range(8):
                    kb = g8 * 8 + kk
                    for kc in range(2):
                        P.op("pe", lambda e, kc=kc, kb=kb, kk=kk, h=h: e.matmul(
                            ps[3][:, kk * 64:(kk + 1) * 64], lhsT=ckvnT[:, kc, kb * 128:(kb + 1) * 128], rhs=wv[:, kc, h, :],
                            start=(kc == 0), stop=(kc == 1)),
                             reads=[bwv, bckvnT], writes=[pb[3]], inc=(kc == 1 and kk == 7))
                P.op("act", lambda e, g8=g8, hb=hb: e.copy(out=vaug[hb][:, g8 * 8:(g8 + 1) * 8, 0:64],
                                                           in_=ps[3].rearrange("p (k d) -> p k d", k=8)),
                     reads=[pb[3]], writes=[bva[hb]])
            stages = []
            SB = [0, 1, 5]
            for c in range(4):
                ob = c % 2
                psO, bO = ps[6 + ob], pb[6 + ob]
                nkb = 4 * c + 4
                for kb in range(nkb):
                    qlo = max(kb, 4 * c)
                    ncol = (4 * c + 4 - qlo) * 128
                    sb = SB[nS % 3]
                    nS += 1
                    eb = nE % 6
                    nE += 1
                    diag = kb >= 4 * c
                    db = nD % 2
                    if diag:
                        nD += 1

                    def front(kb=kb, qlo=qlo, ncol=ncol, sb=sb, eb=eb, diag=diag, db=db, hb=hb):
                        P.op("pe", lambda e: e.matmul(ps[sb][:, 0:ncol], lhsT=kT[hb][0:96, kb * 128:(kb + 1) * 128],
                                                      rhs=qT[hb][0:96, qlo * 128:qlo * 128 + ncol], start=True, stop=True),
                             reads=[bkT[hb], bqT[hb]], writes=[pb[sb]])
                        c0 = 0
                        if diag:
                            P.op("dve", lambda e: e.tensor_tensor(out=tmpd[db], in0=ps[sb][:, 0:128], in1=mdiag, op=ALU.add),
                                 reads=[pb[sb], bmd], writes=[btd[db]])
                            P.op("act", lambda e: e.activation(out=E[eb][:, 0:128], in_=tmpd[db], func=AF.Exp, scale=scale),
                                 reads=[btd[db]], writes=[bE[eb]])
                            c0 = 128
                        if ncol > c0:
                            P.op("act", lambda e: e.activation(out=E[eb][:, c0:ncol], in_=ps[sb][:, c0:ncol], func=AF.Exp,
                                                               scale=scale), reads=[pb[sb]], writes=[bE[eb]])

                    def back(kb=kb, qlo=qlo, eb=eb, c=c, ob=ob, psO=psO, bO=bO, hb=hb, h=h, nkb=nkb):
                        nq = 4 * c + 4 - qlo
                        for qi in range(nq):
                            oi = qlo + qi - 4 * c
                            P.op("pe", lambda e, qi=qi, oi=oi: e.matmul(
                                psO[:, oi * 65:(oi + 1) * 65], lhsT=E[eb][:, qi * 128:(qi + 1) * 128], rhs=vaug[hb][:, kb, :],
                                start=(kb == 0 and qi == 0), stop=False, skip_group_check=True),
                                 reads=[bE[eb], bva[hb]], writes=[bO], inc=(qi == nq - 1))
                        if kb == nkb - 1:
                            rb = brec[ob]
                            P.op("dve", lambda e: e.reciprocal(out=rec[:, ob * 4:(ob + 1) * 4],
                                                               in_=psO[:, 0:260].rearrange("p (q d) -> p q d", d=65)[:, :, 64]),
                                 reads=[bO], writes=[rb])
                            for oi in range(4):
                                qb = 4 * c + oi
                                P.op("dve", lambda e, oi=oi, qb=qb: e.tensor_scalar(
                                    out=o_all[:, qb, h * 64:(h + 1) * 64], in0=psO[:, oi * 65:oi * 65 + 64],
                                    scalar1=rec[:, ob * 4 + oi:ob * 4 + oi + 1], scalar2=None, op0=ALU.mult),
                                     reads=[bO, rb], writes=[bo_all[qb]])

                    stages.append((front, back))
            run_pipeline(stages, 2)
        P.barrier()
        A.release()
        A.mark()
        oT = [A.alloc([8, 128], BF16) for _ in range(2)]
        boT = P.bufs(2, "oT")
        hr = [A.alloc([D], F32) for _ in range(2)]
        bhr = P.bufs(2, "hr")
        for qb in range(16):
            out_proj_tile(cx, o_all[:, qb, :], bo_all[qb], wo, bwo, src, dst, tok0 + qb * 128, oT, boT, hr, bhr, qb % 2,
                          psT, bT, ps[0:2], pb[0:2])
        P.barrier()
        A.release()
    A.release()


def rev_cols(ap, start, n):
    a = [list(x) for x in ap.ap]
    assert len(a) == 2 and a[1][0] == 1, a
    return bass.AP(ap.tensor, ap.offset + start, [a[0], [-1, n]])


def nsa_setup(cx):
    P, A, ps, pb = cx.P, cx.A, cx.ps, cx.pb
    nc = cx.nc
    cx.rtab_t = nc.dram_tensor("rtab", [16, 4096], F32)
    rtab = cx.rtab_t.ap()
    A.mark()
    tbl = A.alloc([16], F32)
    btbl = P.buf("tbl")
    P.op("dve", lambda e: e.memset(tbl[0:64, :], NEGM), writes=[btbl])
    P.dma("sp", [(tbl[0:32, :], cx.inp["rel_bias"])], writes=[btbl])
    oh = A.alloc([4096], F32)
    boh = P.buf("oh")
    P.dma("sp", [(oh[0:33, :], cx.cd["c_oh"])], writes=[boh])
    rt = A.alloc([4096], F32)
    brt = P.buf("rt")
    for ch in range(8):
        b = ch % 2
        P.op("pe", lambda e, ch=ch, b=b: e.matmul(ps[b][0:16, :], lhsT=tbl[0:33, :], rhs=oh[0:33, ch * 512:(ch + 1) * 512],
                                                  start=True, stop=True),
             reads=[btbl, boh], writes=[pb[b]])
        P.op("act", lambda e, ch=ch, b=b: e.copy(out=rt[0:16, ch * 512:(ch + 1) * 512], in_=ps[b][0:16, :]),
             reads=[pb[b]], writes=[brt])
    P.dma("sp", [(rtab, rt[0:16, :])], reads=[brt], key="rtab_st")
    P.barrier()
    A.release()
    dbg(cx, 0)


def nsa_phase(cx, j, li, src, dst):
    P, A, ps, pb = cx.P, cx.A, cx.ps, cx.pb
    A.mark()
    scale = 0.125
    Win_d = cx.inp["nsa_w_in"][j]
    w_in_r = Win_d.rearrange("(kc p) c -> p kc c", p=128)
    rt = cx.rtab_t
    W1 = [A.alloc([32, 128], BF16) for _ in range(2)]
    bW1 = P.bufs(2, "W1")
    w2 = [A.alloc([64], BF16) for _ in range(2)]
    bw2 = P.bufs(2, "w2")
    for kv, nm in enumerate(["k", "v"]):
        P.dma("pool", [(W1[kv][0:64], cx.inp["nsa_cmp_w1_%s" % nm][j].rearrange("(l d) c -> d l c", d=64))], writes=[bW1[kv]])
        P.dma("pool", [(w2[kv], cx.inp["nsa_cmp_w2_%s" % nm][j])], writes=[bw2[kv]])
    posf = A.alloc([2, 32], F32)
    posb = A.alloc([2, 32], BF16)
    bposf, bposb = P.buf("posf"), P.buf("posb")
    P.dma("sp", [(posf[0:64, 0, :], cx.inp["nsa_cmp_pos_k"][j].rearrange("l d -> d l")),
                 (posf[0:64, 1, :], cx.inp["nsa_cmp_pos_v"][j].rearrange("l d -> d l"))], writes=[bposf],
          allow_slow_non_contiguous=True)
    P.op("act", lambda e: e.copy(out=posb[0:64], in_=posf[0:64]), reads=[bposf], writes=[bposb])
    cpos = A.alloc([2], F32)
    bcpos = P.buf("cpos")
    for kv in range(2):
        for l in range(32):
            P.op("pe", lambda e, kv=kv, l=l: e.matmul(ps[4 + kv][:, 0:1], lhsT=W1[kv][0:64, l, :], rhs=posb[0:64, kv, l:l + 1],
                                                      start=(l == 0), stop=(l == 31)),
                 reads=[bW1[kv], bposb], writes=[pb[4 + kv]], inc=(l == 31))
        P.op("act", lambda e, kv=kv: e.copy(out=cpos[:, kv:kv + 1], in_=ps[4 + kv][:, 0:1]), reads=[pb[4 + kv]], writes=[bcpos])
    gain, bgain = load_gain(cx, cx.inp["mix_norm"][li], D)
    c31 = A.alloc([16], F32)
    bc31 = P.buf("c31")
    P.dma("sp", [(c31, cx.inp["rel_bias"][31].partition_broadcast(128))], writes=[bc31])
    m4 = A.alloc([128], F32)
    bm4 = P.buf("m4")
    P.dma("sp", [(m4, cx.cd["c_m4"])], writes=[bm4])
    selc = A.alloc([2, 16, 32], F32)
    bselc = P.buf("selc")
    P.dma("sp", [(selc[:, 0], cx.cd["c_cm"]), (selc[:, 1], cx.cd["c_add"])], writes=[bselc])
    junk = A.alloc([D], BF16)
    bjunk = P.buf("junk")
    ss = A.alloc([8], F32)
    bss = P.bufs(8, "ss")
    psT, bT = ps[6], pb[6]
    dbg(cx, 1)

    for sq in range(NSEQ):
        tok0 = sq * S
        mT = A.alloc([8, S], BF16) if sq == 0 else mT
        gates = A.alloc([16, 48], F32) if sq == 0 else gates
        o_all = A.alloc([16, D], BF16) if sq == 0 else o_all
        if sq == 0:
            bmT, bgates = P.buf("mT"), P.bufs(16, "gates")
            bo_all = P.bufs(16, "o_all")
        A.mark()
        wgt = A.alloc([8, 48], BF16)
        bwgt = P.buf("wgt")
        P.dma("pool", [(wgt, w_in_r[:, :, 2560:2608])], writes=[bwgt])
        hn = [A.alloc([D], F32) for _ in range(2)]
        bhn = P.bufs(2, "hn")
        xn = [A.alloc([D], BF16) for _ in range(2)]
        bxn = P.bufs(2, "xn")
        for qb in range(16):
            norm_T_tile(cx, src, tok0 + qb * 128, gain, bgain, hn, bhn, qb % 2, ss, bss, qb % 8, junk, bjunk,
                        xn, bxn, psT, bT, mT, bmT, qb * 128)
            b = qb % 2
            for kc in range(8):
                P.op("pe", lambda e, kc=kc, qb=qb, b=b: e.matmul(ps[b][:, 0:48], lhsT=mT[:, kc, qb * 128:(qb + 1) * 128],
                                                                rhs=wgt[:, kc, :], start=(kc == 0), stop=(kc == 7)),
                     reads=[bmT, bwgt], writes=[pb[b]], inc=(kc == 7))
            P.op("act", lambda e, qb=qb, b=b: e.activation(out=gates[:, qb, :], in_=ps[b][:, 0:48], func=AF.Sigmoid),
                 reads=[pb[b]], writes=[bgates[qb]])
        P.barrier()
        A.release()
        dbg(cx, 2)
        for g in range(4):
            A.mark()
            wg_ = A.alloc([8, 640], BF16)
            bwg_ = P.buf("wing")
            prs = [(wg_[:, :, 0:256], w_in_r[:, :, g * 256:(g + 1) * 256])]
            for i in range(6):
                prs.append((wg_[:, :, 256 + i * 64:320 + i * 64], w_in_r[:, :, 1024 + i * 256 + g * 64:1024 + i * 256 + (g + 1) * 64]))
            P.dma("pool", prs, writes=[bwg_])
            x01 = A.alloc([4, 256], F32)
            bx01 = P.buf("x01")
            P.dma("sp", [(x01, bass.AP(rt, (4 * g) * 4096 + 1792, [[1, 128], [4096, 4], [1, 256]]))], writes=[bx01])
            qa = [A.alloc([S], BF16) for _ in range(4)]
            bqa = P.bufs(4, "qa")
            ka = A.alloc([S], BF16)
            bka = P.buf("ka")
            P.dma("sp", [(ka[64:96, :], cx.cd["c_bexp"])], writes=[bka])
            kw = A.alloc([S], BF16)
            kcf = A.alloc([S], BF16)
            vcf = A.alloc([S], BF16)
            bkw, bkcf, bvcf = P.buf("kw"), P.buf("kcf"), P.buf("vcf")
            vs = A.alloc([16, 65], BF16)
            vw = A.alloc([16, 65], BF16)
            bvs, bvw = P.buf("vs"), P.buf("vw")
            P.op("dve", lambda e: e.memset(vs[:, :, 64:65], 1.0), writes=[bvs])
            P.op("dve", lambda e: e.memset(vw[:, :, 64:65], 1.0), writes=[bvw])
            kcT = A.alloc([128], BF16)
            bkcT = P.buf("kcT")
            vcx = A.alloc([97], BF16)
            bvcx = P.buf("vcx")
            P.op("dve", lambda e: e.memset(vcx[:, 64:65], 1.0), writes=[bvcx])
            P.dma("sp", [(vcx[0:127, 65:97], cx.cd["c_ov"])], writes=[bvcx])
            ocmp = A.alloc([16, 4, 64], BF16)
            bocmp = P.bufs(16, "ocmp")
            imp = A.alloc([16, 32], F32)
            bimp = P.bufs(16, "imp")
            nst = A.alloc([S], BF16)
            bnst = P.buf("nst")
            pi = 0
            fm = [(qa[0], bqa[0], 0), (qa[1], bqa[1], 64), (qa[2], bqa[2], 128), (qa[3], bqa[3], 192),
                  (kcf, bkcf, 256), (vcf, bvcf, 320), (ka, bka, 384), (kw, bkw, 512)]
            for (dt_, bdt, c0) in fm:
                for c in range(4):
                    b = 4 + pi % 2
                    pi += 1
                    cs = slice(c * 512, (c + 1) * 512)
                    for kc in range(8):
                        P.op("pe", lambda e, kc=kc, cs=cs, c0=c0, b=b: e.matmul(ps[b][0:64, :], lhsT=wg_[:, kc, c0:c0 + 64],
                                                                               rhs=mT[:, kc, cs], start=(kc == 0), stop=(kc == 7)),
                             reads=[bwg_, bmT], writes=[pb[b]], inc=(kc == 7))
                    P.op("act", lambda e, dt_=dt_, cs=cs, b=b: e.copy(out=dt_[0:64, cs], in_=ps[b][0:64, :]),
                         reads=[pb[b]], writes=[bdt])
            for (vt, bvt, c0) in [(vs, bvs, 448), (vw, bvw, 576)]:
                for g8 in range(2):
                    b = 4 + pi % 2
                    pi += 1
                    for kk in range(8):
                        kb = g8 * 8 + kk
                        for kc in range(8):
                            P.op("pe", lambda e, kc=kc, kb=kb, kk=kk, c0=c0, b=b: e.matmul(
                                ps[b][:, kk * 64:(kk + 1) * 64], lhsT=mT[:, kc, kb * 128:(kb + 1) * 128], rhs=wg_[:, kc, c0:c0 + 64],
                                start=(kc == 0), stop=(kc == 7)),
                                 reads=[bwg_, bmT], writes=[pb[b]], inc=(kc == 7 and kk == 7))
                    P.op("act", lambda e, vt=vt, g8=g8, b=b: e.copy(out=vt[:, g8 * 8:(g8 + 1) * 8, 0:64],
                                                                  in_=ps[b].rearrange("p (k d) -> p k d", k=8)),
                         reads=[pb[b]], writes=[bvt])
            dbg(cx, 3)
            xh = A.alloc([128], F32)
            x2 = A.alloc([128], F32)
            sgm = A.alloc([128], F32)
            gh = A.alloc([128], BF16)
            bxh, bx2, bsgm, bgh = P.buf("xh"), P.buf("x2"), P.buf("sgm"), P.buf("gh")
            for kv, (cf, bcf) in enumerate([(kcf, bkcf), (vcf, bvcf)]):
                b = 4 + kv
                for l in range(32):
                    P.op("pe", lambda e, kv=kv, l=l, cf=cf, b=b: e.matmul(ps[b][:, 0:127], lhsT=W1[kv][0:64, l, :],
                                                                         rhs=cf[0:64, l:l + 16 * 126 + 1:16],
                                                                         start=(l == 0), stop=(l == 31)),
                         reads=[bW1[kv], bcf], writes=[pb[b]], inc=(l == 31))
                P.op("act", lambda e, kv=kv, b=b: e.activation(out=xh[:, 0:127], in_=ps[b][:, 0:127], func=AF.Identity,
                                                               bias=cpos[:, kv:kv + 1], scale=1.0),
                     reads=[pb[b], bcpos], writes=[bxh])
                P.op("dve", lambda e: e.tensor_tensor(out=x2[:, 0:127], in0=xh[:, 0:127], in1=xh[:, 0:127], op=ALU.mult),
                     reads=[bxh], writes=[bx2])
                P.op("dve", lambda e: e.tensor_scalar(out=x2[:, 0:127], in0=x2[:, 0:127], scalar1=0.044715, scalar2=1.0,
                                                      op0=ALU.mult, op1=ALU.add), reads=[bx2], writes=[bx2])
                P.op("dve", lambda e: e.tensor_tensor(out=x2[:, 0:127], in0=x2[:, 0:127], in1=xh[:, 0:127], op=ALU.mult),
                     reads=[bx2, bxh], writes=[bx2])
                P.op("act", lambda e: e.activation(out=sgm[:, 0:127], in_=x2[:, 0:127], func=AF.Sigmoid, scale=1.5957691216057308),
                     reads=[bx2], writes=[bsgm])
                P.op("dve", lambda e: e.tensor_tensor(out=gh[:, 0:127], in0=xh[:, 0:127], in1=sgm[:, 0:127], op=ALU.mult),
                     reads=[bxh, bsgm], writes=[bgh])
                if kv == 0:
                    P.op("pe", lambda e: e.matmul(ps[6][0:64, 0:127], lhsT=w2[0], rhs=gh[:, 0:127], start=True, stop=True),
                         reads=[bw2[0], bgh], writes=[pb[6]])
                    P.op("act", lambda e: e.copy(out=kcT[0:64, 0:127], in_=ps[6][0:64, 0:127]), reads=[pb[6]], writes=[bkcT])
                else:
                    P.op("pe", lambda e: e.matmul(ps[6][0:127, 0:64], lhsT=gh[:, 0:127], rhs=w2[1], start=True, stop=True),
                         reads=[bw2[1], bgh], writes=[pb[6]])
                    P.op("act", lambda e: e.copy(out=vcx[0:127, 0:64], in_=ps[6][0:127, 0:64]), reads=[pb[6]], writes=[bvcx])
            dbg(cx, 4)
            xb = [A.alloc([S], F32) for _ in range(2)]
            bxb = P.bufs(2, "xb")
            tmpc = [A.alloc([512], F32) for _ in range(2)]
            btmpc = P.bufs(2, "tmpc")
            Ec = [A.alloc([512], BF16) for _ in range(3)]
            bEc = P.bufs(3, "Ec")
            rc = A.alloc([8], F32)
            brc = P.bufs(2, "rc")
            rg = A.alloc([8], F32)
            brg = P.bufs(2, "rg")
            nc_ = 0
            stages3 = []
            SB3 = [0, 1, 7]
            for r in range(4):
                h = 4 * g + r
                xbr = xb[r % 2]
                bxbr = bxb[r % 2]
                for c in range(4):
                    sb = SB3[nc_ % 3]
                    eb = nc_ % 3
                    tb = nc_ % 2
                    ob = nc_ % 2
                    nc_ += 1
                    psC, bC = ps[2 + ob], pb[2 + ob]

                    def front(r=r, h=h, c=c, sb=sb, eb=eb, tb=tb, xbr=xbr, bxbr=bxbr):
                        if c == 0:
                            P.dma("sp", [(xbr[0:127, :], bass.AP(rt, h * 4096 + 31, [[16, 127], [1, 2048]]))], writes=[bxbr],
                                  key="xb%d" % (r % 2))
                        cs = slice(c * 512, (c + 1) * 512)
                        P.op("pe", lambda e: e.matmul(ps[sb][0:127, :], lhsT=kcT[0:64, 0:127], rhs=qa[r][0:64, cs],
                                                      start=True, stop=True),
                             reads=[bkcT, bqa[r]], writes=[pb[sb]])
                        P.op("dve", lambda e: e.scalar_tensor_tensor(
                            out=tmpc[tb][0:127, :], in0=ps[sb][0:127, :], scalar=scale, in1=rev_cols(xbr[0:127, :], 2047 - c * 512, 512),
                            op0=ALU.mult, op1=ALU.add), reads=[pb[sb], bxbr], writes=[btmpc[tb]])
                        P.op("act", lambda e: e.activation(out=Ec[eb][0:127, :], in_=tmpc[tb][0:127, :], func=AF.Exp),
                             reads=[btmpc[tb]], writes=[bEc[eb]])

                    def back(r=r, h=h, c=c, eb=eb, ob=ob, psC=psC, bC=bC):
                        for qi in range(4):
                            P.op("pe", lambda e, qi=qi: e.matmul(
                                psC[:, qi * 97:(qi + 1) * 97], lhsT=Ec[eb][0:127, qi * 128:(qi + 1) * 128], rhs=vcx[0:127, :],
                                start=(qi == 0), stop=False, skip_group_check=True),
                                 reads=[bEc[eb], bvcx], writes=[bC], inc=(qi == 3))
                        pc3 = psC[:, 0:388].rearrange("p (q d) -> p q d", d=97)
                        P.op("dve", lambda e: e.tensor_scalar(out=rc[:, ob * 4:(ob + 1) * 4], in0=pc3[:, :, 64],
                                                              scalar1=1e-30, scalar2=None, op0=ALU.max),
                             reads=[bC], writes=[brc[ob]])
                        P.op("dve", lambda e: e.reciprocal(out=rc[:, ob * 4:(ob + 1) * 4], in_=rc[:, ob * 4:(ob + 1) * 4]),
                             reads=[brc[ob]], writes=[brc[ob]])
                        P.op("dve", lambda e: e.tensor_tensor(out=rg[:, ob * 4:(ob + 1) * 4], in0=rc[:, ob * 4:(ob + 1) * 4],
                                                              in1=gates[:, 4 * c:4 * c + 4, 3 * h], op=ALU.mult),
                             reads=[brc[ob]] + bgates[4 * c:4 * c + 4], writes=[brg[ob]])
                        for qi in range(4):
                            qb = 4 * c + qi
                            P.op("dve", lambda e, qi=qi, qb=qb: e.tensor_scalar(
                                out=ocmp[:, qb, r, :], in0=psC[:, qi * 97:qi * 97 + 64], scalar1=rg[:, ob * 4 + qi:ob * 4 + qi + 1],
                                scalar2=None, op0=ALU.mult), reads=[bC, brg[ob]], writes=[bocmp[qb]])
                            if r == 0:
                                P.op("dve", lambda e, qi=qi, qb=qb: e.tensor_scalar(
                                    out=imp[:, qb, :], in0=psC[:, qi * 97 + 65:qi * 97 + 97], scalar1=rc[:, ob * 4 + qi:ob * 4 + qi + 1],
                                    scalar2=None, op0=ALU.mult), reads=[bC, brc[ob]], writes=[bimp[qb]])
                            else:
                                P.op("dve", lambda e, qi=qi, qb=qb: e.scalar_tensor_tensor(
                                    out=imp[:, qb, :], in0=psC[:, qi * 97 + 65:qi * 97 + 97], scalar=rc[:, ob * 4 + qi:ob * 4 + qi + 1],
                                    in1=imp[:, qb, :], op0=ALU.mult, op1=ALU.add), reads=[bC, brc[ob], bimp[qb]], writes=[bimp[qb]])

                    stages3.append((front, back))
            run_pipeline(stages3, 2)
            dbg(cx, 5)
            sc = A.alloc([32], F32)
            wk = A.alloc([32], F32)
            m8 = A.alloc([16], F32)
            stg = A.alloc([96], BF16)
            bsc, bwk, bm8, bstg = P.buf("sc"), P.buf("wk"), P.buf("m8"), P.buf("stg")
            P.op("dve", lambda e: e.memset(stg, 0.0), writes=[bstg])
            for qb in range(16):
                P.op("dve", lambda e, qb=qb: e.tensor_tensor(out=sc, in0=imp[:, qb, :], in1=selc[:, 0, qb, :], op=ALU.mult),
                     reads=[bimp[qb], bselc], writes=[bsc])
                P.op("dve", lambda e, qb=qb: e.tensor_tensor(out=sc, in0=sc, in1=selc[:, 1, qb, :], op=ALU.add),
                     reads=[bsc, bselc], writes=[bsc])
                P.op("dve", lambda e: e.max(out=m8[:, 0:8], in_=sc), reads=[bsc], writes=[bm8])
                P.op("dve", lambda e: e.match_replace(out=wk, in_to_replace=m8[:, 0:8], in_values=sc, imm_value=-3.0e38),
                     reads=[bsc, bm8], writes=[bwk])
                P.op("dve", lambda e: e.max(out=m8[:, 8:16], in_=wk), reads=[bwk], writes=[bm8])
                P.op("dve", lambda e: e.tensor_scalar(out=wk, in0=sc, scalar1=m8[:, 15:16], scalar2=None, op0=ALU.is_ge),
                     reads=[bsc, bm8], writes=[bwk])
                P.op("dve", lambda e: e.tensor_scalar(out=stg[:, 64:96], in0=wk, scalar1=-NEGM, scalar2=NEGM, op0=ALU.mult, op1=ALU.add),
                     reads=[bwk], writes=[bstg])
                pst = psT.bitcast(BF16)
                P.op("pe", lambda e, pst=pst: e.transpose(out=pst[0:96, 0:128], in_=stg, identity=cx.ident),
                     reads=[bstg, cx.bident], writes=[bT])
                P.op("act", lambda e, qb=qb, pst=pst: e.copy(out=nst[64:96, qb * 128:(qb + 1) * 128], in_=pst[64:96, 0:128]),
                     reads=[bT], writes=[bnst])
            for r in range(4):
                P.op("act", lambda e, r=r: e.copy(out=qa[r][64:96, :], in_=nst[64:96, :]), reads=[bnst], writes=[bqa[r]])
            dbg(cx, 6)
            E = [A.alloc([512], BF16) for _ in range(6)]
            bE = P.bufs(6, "E")
            tmpd = [A.alloc([256], F32) for _ in range(3)]
            btd = P.bufs(3, "tmpd")
            rsw = A.alloc([16], F32)
            brsw = P.bufs(2, "rsw")
            tmpo = [A.alloc([64], F32) for _ in range(2)]
            btmpo = P.bufs(2, "tmpo")
            st = {"S": 0, "E": 0, "D": 0, "O": 0}
            zE = A.alloc([128], BF16)
            bzE = P.buf("zE")
            P.op("dve", lambda e: e.memset(zE, 0.0), writes=[bzE])

            SB5 = [0, 1, 6, 7]
            stages5 = []

            def add_branch(kind, r, h, c, psO, bO, fin):
                kb_lo = 0 if kind == 0 else max(0, 4 * c - 4)
                kbs = list(range(kb_lo, 4 * c + 4))
                vt, bvt = (vs, bvs) if kind == 0 else (vw, bvw)
                for kb in kbs:
                    qlo = max(kb, 4 * c)
                    qhi = 4 * c + 3 if kind == 0 else min(kb + 4, 4 * c + 3)
                    nq = qhi - qlo + 1
                    ncol = nq * 128
                    sb = SB5[st["S"] % 4]
                    st["S"] += 1
                    eb = st["E"] % 6
                    st["E"] += 1
                    d0 = qlo - kb
                    n01 = max(0, min(2, d0 + nq) - d0) if d0 < 2 else 0
                    n4 = 1 if (kind == 1 and qhi - kb == 4) else 0
                    ncst = nq - n01 - n4
                    db1 = st["D"] % 3
                    if n01:
                        st["D"] += 1
                    db4 = st["D"] % 3
                    if n4:
                        st["D"] += 1

                    def front(kind=kind, r=r, h=h, kb=kb, qlo=qlo, ncol=ncol, sb=sb, eb=eb, d0=d0, n01=n01, n4=n4, ncst=ncst,
                              db1=db1, db4=db4):
                        if kind == 0:
                            P.op("pe", lambda e: e.matmul(ps[sb][:, 0:ncol], lhsT=ka[0:96, kb * 128:(kb + 1) * 128],
                                                          rhs=qa[r][0:96, qlo * 128:qlo * 128 + ncol], start=True, stop=True),
                                 reads=[bka, bqa[r]], writes=[pb[sb]])
                        else:
                            P.op("pe", lambda e: e.matmul(ps[sb][:, 0:ncol], lhsT=kw[0:64, kb * 128:(kb + 1) * 128],
                                                          rhs=qa[r][0:64, qlo * 128:qlo * 128 + ncol], start=True, stop=True),
                                 reads=[bkw, bqa[r]], writes=[pb[sb]])
                        col = 0
                        if n01:
                            w = n01 * 128
                            P.op("dve", lambda e: e.scalar_tensor_tensor(
                                out=tmpd[db1][:, 0:w], in0=ps[sb][:, 0:w], scalar=scale,
                                in1=rev_cols(x01[:, r, :], 255 - d0 * 128, w), op0=ALU.mult, op1=ALU.add),
                                 reads=[pb[sb], bx01], writes=[btd[db1]])
                            P.op("act", lambda e: e.activation(out=E[eb][:, 0:w], in_=tmpd[db1][:, 0:w], func=AF.Exp),
                                 reads=[btd[db1]], writes=[bE[eb]])
                            col = w
                        col4 = (n01 + ncst) * 128
                        if n4:
                            P.op("dve", lambda e: e.tensor_tensor(out=tmpd[db4][:, 0:128], in0=ps[sb][:, col4:col4 + 128], in1=m4,
                                                                  op=ALU.add), reads=[pb[sb], bm4], writes=[btd[db4]])
                            P.op("act", lambda e: e.activation(out=E[eb][:, col4:col4 + 128], in_=tmpd[db4][:, 0:128], func=AF.Exp,
                                                               scale=scale, bias=c31[:, h:h + 1]),
                                 reads=[btd[db4], bc31], writes=[bE[eb]])
                        if ncst:
                            w2_ = ncst * 128
                            P.op("act", lambda e: e.activation(out=E[eb][:, col:col + w2_], in_=ps[sb][:, col:col + w2_], func=AF.Exp,
                                                               scale=scale, bias=c31[:, h:h + 1]),
                                 reads=[pb[sb], bc31], writes=[bE[eb]])

                    def back(kb=kb, qlo=qlo, nq=nq, eb=eb, c=c, psO=psO, bO=bO, vt=vt, bvt=bvt, is_first=(kb == kbs[0]),
                             is_last=(kb == kbs[-1]), fin=fin):
                        if is_first:
                            for oi in range(4):
                                P.op("pe", lambda e, oi=oi: e.matmul(psO[:, oi * 65:(oi + 1) * 65], lhsT=zE, rhs=vt[:, 0, :],
                                                                   start=(oi == 0), stop=False, skip_group_check=True),
                                     reads=[bzE, bvt], writes=[bO], inc=(oi == 3))
                        for qi in range(nq):
                            oi = qlo + qi - 4 * c
                            P.op("pe", lambda e, qi=qi, oi=oi: e.matmul(
                                psO[:, oi * 65:(oi + 1) * 65], lhsT=E[eb][:, qi * 128:(qi + 1) * 128], rhs=vt[:, kb, :],
                                start=False, stop=False, skip_group_check=True),
                                 reads=[bE[eb], bvt], writes=[bO], inc=(qi == nq - 1))
                        if is_last and fin is not None:
                            fin()

                    stages5.append((front, back))

            def make_fin(r, h, c, ob, psOs, bOs, psOw, bOw):
                def fin():
                    o3s = psOs[:, 0:260].rearrange("p (q d) -> p q d", d=65)
                    o3w = psOw[:, 0:260].rearrange("p (q d) -> p q d", d=65)
                    rs_ = rsw[:, ob * 8:ob * 8 + 4]
                    rw_ = rsw[:, ob * 8 + 4:ob * 8 + 8]
                    P.op("dve", lambda e: e.reciprocal(out=rs_, in_=o3s[:, :, 64]), reads=[bOs], writes=[brsw[ob]])
                    P.op("dve", lambda e: e.reciprocal(out=rw_, in_=o3w[:, :, 64]), reads=[bOw], writes=[brsw[ob]])
                    P.op("dve", lambda e: e.tensor_tensor(out=rs_, in0=rs_, in1=gates[:, 4 * c:4 * c + 4, 3 * h + 1], op=ALU.mult),
                         reads=[brsw[ob]] + bgates[4 * c:4 * c + 4], writes=[brsw[ob]])
                    P.op("dve", lambda e: e.tensor_tensor(out=rw_, in0=rw_, in1=gates[:, 4 * c:4 * c + 4, 3 * h + 2], op=ALU.mult),
                         reads=[brsw[ob]] + bgates[4 * c:4 * c + 4], writes=[brsw[ob]])
                    for oi in range(4):
                        qb = 4 * c + oi
                        tb = oi % 2
                        P.op("dve", lambda e, oi=oi, qb=qb, tb=tb: e.scalar_tensor_tensor(
                            out=tmpo[tb], in0=psOs[:, oi * 65:oi * 65 + 64], scalar=rsw[:, ob * 8 + oi:ob * 8 + oi + 1],
                            in1=ocmp[:, qb, r, :], op0=ALU.mult, op1=ALU.add),
                             reads=[bOs, brsw[ob], bocmp[qb]], writes=[btmpo[tb]])
                        P.op("dve", lambda e, oi=oi, qb=qb, tb=tb: e.scalar_tensor_tensor(
                            out=o_all[:, qb, h * 64:(h + 1) * 64], in0=psOw[:, oi * 65:oi * 65 + 64],
                            scalar=rsw[:, ob * 8 + 4 + oi:ob * 8 + 4 + oi + 1], in1=tmpo[tb], op0=ALU.mult, op1=ALU.add),
                             reads=[bOw, brsw[ob], btmpo[tb]], writes=[bo_all[qb]])
                return fin

            for r in range(4):
                h = 4 * g + r
                for c in range(4):
                    ob = st["O"] % 2
                    st["O"] += 1
                    psOs, bOs = ps[2 + ob], pb[2 + ob]
                    psOw, bOw = ps[4 + ob], pb[4 + ob]
                    add_branch(0, r, h, c, psOs, bOs, None)
                    add_branch(1, r, h, c, psOw, bOw, make_fin(r, h, c, ob, psOs, bOs, psOw, bOw))
            run_pipeline(stages5, 3)
            P.barrier()
            A.release()
            dbg(cx, 7)
        A.mark()
        wo = A.alloc([8, 1024], BF16)
        bwo = P.buf("wo")
        P.dma("pool", [(wo, cx.inp["nsa_w_o"][j].rearrange("(kc p) c -> p kc c", p=128))], writes=[bwo])
        oT = [A.alloc([8, 128], BF16) for _ in range(2)]
        boT = P.bufs(2, "oT")
        hr = [A.alloc([D], F32) for _ in range(2)]
        bhr = P.bufs(2, "hr")
        for qb in range(16):
            out_proj_tile(cx, o_all[:, qb, :], bo_all[qb], wo, bwo, src, dst, tok0 + qb * 128, oT, boT, hr, bhr, qb % 2,
                          psT, bT, ps[0:2], pb[0:2])
        P.barrier()
        A.release()
    A.release()


def final_norm_phase(cx, src, dst):
    P, A = cx.P, cx.A
    A.mark()
    gain, bgain = load_gain(cx, cx.inp["final_norm"], D)
    hn = [A.alloc([D], F32) for _ in range(2)]
    bhn = P.bufs(2, "fhn")
    yo = [A.alloc([D], F32) for _ in range(2)]
    byo = P.bufs(2, "fyo")
    junk = A.alloc([D], BF16)
    bjunk = P.buf("junk")
    ss = A.alloc([8], F32)
    bss = P.bufs(8, "ss")
    for i in range(T // 128):
        sl = i % 2
        c = i % 8
        P.dma("sp", [(hn[sl], src[i * 128:(i + 1) * 128, :])], writes=[bhn[sl]], key="fhn%d" % sl)
        rms_rstd(cx, hn[sl], bhn[sl], D, ss, bss[c], c, junk, bjunk)
        P.op("dve", lambda e, sl=sl, c=c: e.scalar_tensor_tensor(out=yo[sl], in0=hn[sl], scalar=ss[:, c:c + 1], in1=gain,
                                                                 op0=ALU.mult, op1=ALU.mult),
             reads=[bhn[sl], bss[c], bgain], writes=[byo[sl]])
        P.dma("sp", [(dst[i * 128:(i + 1) * 128, :], yo[sl])], reads=[byo[sl]], key="fyo%d" % sl)
    P.barrier()
    A.release()


INPUT_SHAPES = {
    "ffn_norm_a": (DEPTH, D), "ffn_a_w_gate": (DEPTH, D, FF), "ffn_a_w_up": (DEPTH, D, FF), "ffn_a_w_down": (DEPTH, FF, D),
    "mix_norm": (DEPTH, D), "ffn_norm_b": (DEPTH, D), "ffn_b_w_gate": (DEPTH, D, FF), "ffn_b_w_up": (DEPTH, D, FF),
    "ffn_b_w_down": (DEPTH, FF, D), "final_norm": (D,), "rel_bias": (32, 16),
    "mla_w_in": (2, D, 672), "mla_q_norm": (2, 384), "mla_kv_norm": (2, 256), "mla_w_uq": (2, 384, 1536),
    "mla_w_ukv": (2, 256, 2048), "mla_w_o": (2, 1024, 1024),
    "nsa_w_in": (2, D, 2608), "nsa_cmp_pos_k": (2, 32, 64), "nsa_cmp_w1_k": (2, 2048, 128), "nsa_cmp_w2_k": (2, 128, 64),
    "nsa_cmp_pos_v": (2, 32, 64), "nsa_cmp_w1_v": (2, 2048, 128), "nsa_cmp_w2_v": (2, 128, 64), "nsa_w_o": (2, 1024, 1024),
}
ARENA_WORDS = 50944


def host_consts():
    c = {}
    c["c_ident"] = np.eye(128, dtype=np.float32).astype(ml_dtypes.bfloat16)
    half = 16
    inv = (np.float32(10000.0) ** (-np.arange(half, dtype=np.float32) * np.float32(2.0) / np.float32(32))).astype(np.float32)
    ang = np.arange(S, dtype=np.float32)[:, None] * inv[None, :]
    cos, sin = np.cos(ang).astype(np.float32), np.sin(ang).astype(np.float32)
    c["c_rope_cos"] = np.ascontiguousarray(np.concatenate([cos.T, cos.T], axis=0))
    c["c_rope_sin"] = np.ascontiguousarray(np.concatenate([sin.T, sin.T], axis=0))
    k = np.arange(128)[:, None]
    t = np.arange(128)[None, :]
    c["c_mdiag"] = np.where(t >= k, 0.0, NEGM).astype(np.float32)
    c["c_m4"] = np.where(t < k, 0.0, NEGM).astype(np.float32)
    def bucket(dist):
        n = np.maximum(dist, 0)
        nf = np.maximum(n, 1).astype(np.float32)
        large = 16 + (np.log(nf / np.float32(16)) / np.float32(math.log(8.0)) * np.float32(16)).astype(np.int32)
        return np.where(n < 16, n, np.minimum(large, 31))
    i = np.arange(4096)
    dist = 2047 - i
    oh = np.zeros((33, 4096), np.float32)
    bk = bucket(dist)
    oh[bk[dist >= 0], i[dist >= 0]] = 1.0
    oh[32, i[dist < 0]] = 1.0
    c["c_oh"] = oh
    kk = np.arange(S)
    c["c_bexp"] = (kk[None, :] // 64 == np.arange(32)[:, None]).astype(np.float32).astype(ml_dtypes.bfloat16)
    cs_ = np.arange(127) * 16
    ce_ = cs_ + 32
    ss_ = np.arange(32) * 64
    se_ = ss_ + 64
    ov = np.minimum(ce_[:, None], se_[None, :]) - np.maximum(cs_[:, None], ss_[None, :])
    c["c_ov"] = (np.clip(ov, 0, None) / 32.0).astype(np.float32).astype(ml_dtypes.bfloat16)
    tl = np.arange(128)[:, None, None]
    qb_ = np.arange(16)[None, :, None]
    jj = np.arange(32)[None, None, :]
    blk = (qb_ * 128 + tl) // 64
    forced = (jj == 0) | (jj == blk) | (jj == blk - 1)
    causal = jj <= blk
    c["c_cm"] = (causal & ~forced).astype(np.float32)
    c["c_add"] = np.where(forced, 1e6, np.where(causal, 0.0, -1e6)).astype(np.float32)
    return c


def default_phases():
    ph = []
    for i in range(DEPTH):
        ph.append(("ffn", i, "a"))
        ph.append(("mla", i // 2) if i % 2 == 0 else ("nsa", i // 2))
        ph.append(("ffn", i, "b"))
    ph.append(("final",))
    return ph


def build_program(phases, dbg_stop=None):
    nc = bass.Bass("TRN2", target_bir_lowering=False)
    cx = Ctx()
    cx.dbg_stop = dbg_stop
    cx.nc = nc
    cx.inp = {}
    cx.cd = {}
    x_in = nc.dram_tensor("x", [T, D], F32, kind="ExternalInput").ap()
    for name, shp in INPUT_SHAPES.items():
        cx.inp[name] = nc.dram_tensor(name, list(shp), F32, kind="ExternalInput").ap()
    consts = host_consts()
    cd = cx.cd
    for name, arr in consts.items():
        dt = BF16 if arr.dtype == ml_dtypes.bfloat16 else F32
        cd[name] = nc.dram_tensor(name, list(arr.shape), dt, kind="ExternalInput").ap()
    y = nc.dram_tensor("y", [T, D], F32, kind="ExternalOutput").ap()
    hbuf = nc.dram_tensor("hbuf", [T, D], F32).ap()
    with ExitStack() as es:
        big = es.enter_context(nc.sbuf_tensor("big", [128, ARENA_WORDS], F32))
        cx.ps = [es.enter_context(nc.psum_tensor("ps%d" % i, [128, 512], F32))[:, :] for i in range(8)]
        P = Prog(nc, es)
        cx.P = P
        cx.pb = P.bufs(8, "psb")
        cx.A = Arena(big, ARENA_WORDS)
        cx.ident = cx.A.alloc([128], BF16)
        cx.bident = P.buf("ident")
        P.dma("sp", [(cx.ident, cd["c_ident"])], writes=[cx.bident])
        cur = x_in
        try:
          for ph in phases:
            if ph[0] == "ffn":
                ffn_phase(cx, ph[1], ph[2], cur, hbuf)
                cur = hbuf
            elif ph[0] == "mla":
                mla_phase(cx, ph[1], 2 * ph[1], cur, hbuf)
                cur = hbuf
            elif ph[0] == "nsa":
                if not getattr(cx, "nsa_ready", False):
                    nsa_setup(cx)
                    cx.nsa_ready = True
                nsa_phase(cx, ph[1], 2 * ph[1] + 1, cur, hbuf)
                cur = hbuf
            elif ph[0] == "final":
                final_norm_phase(cx, cur, y)
                cur = y
            else:
                raise NotImplementedError(ph)
        except StopBuild:
            cur = x_in
        if cur is not y:
            P.dma("sp", [(y, cur)], key="ycopy")
            P.barrier()
        P.run_block()
    cx.consts = consts
    return nc, cx


_CACHE = {}


def kernel(**inputs):
    phases = default_phases()
    key = "full"
    if key not in _CACHE:
        _CACHE[key] = build_program(phases)
    nc, cx = _CACHE[key]
    x = np.ascontiguousarray(np.asarray(inputs["x"], dtype=np.float32)).reshape(N_CORES, T, D)
    shared = {k: np.ascontiguousarray(np.asarray(inputs[k], dtype=np.float32)) for k in INPUT_SHAPES}
    shared.update(cx.consts)
    in_maps = []
    for c in range(N_CORES):
        m = dict(shared)
        m["x"] = x[c]
        in_maps.append(m)
    res = run_bass_kernel_spmd(nc, in_maps, core_ids=list(range(N_CORES)))
    out = np.stack([np.asarray(r["y"], dtype=np.float32) for r in res.results], axis=0)
    return out.reshape(16, S, D)
```

```python
import math
from contextlib import ExitStack

import numpy as np
import ml_dtypes

import concourse.bass as bass
import concourse.mybir as mybir
from concourse.bass_utils import run_bass_kernel_spmd

F32 = mybir.dt.float32
BF16 = mybir.dt.bfloat16
AF = mybir.ActivationFunctionType
ALU = mybir.AluOpType

N_CORES = 8
D = 1024
S = 2048
NSEQ = 2
T = NSEQ * S
FF = 2816
NHC = FF // 128
DEPTH = 4
EPS = 1e-6
NEGM = -16384.0

ENGS = ["pe", "act", "dve", "pool", "sp"]
SELF_SYNC = {"pe": False, "act": True, "dve": True, "pool": True, "sp": False}


class Buf:
    __slots__ = ("name", "base", "w", "r")

    def __init__(self, name, base):
        self.name = name
        self.base = base
        self.w = []
        self.r = []


class Prog:
    def __init__(self, nc, es):
        self.nc = nc
        self.es = es
        self.q = {e: [] for e in ENGS}
        self.cnt = {e: 0 for e in ENGS}
        self.sems = {e: es.enter_context(nc.semaphore("s_" + e)) for e in ENGS if e != "sp"}
        self.dcnt = {}
        self.nbuf = 0
        self.ninst = 0

    def buf(self, name=None):
        self.nbuf += 1
        return Buf("%s_%d" % (name or "b", self.nbuf), name or "b")

    def bufs(self, n, name="b"):
        return [self.buf("%s%d" % (name, i)) for i in range(n)]

    def _deps(self, reads, writes):
        waits = []
        for b in reads:
            waits += b.w
        for b in writes:
            waits += b.w
            waits += b.r
        return waits

    def _commit(self, ev, reads, writes):
        for b in reads:
            b.r.append(ev)
        for b in writes:
            b.w = [ev]
            b.r = []

    def op(self, eng, fn, reads=(), writes=(), inc=True):
        waits = self._deps(reads, writes)
        idx = self.cnt[eng] + 1
        if inc:
            self.cnt[eng] = idx
        self._commit((eng, idx), reads, writes)
        self.q[eng].append((fn, waits, (eng, 1) if inc else None))
        self.ninst += 1

    def dma(self, eng, pairs, reads=(), writes=(), key=None, **kw):
        key = "d_" + (key or writes[0].base)
        if key not in self.sems:
            self.sems[key] = self.es.enter_context(self.nc.semaphore(key))
            self.dcnt[key] = 0
        waits = self._deps(reads, writes)
        self.dcnt[key] += 16 * len(pairs)
        self._commit((key, self.dcnt[key]), reads, writes)
        for i, (o, a) in enumerate(pairs):
            self.q[eng].append(((lambda e, o=o, a=a: e.dma_start(out=o, in_=a, **kw)),
                                waits if i == 0 else [], (key, 16)))
            self.ninst += 1

    def barrier(self):
        evs = [(e, self.cnt[e]) for e in ENGS if e != "sp" and self.cnt[e] > 0]
        evs += [(k, v) for k, v in self.dcnt.items() if v > 0]
        for e in ENGS:
            self.q[e].append((None, list(evs), None))

    def replay(self, eng, e):
        waited = {}
        for fn, waits, inc in self.q[eng]:
            need = {}
            for k, v in waits:
                if k == eng and not SELF_SYNC[eng]:
                    continue
                if v > need.get(k, 0):
                    need[k] = v
            for k, v in need.items():
                if waited.get(k, 0) < v:
                    e.wait_ge(self.sems[k], v)
                    waited[k] = v
            if fn is not None:
                ins = fn(e)
                if inc is not None:
                    ins.then_inc(self.sems[inc[0]], inc[1])

    def run_block(self):
        for e in ENGS:
            assert self.cnt[e] < 65000, (e, self.cnt[e])
        with self.nc.Block() as block:
            @block.sync
            def _(e):
                self.replay("sp", e)

            @block.tensor
            def _(e):
                self.replay("pe", e)

            @block.scalar
            def _(e):
                self.replay("act", e)

            @block.vector
            def _(e):
                self.replay("dve", e)

            @block.gpsimd
            def _(e):
                self.replay("pool", e)


class Arena:
    def __init__(self, big, words):
        self.big = big
        self.words = words
        self.top = 0
        self.marks = []
        self.peak = 0

    def alloc(self, shape_free, dtype, parts=128):
        n = int(np.prod(shape_free))
        nb = 2 if dtype == BF16 else 4
        w = (n * nb + 3) // 4
        w = (w + 7) // 8 * 8
        assert self.top + w <= self.words, ("SBUF arena overflow", self.top, w, self.words)
        ap = self.big[0:parts, self.top:self.top + w]
        self.top += w
        self.peak = max(self.peak, self.top)
        if dtype != F32:
            ap = ap.bitcast(dtype)
        ap = ap[:, 0:n]
        if len(shape_free) > 1:
            names = " ".join("d%d" % i for i in range(len(shape_free)))
            kw = {"d%d" % i: int(s) for i, s in enumerate(shape_free)}
            ap = ap.rearrange("p (%s) -> p %s" % (names, names), **kw)
        return ap

    def mark(self):
        self.marks.append(self.top)

    def release(self):
        self.top = self.marks.pop()


class Ctx:
    dbg_stop = None


class StopBuild(Exception):
    pass


def dbg(cx, level):
    if cx.dbg_stop is not None and cx.dbg_stop == level:
        cx.P.barrier()
        raise StopBuild()


def rms_rstd(cx, x_ap, bx, n, ss, bss, col, junk, bjunk):
    P = cx.P
    c = ss[:, col:col + 1]
    P.op("act", lambda e: e.activation(out=junk[:, 0:n], in_=x_ap, func=AF.Square, accum_out=c),
         reads=[bx], writes=[bjunk, bss])
    P.op("dve", lambda e: e.tensor_scalar(out=c, in0=c, scalar1=1.0 / n, scalar2=EPS, op0=ALU.mult, op1=ALU.add),
         reads=[bss], writes=[bss])
    P.op("act", lambda e: e.activation(out=c, in_=c, func=AF.Ln), reads=[bss], writes=[bss])
    P.op("act", lambda e: e.activation(out=c, in_=c, func=AF.Exp, scale=-0.5), reads=[bss], writes=[bss])


def load_gain(cx, vec_ap, n, name="gain"):
    P, A = cx.P, cx.A
    g = A.alloc([n], F32)
    bg = P.buf(name)
    P.dma("sp", [(g, vec_ap.partition_broadcast(128))], writes=[bg])
    return g, bg


def ffn_phase(cx, li, which, src, dst):
    P, A, ps, pb = cx.P, cx.A, cx.ps, cx.pb
    A.mark()
    Wg_d = cx.inp["ffn_%s_w_gate" % which][li]
    Wu_d = cx.inp["ffn_%s_w_up" % which][li]
    Wd_d = cx.inp["ffn_%s_w_down" % which][li]
    gn_d = cx.inp["ffn_norm_%s" % which][li]
    wg = A.alloc([8, FF], BF16)
    wu = A.alloc([8, FF], BF16)
    wd = A.alloc([NHC, D], BF16)
    CG = [(0, 768), (768, 1536), (1536, 2304), (2304, 2816)]
    bwg, bwu, bwd = P.bufs(4, "wg"), P.bufs(4, "wu"), P.bufs(4, "wd")
    for gi, (c0, c1) in enumerate(CG):
        P.dma("pool", [(wg[:, :, c0:c1], Wg_d[:, c0:c1].rearrange("(kc p) c -> p kc c", p=128))], writes=[bwg[gi]],
              key="wg%d" % gi)
        P.dma("pool", [(wu[:, :, c0:c1], Wu_d[:, c0:c1].rearrange("(kc p) c -> p kc c", p=128))], writes=[bwu[gi]],
              key="wu%d" % gi)
    for gi, (c0, c1) in enumerate(CG):
        P.dma("pool", [(wd[:, c0 // 128:c1 // 128, :], Wd_d[c0:c1, :].rearrange("(hc p) c -> p hc c", p=128))],
              writes=[bwd[gi]], key="wd%d" % gi)
    gain, bgain = load_gain(cx, gn_d, D)
    hn = [A.alloc([D], F32) for _ in range(2)]
    bhn = P.bufs(2, "hn")
    junk = A.alloc([D], BF16)
    bjunk = P.buf("junk")
    ss = A.alloc([8], F32)
    bss = P.bufs(8, "ss")
    xn = A.alloc([4, D], BF16)
    bxn = P.bufs(4, "xn")
    xT = A.alloc([8, 512], BF16)
    bxT = P.buf("xT")
    sg = [A.alloc([512], BF16) for _ in range(2)]
    bsg = P.bufs(2, "sg")
    hT = A.alloc([NHC, 512], BF16)
    bhT = P.bufs(NHC, "hT")
    hr = [A.alloc([D], F32) for _ in range(2)]
    bhr = P.bufs(2, "hr")
    psA, psB, psD, psT = ps[0:2], ps[2:4], ps[4:6], ps[6]
    bA, bB, bD, bT = pb[0:2], pb[2:4], pb[4:6], pb[6]
    ident = cx.ident
    cnt = {"n": 0, "g": 0, "d": 0}

    def norm_a(tt):
        for j in range(4):
            n = cnt["n"]
            cnt["n"] += 1
            sl = n % 2
            r0 = tt * 512 + j * 128
            P.dma("sp", [(hn[sl], src[r0:r0 + 128, :])], writes=[bhn[sl]], key="hn%d" % sl)
            c = n % 8
            rms_rstd(cx, hn[sl], bhn[sl], D, ss, bss[c], c, junk, bjunk)
            P.op("dve", lambda e, sl=sl, c=c, j=j: e.scalar_tensor_tensor(
                out=xn[:, j, :], in0=hn[sl], scalar=ss[:, c:c + 1], in1=gain, op0=ALU.mult, op1=ALU.mult),
                 reads=[bhn[sl], bss[c], bgain], writes=[bxn[j]])

    def norm_b(tt):
        pst = psT.bitcast(BF16)
        for j in range(4):
            for kc in range(8):
                P.op("pe", lambda e, j=j, kc=kc: e.transpose(out=pst[:, kc * 128:(kc + 1) * 128],
                                                               in_=xn[:, j, kc * 128:(kc + 1) * 128], identity=ident),
                     reads=[bxn[j], cx.bident], writes=[bT], inc=(kc == 7))
            P.op("act", lambda e, j=j: e.copy(out=xT[:, :, j * 128:(j + 1) * 128],
                                              in_=pst.rearrange("p (k t) -> p k t", k=8)),
                 reads=[bT], writes=[bxT])

    def gateup(tt):
        for hc in range(NHC):
            g = cnt["g"]
            cnt["g"] += 1
            s2 = g % 2
            gi = min(hc // 6, 3)
            for kc in range(8):
                P.op("pe", lambda e, hc=hc, kc=kc, s2=s2: e.matmul(psA[s2], lhsT=wg[:, kc, hc * 128:(hc + 1) * 128],
                                                                    rhs=xT[:, kc, :], start=(kc == 0), stop=(kc == 7)),
                     reads=[bxT, bwg[gi]], writes=[bA[s2]], inc=(kc == 7))
            for kc in range(8):
                P.op("pe", lambda e, hc=hc, kc=kc, s2=s2: e.matmul(psB[s2], lhsT=wu[:, kc, hc * 128:(hc + 1) * 128],
                                                                    rhs=xT[:, kc, :], start=(kc == 0), stop=(kc == 7)),
                     reads=[bxT, bwu[gi]], writes=[bB[s2]], inc=(kc == 7))
            P.op("act", lambda e, s2=s2: e.activation(out=sg[s2], in_=psA[s2], func=AF.Silu),
                 reads=[bA[s2]], writes=[bsg[s2]])
            P.op("dve", lambda e, s2=s2, hc=hc: e.tensor_tensor(out=hT[:, hc, :], in0=sg[s2], in1=psB[s2], op=ALU.mult),
                 reads=[bsg[s2], bB[s2]], writes=[bhT[hc]])

    def down(tt):
        for j in range(4):
            r0 = tt * 512 + j * 128
            sl = (tt * 4 + j) % 2
            P.dma("sp", [(hr[sl], src[r0:r0 + 128, :])], writes=[bhr[sl]], key="hr%d" % sl)
            for half in range(2):
                d = cnt["d"]
                cnt["d"] += 1
                s2 = d % 2
                for hc in range(NHC):
                    gi = min(hc // 6, 3)
                    P.op("pe", lambda e, hc=hc, j=j, half=half, s2=s2: e.matmul(
                        psD[s2], lhsT=hT[:, hc, j * 128:(j + 1) * 128], rhs=wd[:, hc, half * 512:(half + 1) * 512],
                        start=(hc == 0), stop=(hc == NHC - 1)),
                         reads=[bhT[hc], bwd[gi]], writes=[bD[s2]], inc=(hc == NHC - 1))
                P.op("dve", lambda e, sl=sl, half=half, s2=s2: e.scalar_tensor_tensor(
                    out=hr[sl][:, half * 512:(half + 1) * 512], in0=psD[s2], scalar=0.5,
                    in1=hr[sl][:, half * 512:(half + 1) * 512], op0=ALU.mult, op1=ALU.add),
                     reads=[bD[s2], bhr[sl]], writes=[bhr[sl]])
            P.dma("sp", [(dst[r0:r0 + 128, :], hr[sl])], reads=[bhr[sl]], key="hrst%d" % sl)

    NT = T // 512
    norm_a(0)
    norm_b(0)
    for tt in range(NT):
        gateup(tt)
        if tt + 1 < NT:
            norm_a(tt + 1)
        down(tt)
        if tt + 1 < NT:
            norm_b(tt + 1)
    P.barrier()
    A.release()


def run_pipeline(stages, L):
    n = len(stages)
    for i in range(n + L):
        if i < n:
            stages[i][0]()
        if i >= L:
            stages[i - L][1]()


def norm_T_tile(cx, src, r0, gain, bgain, hn, bhn, sl, ss, bss, c, junk, bjunk, xn, bxn, psT, bT, dstT, bdstT, col0):
    P = cx.P
    P.dma("sp", [(hn[sl], src[r0:r0 + 128, :])], writes=[bhn[sl]], key="mhn%d" % sl)
    rms_rstd(cx, hn[sl], bhn[sl], D, ss, bss[c], c, junk, bjunk)
    P.op("dve", lambda e: e.scalar_tensor_tensor(out=xn[sl], in0=hn[sl], scalar=ss[:, c:c + 1], in1=gain,
                                                 op0=ALU.mult, op1=ALU.mult),
         reads=[bhn[sl], bss[c], bgain], writes=[bxn[sl]])
    pst = psT.bitcast(BF16)
    for kc in range(8):
        P.op("pe", lambda e, kc=kc: e.transpose(out=pst[:, kc * 128:(kc + 1) * 128], in_=xn[sl][:, kc * 128:(kc + 1) * 128],
                                                identity=cx.ident),
             reads=[bxn[sl], cx.bident], writes=[bT], inc=(kc == 7))
    P.op("act", lambda e: e.copy(out=dstT[:, 0:8, col0:col0 + 128], in_=pst.rearrange("p (k t) -> p k t", k=8)),
         reads=[bT], writes=[bdstT])


def out_proj_tile(cx, o_tile, bo, wo, bwo, src, dst, r0, oT, boT, hr, bhr, sl, psT, bT, psW, bW):
    P = cx.P
    pst = psT.bitcast(BF16)
    for kc in range(8):
        P.op("pe", lambda e, kc=kc: e.transpose(out=pst[:, kc * 128:(kc + 1) * 128], in_=o_tile[:, kc * 128:(kc + 1) * 128],
                                                identity=cx.ident),
             reads=[bo, cx.bident], writes=[bT], inc=(kc == 7))
    P.op("act", lambda e: e.copy(out=oT[sl], in_=pst.rearrange("p (k t) -> p k t", k=8)), reads=[bT], writes=[boT[sl]])
    P.dma("sp", [(hr[sl], src[r0:r0 + 128, :])], writes=[bhr[sl]], key="mhr%d" % sl)
    for half in range(2):
        for kc in range(8):
            P.op("pe", lambda e, kc=kc, half=half: e.matmul(psW[half], lhsT=oT[sl][:, kc, :],
                                                            rhs=wo[:, kc, half * 512:(half + 1) * 512],
                                                            start=(kc == 0), stop=(kc == 7)),
                 reads=[boT[sl], bwo], writes=[bW[half]], inc=(kc == 7))
        P.op("dve", lambda e, half=half: e.tensor_tensor(out=hr[sl][:, half * 512:(half + 1) * 512], in0=psW[half],
                                                         in1=hr[sl][:, half * 512:(half + 1) * 512], op=ALU.add),
             reads=[bW[half], bhr[sl]], writes=[bhr[sl]])
    P.dma("sp", [(dst[r0:r0 + 128, :], hr[sl])], reads=[bhr[sl]], key="mhrst%d" % sl)


def mla_phase(cx, j, li, src, dst):
    P, A, ps, pb = cx.P, cx.A, cx.ps, cx.pb
    A.mark()
    scale = 96.0 ** -0.5
    Win_d, Wuq_d, Wukv_d, Wo_d = cx.inp["mla_w_in"][j], cx.inp["mla_w_uq"][j], cx.inp["mla_w_ukv"][j], cx.inp["mla_w_o"][j]
    win = A.alloc([8, 672], BF16)
    wkr = A.alloc([8, 96], BF16)
    wq = A.alloc([3, 1536], BF16)
    wqs = A.alloc([3, 16, 96], BF16)
    wkn = A.alloc([2, 16, 64], BF16)
    wv = A.alloc([2, 16, 64], BF16)
    wo = A.alloc([8, 1024], BF16)
    bwin, bwkr, bwq, bwqs, bwkn, bwv, bwo = (P.buf(n) for n in ["win", "wkr", "wq", "wqs", "wkn", "wv", "wo"])
    P.dma("pool", [(win, Win_d.rearrange("(kc p) c -> p kc c", p=128))], writes=[bwin])
    w_in_r = Win_d.rearrange("(kc p) c -> p kc c", p=128)
    P.op("dve", lambda e: e.memset(wkr, 0.0), writes=[bwkr])
    P.dma("pool", [(wkr[:, :, 64:80], w_in_r[:, :, 656:672]), (wkr[:, :, 80:96], w_in_r[:, :, 640:656])], writes=[bwkr])
    P.op("act", lambda e: e.mul(out=wkr[:, :, 64:80], in_=wkr[:, :, 64:80], mul=-1.0), reads=[bwkr], writes=[bwkr])
    wuq_r = Wuq_d.rearrange("(kc p) (h d) -> p kc h d", p=128, d=96)
    P.dma("pool", [(wq, Wuq_d.rearrange("(kc p) c -> p kc c", p=128))], writes=[bwq])
    P.op("dve", lambda e: e.memset(wqs, 0.0), writes=[bwqs])
    for kc in range(3):
        P.dma("pool", [(wqs[:, kc, :, 64:80], wuq_r[:, kc, :, 80:96]), (wqs[:, kc, :, 80:96], wuq_r[:, kc, :, 64:80])],
              writes=[bwqs], key="wqs")
    P.op("act", lambda e: e.mul(out=wqs[:, :, :, 64:80], in_=wqs[:, :, :, 64:80], mul=-1.0), reads=[bwqs], writes=[bwqs])
    wukv_r = Wukv_d.rearrange("(kc p) (h d) -> p kc h d", p=128, d=128)
    for kc in range(2):
        P.dma("pool", [(wkn[:, kc, :, :], wukv_r[:, kc, :, 0:64])], writes=[bwkn], key="wkn")
        P.dma("pool", [(wv[:, kc, :, :], wukv_r[:, kc, :, 64:128])], writes=[bwv], key="wv")
    P.dma("pool", [(wo, Wo_d.rearrange("(kc p) c -> p kc c", p=128))], writes=[bwo])
    gain, bgain = load_gain(cx, cx.inp["mix_norm"][li], D)
    gq, bgq = load_gain(cx, cx.inp["mla_q_norm"][j], 384, "gainq")
    gkv, bgkv = load_gain(cx, cx.inp["mla_kv_norm"][j], 256, "gainkv")
    CC = A.alloc([S], F32)
    SS = A.alloc([S], F32)
    bcs = P.buf("cs")
    P.dma("sp", [(CC[64:96, :], cx.cd["c_rope_cos"]), (SS[64:96, :], cx.cd["c_rope_sin"])], writes=[bcs])
    mdiag = A.alloc([128], F32)
    bmd = P.buf("mdiag")
    P.dma("sp", [(mdiag, cx.cd["c_mdiag"])], writes=[bmd])
    junk = A.alloc([D], BF16)
    bjunk = P.buf("junk")
    ss = A.alloc([8], F32)
    bss = P.bufs(8, "ss")
    ssq = A.alloc([8], F32)
    bssq = P.bufs(8, "ssq")
    cqnT = A.alloc([3, S], BF16)
    ckvnT = A.alloc([2, S], BF16)
    bcqnT, bckvnT = P.buf("cqnT"), P.buf("ckvnT")
    kT = [A.alloc([S], BF16) for _ in range(2)]
    bkT = P.bufs(2, "kT")
    o_all = A.alloc([16, D], BF16)
    bo_all = P.bufs(16, "o_all")
    psT, bT = ps[6], pb[6]

    for sq in range(NSEQ):
        tok0 = sq * S
        A.mark()
        hn = [A.alloc([D], F32) for _ in range(2)]
        bhn = P.bufs(2, "hn")
        xn = [A.alloc([D], BF16) for _ in range(2)]
        bxn = P.bufs(2, "xn")
        mT = A.alloc([8, 512], BF16)
        bmT = P.buf("mT")
        cqn = [A.alloc([384], BF16) for _ in range(2)]
        ckvn = [A.alloc([256], BF16) for _ in range(2)]
        bcqn, bckvn = P.bufs(2, "cqn"), P.bufs(2, "ckvn")
        tmpa = A.alloc([512], F32)
        tmpb = A.alloc([512], F32)
        btmpa, btmpb = P.buf("tmpa"), P.buf("tmpb")
        n = 0
        for c in range(4):
            for jj in range(4):
                norm_T_tile(cx, src, tok0 + c * 512 + jj * 128, gain, bgain, hn, bhn, n % 2, ss, bss, n % 8, junk, bjunk,
                            xn, bxn, psT, bT, mT, bmT, jj * 128)
                n += 1
            for kc in range(8):
                P.op("pe", lambda e, kc=kc: e.matmul(ps[4][0:96, :], lhsT=win[:, kc, 576:672], rhs=mT[:, kc, :],
                                                     start=(kc == 0), stop=(kc == 7)),
                     reads=[bwin, bmT], writes=[pb[4]], inc=(kc == 7))
            for kc in range(8):
                P.op("pe", lambda e, kc=kc: e.matmul(ps[5][0:96, :], lhsT=wkr[:, kc, :], rhs=mT[:, kc, :],
                                                     start=(kc == 0), stop=(kc == 7)),
                     reads=[bwkr, bmT], writes=[pb[5]], inc=(kc == 7))
            cs = slice(c * 512, (c + 1) * 512)
            P.op("dve", lambda e, cs=cs: e.tensor_tensor(out=tmpa[64:96, :], in0=ps[4][64:96, :], in1=CC[64:96, cs], op=ALU.mult),
                 reads=[pb[4], bcs], writes=[btmpa])
            P.op("dve", lambda e, cs=cs: e.tensor_tensor(out=tmpb[64:96, :], in0=ps[5][64:96, :], in1=SS[64:96, cs], op=ALU.mult),
                 reads=[pb[5], bcs], writes=[btmpb])
            for b2 in range(2):
                P.op("dve", lambda e, cs=cs, b2=b2: e.tensor_tensor(out=kT[b2][64:96, cs], in0=tmpa[64:96, :], in1=tmpb[64:96, :],
                                                                    op=ALU.add),
                     reads=[btmpa, btmpb], writes=[bkT[b2]])
            for jj in range(4):
                tsl = slice(jj * 128, (jj + 1) * 128)
                s2 = jj % 2
                for kc in range(8):
                    P.op("pe", lambda e, kc=kc, tsl=tsl, s2=s2: e.matmul(ps[s2][:, 0:384], lhsT=mT[:, kc, tsl],
                                                                        rhs=win[:, kc, 0:384], start=(kc == 0), stop=(kc == 7)),
                         reads=[bwin, bmT], writes=[pb[s2]], inc=(kc == 7))
                for kc in range(8):
                    P.op("pe", lambda e, kc=kc, tsl=tsl, s2=s2: e.matmul(ps[2 + s2][:, 0:256], lhsT=mT[:, kc, tsl],
                                                                        rhs=win[:, kc, 384:640], start=(kc == 0), stop=(kc == 7)),
                         reads=[bwin, bmT], writes=[pb[2 + s2]], inc=(kc == 7))
                cq = (c * 4 + jj) % 8
                rms_rstd(cx, ps[s2][:, 0:384], pb[s2], 384, ssq, bssq[cq], cq, junk, bjunk)
                P.op("dve", lambda e, s2=s2, cq=cq: e.scalar_tensor_tensor(out=cqn[s2], in0=ps[s2][:, 0:384],
                                                                           scalar=ssq[:, cq:cq + 1], in1=gq,
                                                                           op0=ALU.mult, op1=ALU.mult),
                     reads=[pb[s2], bssq[cq], bgq], writes=[bcqn[s2]])
                rms_rstd(cx, ps[2 + s2][:, 0:256], pb[2 + s2], 256, ss, bss[cq], cq, junk, bjunk)
                P.op("dve", lambda e, s2=s2, cq=cq: e.scalar_tensor_tensor(out=ckvn[s2], in0=ps[2 + s2][:, 0:256],
                                                                           scalar=ss[:, cq:cq + 1], in1=gkv,
                                                                           op0=ALU.mult, op1=ALU.mult),
                     reads=[pb[2 + s2], bss[cq], bgkv], writes=[bckvn[s2]])
                pst = psT.bitcast(BF16)
                for kc in range(3):
                    P.op("pe", lambda e, kc=kc, s2=s2: e.transpose(out=pst[:, kc * 128:(kc + 1) * 128],
                                                                   in_=cqn[s2][:, kc * 128:(kc + 1) * 128], identity=cx.ident),
                         reads=[bcqn[s2], cx.bident], writes=[bT], inc=False)
                for kc in range(2):
                    P.op("pe", lambda e, kc=kc, s2=s2: e.transpose(out=pst[:, (3 + kc) * 128:(4 + kc) * 128],
                                                                   in_=ckvn[s2][:, kc * 128:(kc + 1) * 128], identity=cx.ident),
                         reads=[bckvn[s2], cx.bident], writes=[bT], inc=(kc == 1))
                g0 = c * 512 + jj * 128
                P.op("act", lambda e, g0=g0: e.copy(out=cqnT[:, :, g0:g0 + 128],
                                                    in_=pst[:, 0:384].rearrange("p (k t) -> p k t", k=3)),
                     reads=[bT], writes=[bcqnT])
                P.op("act", lambda e, g0=g0: e.copy(out=ckvnT[:, :, g0:g0 + 128],
                                                    in_=pst[:, 384:640].rearrange("p (k t) -> p k t", k=2)),
                     reads=[bT], writes=[bckvnT])
        P.barrier()
        A.release()
        A.mark()
        qT = [A.alloc([S], BF16) for _ in range(2)]
        bqT = P.bufs(2, "qT")
        vaug = [A.alloc([16, 65], BF16) for _ in range(2)]
        bva = P.bufs(2, "vaug")
        for b2 in range(2):
            P.op("dve", lambda e, b2=b2: e.memset(vaug[b2][:, :, 64:65], 1.0), writes=[bva[b2]])
        E = [A.alloc([512], BF16) for _ in range(6)]
        bE = P.bufs(6, "E")
        tmpd = [A.alloc([128], F32) for _ in range(2)]
        btd = P.bufs(2, "tmpd")
        tq1 = A.alloc([512], F32)
        tq2 = A.alloc([512], F32)
        btq1, btq2 = P.buf("tq1"), P.buf("tq2")
        rec = A.alloc([8], F32)
        brec = P.bufs(2, "rec")
        nS = 0
        nE = 0
        nD = 0
        def proj(h):
            hb = h % 2
            for c in range(4):
                cs = slice(c * 512, (c + 1) * 512)
                for kc in range(3):
                    P.op("pe", lambda e, kc=kc, cs=cs, h=h: e.matmul(ps[2][0:96, :], lhsT=wq[:, kc, h * 96:(h + 1) * 96], rhs=cqnT[:, kc, cs],
                                                                    start=(kc == 0), stop=(kc == 2)),
                         reads=[bwq, bcqnT], writes=[pb[2]], inc=(kc == 2))
                for kc in range(3):
                    P.op("pe", lambda e, kc=kc, cs=cs, h=h: e.matmul(ps[3][0:96, :], lhsT=wqs[:, kc, h, :], rhs=cqnT[:, kc, cs],
                                                                    start=(kc == 0), stop=(kc == 2)),
                         reads=[bwqs, bcqnT], writes=[pb[3]], inc=(kc == 2))
                for kc in range(2):
                    P.op("pe", lambda e, kc=kc, cs=cs, h=h: e.matmul(ps[4][0:64, :], lhsT=wkn[:, kc, h, :], rhs=ckvnT[:, kc, cs],
                                                                    start=(kc == 0), stop=(kc == 1)),
                         reads=[bwkn, bckvnT], writes=[pb[4]], inc=(kc == 1))
                P.op("act", lambda e, cs=cs, hb=hb: e.copy(out=qT[hb][0:64, cs], in_=ps[2][0:64, :]),
                     reads=[pb[2]], writes=[bqT[hb]])
                P.op("dve", lambda e, cs=cs: e.tensor_tensor(out=tq1[64:96, :], in0=ps[2][64:96, :], in1=CC[64:96, cs], op=ALU.mult),
                     reads=[pb[2], bcs], writes=[btq1])
                P.op("dve", lambda e, cs=cs: e.tensor_tensor(out=tq2[64:96, :], in0=ps[3][64:96, :], in1=SS[64:96, cs], op=ALU.mult),
                     reads=[pb[3], bcs], writes=[btq2])
                P.op("dve", lambda e, cs=cs, hb=hb: e.tensor_tensor(out=qT[hb][64:96, cs], in0=tq1[64:96, :], in1=tq2[64:96, :],
                                                                    op=ALU.add),
                     reads=[btq1, btq2], writes=[bqT[hb]])
                P.op("act", lambda e, cs=cs, hb=hb: e.copy(out=kT[hb][0:64, cs], in_=ps[4][0:64, :]),
                     reads=[pb[4]], writes=[bkT[hb]])
            for g8 in range(2):
                for kk in range(8):
                    kb = g8 * 8 + kk
                    for kc in range(2):
                        P.op("pe", lambda e, kc=kc, kb=kb, kk=kk, h=h: e.matmul(
                            ps[3][:, kk * 64:(kk + 1) * 64], lhsT=ckvnT[:, kc, kb * 128:(kb + 1) * 128], rhs=wv[:, kc, h, :],
                            start=(kc == 0), stop=(kc == 1)),
                             reads=[bwv, bckvnT], writes=[pb[3]], inc=(kc == 1 and kk == 7))
                P.op("act", lambda e, g8=g8, hb=hb: e.copy(out=vaug[hb][:, g8 * 8:(g8 + 1) * 8, 0:64],
                                                           in_=ps[3].rearrange("p (k d) -> p k d", k=8)),
                     reads=[pb[3]], writes=[bva[hb]])

        import os as _os
        ILV = _os.environ.get("MLA_ILV", "1") == "1"
        if ILV:
            proj(0)
        for h in range(16):
            hb = h % 2
            if not ILV:
                proj(h)
            stages = []
            SB = [0, 1, 5]
            for c in range(4):
                ob = c % 2
                psO, bO = ps[6 + ob], pb[6 + ob]
                nkb = 4 * c + 4
                for kb in range(nkb):
                    qlo = max(kb, 4 * c)
                    ncol = (4 * c + 4 - qlo) * 128
                    sb = SB[nS % 3]
                    nS += 1
                    eb = nE % 6
                    nE += 1
                    diag = kb >= 4 * c
                    db = nD % 2
                    if diag:
                        nD += 1

                    def front(kb=kb, qlo=qlo, ncol=ncol, sb=sb, eb=eb, diag=diag, db=db, hb=hb):
                        P.op("pe", lambda e: e.matmul(ps[sb][:, 0:ncol], lhsT=kT[hb][0:96, kb * 128:(kb + 1) * 128],
                                                      rhs=qT[hb][0:96, qlo * 128:qlo * 128 + ncol], start=True, stop=True),
                             reads=[bkT[hb], bqT[hb]], writes=[pb[sb]])
                        c0 = 0
                        if diag:
                            P.op("dve", lambda e: e.tensor_tensor(out=tmpd[db], in0=ps[sb][:, 0:128], in1=mdiag, op=ALU.add),
                                 reads=[pb[sb], bmd], writes=[btd[db]])
                            P.op("act", lambda e: e.activation(out=E[eb][:, 0:128], in_=tmpd[db], func=AF.Exp, scale=scale),
                                 reads=[btd[db]], writes=[bE[eb]])
                            c0 = 128
                        if ncol > c0:
                            P.op("act", lambda e: e.activation(out=E[eb][:, c0:ncol], in_=ps[sb][:, c0:ncol], func=AF.Exp,
                                                               scale=scale), reads=[pb[sb]], writes=[bE[eb]])

                    def back(kb=kb, qlo=qlo, eb=eb, c=c, ob=ob, psO=psO, bO=bO, hb=hb, h=h, nkb=nkb):
                        nq = 4 * c + 4 - qlo
                        for qi in range(nq):
                            oi = qlo + qi - 4 * c
                            P.op("pe", lambda e, qi=qi, oi=oi: e.matmul(
                                psO[:, oi * 65:(oi + 1) * 65], lhsT=E[eb][:, qi * 128:(qi + 1) * 128], rhs=vaug[hb][:, kb, :],
                                start=(kb == 0 and qi == 0), stop=False, skip_group_check=True),
                                 reads=[bE[eb], bva[hb]], writes=[bO], inc=(qi == nq - 1))
                        if kb == nkb - 1:
                            rb = brec[ob]
                            P.op("dve", lambda e: e.reciprocal(out=rec[:, ob * 4:(ob + 1) * 4],
                                                               in_=psO[:, 0:260].rearrange("p (q d) -> p q d", d=65)[:, :, 64]),
                                 reads=[bO], writes=[rb])
                            for oi in range(4):
                                qb = 4 * c + oi
                                P.op("dve", lambda e, oi=oi, qb=qb: e.tensor_scalar(
                                    out=o_all[:, qb, h * 64:(h + 1) * 64], in0=psO[:, oi * 65:oi * 65 + 64],
                                    scalar1=rec[:, ob * 4 + oi:ob * 4 + oi + 1], scalar2=None, op0=ALU.mult),
                                     reads=[bO, rb], writes=[bo_all[qb]])

                    stages.append((front, back))
            if ILV and h + 1 < 16:
                mid = len(stages) // 2
                f0, b0 = stages[mid]
                stages[mid] = ((lambda f0=f0, h=h: (proj(h + 1), f0())), b0)
            run_pipeline(stages, 2)
        P.barrier()
        A.release()
        A.mark()
        oT = [A.alloc([8, 128], BF16) for _ in range(2)]
        boT = P.bufs(2, "oT")
        hr = [A.alloc([D], F32) for _ in range(2)]
        bhr = P.bufs(2, "hr")
        for qb in range(16):
            out_proj_tile(cx, o_all[:, qb, :], bo_all[qb], wo, bwo, src, dst, tok0 + qb * 128, oT, boT, hr, bhr, qb % 2,
                          psT, bT, ps[0:2], pb[0:2])
        P.barrier()
        A.release()
    A.release()


def rev_cols(ap, start, n):
    a = [list(x) for x in ap.ap]
    assert len(a) == 2 and a[1][0] == 1, a
    return bass.AP(ap.tensor, ap.offset + start, [a[0], [-1, n]])


def nsa_setup(cx):
    P, A, ps, pb = cx.P, cx.A, cx.ps, cx.pb
    nc = cx.nc
    cx.rtab_t = nc.dram_tensor("rtab", [17, 4096], F32)
    rtab = cx.rtab_t.ap()[0:16, :]
    A.mark()
    tbl = A.alloc([16], F32)
    btbl = P.buf("tbl")
    P.op("dve", lambda e: e.memset(tbl[0:64, :], NEGM), writes=[btbl])
    P.dma("sp", [(tbl[0:32, :], cx.inp["rel_bias"])], writes=[btbl])
    oh = A.alloc([4096], F32)
    boh = P.buf("oh")
    P.dma("sp", [(oh[0:33, :], cx.cd["c_oh"])], writes=[boh])
    rt = A.alloc([4096], F32)
    brt = P.buf("rt")
    for ch in range(8):
        b = ch % 2
        P.op("pe", lambda e, ch=ch, b=b: e.matmul(ps[b][0:16, :], lhsT=tbl[0:33, :], rhs=oh[0:33, ch * 512:(ch + 1) * 512],
                                                  start=True, stop=True),
             reads=[btbl, boh], writes=[pb[b]])
        P.op("act", lambda e, ch=ch, b=b: e.copy(out=rt[0:16, ch * 512:(ch + 1) * 512], in_=ps[b][0:16, :]),
             reads=[pb[b]], writes=[brt])
    P.dma("sp", [(rtab, rt[0:16, :])], reads=[brt], key="rtab_st")
    P.barrier()
    A.release()
    dbg(cx, 0)


def nsa_phase(cx, j, li, src, dst):
    P, A, ps, pb = cx.P, cx.A, cx.ps, cx.pb
    A.mark()
    scale = 0.125
    Win_d = cx.inp["nsa_w_in"][j]
    w_in_r = Win_d.rearrange("(kc p) c -> p kc c", p=128)
    rt = cx.rtab_t
    W1 = [A.alloc([32, 128], BF16) for _ in range(2)]
    bW1 = P.bufs(2, "W1")
    w2 = [A.alloc([64], BF16) for _ in range(2)]
    bw2 = P.bufs(2, "w2")
    for kv, nm in enumerate(["k", "v"]):
        P.dma("pool", [(W1[kv][0:64], cx.inp["nsa_cmp_w1_%s" % nm][j].rearrange("(l d) c -> d l c", d=64))], writes=[bW1[kv]])
        P.dma("pool", [(w2[kv], cx.inp["nsa_cmp_w2_%s" % nm][j])], writes=[bw2[kv]])
    posf = A.alloc([2, 32], F32)
    posb = A.alloc([2, 32], BF16)
    bposf, bposb = P.buf("posf"), P.buf("posb")
    P.dma("sp", [(posf[0:64, 0, :], cx.inp["nsa_cmp_pos_k"][j].rearrange("l d -> d l")),
                 (posf[0:64, 1, :], cx.inp["nsa_cmp_pos_v"][j].rearrange("l d -> d l"))], writes=[bposf],
          allow_slow_non_contiguous=True)
    P.op("act", lambda e: e.copy(out=posb[0:64], in_=posf[0:64]), reads=[bposf], writes=[bposb])
    cpos = A.alloc([2], F32)
    bcpos = P.buf("cpos")
    for kv in range(2):
        for l in range(32):
            P.op("pe", lambda e, kv=kv, l=l: e.matmul(ps[4 + kv][:, 0:1], lhsT=W1[kv][0:64, l, :], rhs=posb[0:64, kv, l:l + 1],
                                                      start=(l == 0), stop=(l == 31)),
                 reads=[bW1[kv], bposb], writes=[pb[4 + kv]], inc=(l == 31))
        P.op("act", lambda e, kv=kv: e.copy(out=cpos[:, kv:kv + 1], in_=ps[4 + kv][:, 0:1]), reads=[pb[4 + kv]], writes=[bcpos])
    gain, bgain = load_gain(cx, cx.inp["mix_norm"][li], D)
    c31 = A.alloc([16], F32)
    bc31 = P.buf("c31")
    P.dma("sp", [(c31, cx.inp["rel_bias"][31].partition_broadcast(128))], writes=[bc31])
    m4 = A.alloc([128], F32)
    bm4 = P.buf("m4")
    P.dma("sp", [(m4, cx.cd["c_m4"])], writes=[bm4])
    selc = A.alloc([2, 16, 32], F32)
    bselc = P.buf("selc")
    P.dma("sp", [(selc[:, 0], cx.cd["c_cm"]), (selc[:, 1], cx.cd["c_add"])], writes=[bselc])
    junk = A.alloc([D], BF16)
    bjunk = P.buf("junk")
    ss = A.alloc([8], F32)
    bss = P.bufs(8, "ss")
    psT, bT = ps[6], pb[6]
    dbg(cx, 1)

    for sq in range(NSEQ):
        tok0 = sq * S
        mT = A.alloc([8, S], BF16) if sq == 0 else mT
        gates = A.alloc([16, 48], F32) if sq == 0 else gates
        o_all = A.alloc([16, D], BF16) if sq == 0 else o_all
        if sq == 0:
            bmT, bgates = P.buf("mT"), P.bufs(16, "gates")
            bo_all = P.bufs(16, "o_all")
        A.mark()
        wgt = A.alloc([8, 48], BF16)
        bwgt = P.buf("wgt")
        P.dma("pool", [(wgt, w_in_r[:, :, 2560:2608])], writes=[bwgt])
        hn = [A.alloc([D], F32) for _ in range(2)]
        bhn = P.bufs(2, "hn")
        xn = [A.alloc([D], BF16) for _ in range(2)]
        bxn = P.bufs(2, "xn")
        for qb in range(16):
            norm_T_tile(cx, src, tok0 + qb * 128, gain, bgain, hn, bhn, qb % 2, ss, bss, qb % 8, junk, bjunk,
                        xn, bxn, psT, bT, mT, bmT, qb * 128)
            b = qb % 2
            for kc in range(8):
                P.op("pe", lambda e, kc=kc, qb=qb, b=b: e.matmul(ps[b][:, 0:48], lhsT=mT[:, kc, qb * 128:(qb + 1) * 128],
                                                                rhs=wgt[:, kc, :], start=(kc == 0), stop=(kc == 7)),
                     reads=[bmT, bwgt], writes=[pb[b]], inc=(kc == 7))
            P.op("dve", lambda e, qb=qb, b=b: e.tensor_copy(out=gates[:, qb, :], in_=ps[b][:, 0:48]),
                 reads=[pb[b]], writes=[bgates[qb]])
        P.op("act", lambda e: e.activation(out=gates, in_=gates, func=AF.Sigmoid), reads=bgates, writes=bgates)
        P.barrier()
        A.release()
        dbg(cx, 2)
        for g in range(4):
            A.mark()
            wg_ = A.alloc([8, 640], BF16)
            bwg_ = P.buf("wing")
            prs = [(wg_[:, :, 0:256], w_in_r[:, :, g * 256:(g + 1) * 256])]
            for i in range(6):
                prs.append((wg_[:, :, 256 + i * 64:320 + i * 64], w_in_r[:, :, 1024 + i * 256 + g * 64:1024 + i * 256 + (g + 1) * 64]))
            P.dma("pool", prs, writes=[bwg_])
            x01 = A.alloc([4, 256], F32)
            bx01 = P.buf("x01")
            P.dma("sp", [(x01, bass.AP(rt, (4 * g) * 4096 + 1792, [[1, 128], [4096, 4], [1, 256]]))], writes=[bx01])
            qa = [A.alloc([S], BF16) for _ in range(4)]
            bqa = P.bufs(4, "qa")
            ka = A.alloc([S], BF16)
            bka = P.buf("ka")
            P.dma("sp", [(ka[64:96, :], cx.cd["c_bexp"])], writes=[bka])
            kw = A.alloc([S], BF16)
            kcf = A.alloc([S], BF16)
            vcf = A.alloc([S], BF16)
            bkw, bkcf, bvcf = P.buf("kw"), P.buf("kcf"), P.buf("vcf")
            vs = A.alloc([16, 65], BF16)
            vw = A.alloc([16, 65], BF16)
            bvs, bvw = P.buf("vs"), P.buf("vw")
            P.op("dve", lambda e: e.memset(vs[:, :, 64:65], 1.0), writes=[bvs])
            P.op("dve", lambda e: e.memset(vw[:, :, 64:65], 1.0), writes=[bvw])
            kcT = A.alloc([128], BF16)
            bkcT = P.buf("kcT")
            vcx = A.alloc([97], BF16)
            bvcx = P.buf("vcx")
            P.op("dve", lambda e: e.memset(vcx[:, 64:65], 1.0), writes=[bvcx])
            P.dma("sp", [(vcx[0:127, 65:97], cx.cd["c_ov"])], writes=[bvcx])
            ocmp = A.alloc([16, 4, 64], BF16)
            bocmp = P.bufs(16, "ocmp")
            imp = A.alloc([16, 32], F32)
            bimp = P.bufs(16, "imp")
            nst = A.alloc([S], BF16)
            bnst = P.buf("nst")
            pi = 0
            fm = [(qa[0], bqa[0], 0), (qa[1], bqa[1], 64), (qa[2], bqa[2], 128), (qa[3], bqa[3], 192),
                  (kcf, bkcf, 256), (vcf, bvcf, 320), (ka, bka, 384), (kw, bkw, 512)]
            for (dt_, bdt, c0) in fm:
                for c in range(4):
                    b = 4 + pi % 2
                    pi += 1
                    cs = slice(c * 512, (c + 1) * 512)
                    for kc in range(8):
                        P.op("pe", lambda e, kc=kc, cs=cs, c0=c0, b=b: e.matmul(ps[b][0:64, :], lhsT=wg_[:, kc, c0:c0 + 64],
                                                                               rhs=mT[:, kc, cs], start=(kc == 0), stop=(kc == 7)),
                             reads=[bwg_, bmT], writes=[pb[b]], inc=(kc == 7))
                    P.op("act", lambda e, dt_=dt_, cs=cs, b=b: e.copy(out=dt_[0:64, cs], in_=ps[b][0:64, :]),
                         reads=[pb[b]], writes=[bdt])
            for (vt, bvt, c0) in [(vs, bvs, 448), (vw, bvw, 576)]:
                for g8 in range(2):
                    b = 4 + pi % 2
                    pi += 1
                    for kk in range(8):
                        kb = g8 * 8 + kk
                        for kc in range(8):
                            P.op("pe", lambda e, kc=kc, kb=kb, kk=kk, c0=c0, b=b: e.matmul(
                                ps[b][:, kk * 64:(kk + 1) * 64], lhsT=mT[:, kc, kb * 128:(kb + 1) * 128], rhs=wg_[:, kc, c0:c0 + 64],
                                start=(kc == 0), stop=(kc == 7)),
                                 reads=[bwg_, bmT], writes=[pb[b]], inc=(kc == 7 and kk == 7))
                    P.op("act", lambda e, vt=vt, g8=g8, b=b: e.copy(out=vt[:, g8 * 8:(g8 + 1) * 8, 0:64],
                                                                  in_=ps[b].rearrange("p (k d) -> p k d", k=8)),
                         reads=[pb[b]], writes=[bvt])
            dbg(cx, 3)
            xh = A.alloc([128], F32)
            x2 = A.alloc([128], F32)
            sgm = A.alloc([128], F32)
            gh = A.alloc([128], BF16)
            bxh, bx2, bsgm, bgh = P.buf("xh"), P.buf("x2"), P.buf("sgm"), P.buf("gh")
            for kv, (cf, bcf) in enumerate([(kcf, bkcf), (vcf, bvcf)]):
                b = 4 + kv
                for l in range(32):
                    P.op("pe", lambda e, kv=kv, l=l, cf=cf, b=b: e.matmul(ps[b][:, 0:127], lhsT=W1[kv][0:64, l, :],
                                                                         rhs=cf[0:64, l:l + 16 * 126 + 1:16],
                                                                         start=(l == 0), stop=(l == 31)),
                         reads=[bW1[kv], bcf], writes=[pb[b]], inc=(l == 31))
                P.op("act", lambda e, kv=kv, b=b: e.activation(out=xh[:, 0:127], in_=ps[b][:, 0:127], func=AF.Identity,
                                                               bias=cpos[:, kv:kv + 1], scale=1.0),
                     reads=[pb[b], bcpos], writes=[bxh])
                P.op("dve", lambda e: e.tensor_tensor(out=x2[:, 0:127], in0=xh[:, 0:127], in1=xh[:, 0:127], op=ALU.mult),
                     reads=[bxh], writes=[bx2])
                P.op("dve", lambda e: e.tensor_scalar(out=x2[:, 0:127], in0=x2[:, 0:127], scalar1=0.044715, scalar2=1.0,
                                                      op0=ALU.mult, op1=ALU.add), reads=[bx2], writes=[bx2])
                P.op("dve", lambda e: e.tensor_tensor(out=x2[:, 0:127], in0=x2[:, 0:127], in1=xh[:, 0:127], op=ALU.mult),
                     reads=[bx2, bxh], writes=[bx2])
                P.op("act", lambda e: e.activation(out=sgm[:, 0:127], in_=x2[:, 0:127], func=AF.Sigmoid, scale=1.5957691216057308),
                     reads=[bx2], writes=[bsgm])
                P.op("dve", lambda e: e.tensor_tensor(out=gh[:, 0:127], in0=xh[:, 0:127], in1=sgm[:, 0:127], op=ALU.mult),
                     reads=[bxh, bsgm], writes=[bgh])
                if kv == 0:
                    P.op("pe", lambda e: e.matmul(ps[6][0:64, 0:127], lhsT=w2[0], rhs=gh[:, 0:127], start=True, stop=True),
                         reads=[bw2[0], bgh], writes=[pb[6]])
                    P.op("act", lambda e: e.copy(out=kcT[0:64, 0:127], in_=ps[6][0:64, 0:127]), reads=[pb[6]], writes=[bkcT])
                else:
                    P.op("pe", lambda e: e.matmul(ps[6][0:127, 0:64], lhsT=gh[:, 0:127], rhs=w2[1], start=True, stop=True),
                         reads=[bw2[1], bgh], writes=[pb[6]])
                    P.op("act", lambda e: e.copy(out=vcx[0:127, 0:64], in_=ps[6][0:127, 0:64]), reads=[pb[6]], writes=[bvcx])
            dbg(cx, 4)
            xb = [A.alloc([S], F32) for _ in range(2)]
            bxb = P.bufs(2, "xb")
            tmpc = [A.alloc([512], F32) for _ in range(2)]
            btmpc = P.bufs(2, "tmpc")
            Ec = [A.alloc([512], BF16) for _ in range(3)]
            bEc = P.bufs(3, "Ec")
            rc = A.alloc([8], F32)
            brc = P.bufs(2, "rc")
            rg = A.alloc([8], F32)
            brg = P.bufs(2, "rg")
            nc_ = 0
            stages3 = []
            SB3 = [0, 1, 7]
            for r in range(4):
                h = 4 * g + r
                xbr = xb[r % 2]
                bxbr = bxb[r % 2]
                for c in range(4):
                    sb = SB3[nc_ % 3]
                    eb = nc_ % 3
                    tb = nc_ % 2
                    ob = nc_ % 2
                    nc_ += 1
                    psC, bC = ps[2 + ob], pb[2 + ob]

                    def front(r=r, h=h, c=c, sb=sb, eb=eb, tb=tb, xbr=xbr, bxbr=bxbr):
                        if c == 0:
                            P.dma("sp", [(xbr, bass.AP(rt, h * 4096 + 31, [[16, 128], [1, 2048]]))], writes=[bxbr],
                                  key="xb%d" % (r % 2))
                        cs = slice(c * 512, (c + 1) * 512)
                        P.op("pe", lambda e: e.matmul(ps[sb][0:127, :], lhsT=kcT[0:64, 0:127], rhs=qa[r][0:64, cs],
                                                      start=True, stop=True),
                             reads=[bkcT, bqa[r]], writes=[pb[sb]])
                        P.op("dve", lambda e: e.scalar_tensor_tensor(
                            out=tmpc[tb][0:127, :], in0=ps[sb][0:127, :], scalar=scale, in1=rev_cols(xbr[0:127, :], 2047 - c * 512, 512),
                            op0=ALU.mult, op1=ALU.add), reads=[pb[sb], bxbr], writes=[btmpc[tb]])
                        P.op("act", lambda e: e.activation(out=Ec[eb][0:127, :], in_=tmpc[tb][0:127, :], func=AF.Exp),
                             reads=[btmpc[tb]], writes=[bEc[eb]])

                    def back(r=r, h=h, c=c, eb=eb, ob=ob, psC=psC, bC=bC):
                        for qi in range(4):
                            P.op("pe", lambda e, qi=qi: e.matmul(
                                psC[:, qi * 97:(qi + 1) * 97], lhsT=Ec[eb][0:127, qi * 128:(qi + 1) * 128], rhs=vcx[0:127, :],
                                start=(qi == 0), stop=False, skip_group_check=True),
                                 reads=[bEc[eb], bvcx], writes=[bC], inc=(qi == 3))
                        pc3 = psC[:, 0:388].rearrange("p (q d) -> p q d", d=97)
                        P.op("dve", lambda e: e.tensor_scalar(out=rc[:, ob * 4:(ob + 1) * 4], in0=pc3[:, :, 64],
                                                              scalar1=1e-30, scalar2=None, op0=ALU.max),
                             reads=[bC], writes=[brc[ob]])
                        P.op("dve", lambda e: e.reciprocal(out=rc[:, ob * 4:(ob + 1) * 4], in_=rc[:, ob * 4:(ob + 1) * 4]),
                             reads=[brc[ob]], writes=[brc[ob]])
                        P.op("dve", lambda e: e.tensor_tensor(out=rg[:, ob * 4:(ob + 1) * 4], in0=rc[:, ob * 4:(ob + 1) * 4],
                                                              in1=gates[:, 4 * c:4 * c + 4, 3 * h], op=ALU.mult),
                             reads=[brc[ob]] + bgates[4 * c:4 * c + 4], writes=[brg[ob]])
                        for qi in range(4):
                            qb = 4 * c + qi
                            P.op("dve", lambda e, qi=qi, qb=qb: e.tensor_scalar(
                                out=ocmp[:, qb, r, :], in0=psC[:, qi * 97:qi * 97 + 64], scalar1=rg[:, ob * 4 + qi:ob * 4 + qi + 1],
                                scalar2=None, op0=ALU.mult), reads=[bC, brg[ob]], writes=[bocmp[qb]])
                            if r == 0:
                                P.op("dve", lambda e, qi=qi, qb=qb: e.tensor_scalar(
                                    out=imp[:, qb, :], in0=psC[:, qi * 97 + 65:qi * 97 + 97], scalar1=rc[:, ob * 4 + qi:ob * 4 + qi + 1],
                                    scalar2=None, op0=ALU.mult), reads=[bC, brc[ob]], writes=[bimp[qb]])
                            else:
                                P.op("dve", lambda e, qi=qi, qb=qb: e.scalar_tensor_tensor(
                                    out=imp[:, qb, :], in0=psC[:, qi * 97 + 65:qi * 97 + 97], scalar=rc[:, ob * 4 + qi:ob * 4 + qi + 1],
                                    in1=imp[:, qb, :], op0=ALU.mult, op1=ALU.add), reads=[bC, brc[ob], bimp[qb]], writes=[bimp[qb]])

                    stages3.append((front, back))
            run_pipeline(stages3, 2)
            dbg(cx, 5)
            sc = A.alloc([32], F32)
            wk = A.alloc([32], F32)
            m8 = A.alloc([16], F32)
            stg = A.alloc([96], BF16)
            bsc, bwk, bm8, bstg = P.buf("sc"), P.buf("wk"), P.buf("m8"), P.buf("stg")
            P.op("dve", lambda e: e.memset(stg, 0.0), writes=[bstg])
            for qb in range(16):
                P.op("dve", lambda e, qb=qb: e.tensor_tensor(out=sc, in0=imp[:, qb, :], in1=selc[:, 0, qb, :], op=ALU.mult),
                     reads=[bimp[qb], bselc], writes=[bsc])
                P.op("dve", lambda e, qb=qb: e.tensor_tensor(out=sc, in0=sc, in1=selc[:, 1, qb, :], op=ALU.add),
                     reads=[bsc, bselc], writes=[bsc])
                P.op("dve", lambda e: e.max(out=m8[:, 0:8], in_=sc), reads=[bsc], writes=[bm8])
                P.op("dve", lambda e: e.match_replace(out=wk, in_to_replace=m8[:, 0:8], in_values=sc, imm_value=-3.0e38),
                     reads=[bsc, bm8], writes=[bwk])
                P.op("dve", lambda e: e.max(out=m8[:, 8:16], in_=wk), reads=[bwk], writes=[bm8])
                P.op("dve", lambda e: e.tensor_scalar(out=wk, in0=sc, scalar1=m8[:, 15:16], scalar2=None, op0=ALU.is_ge),
                     reads=[bsc, bm8], writes=[bwk])
                P.op("dve", lambda e: e.tensor_scalar(out=stg[:, 64:96], in0=wk, scalar1=-NEGM, scalar2=NEGM, op0=ALU.mult, op1=ALU.add),
                     reads=[bwk], writes=[bstg])
                pst = psT.bitcast(BF16)
                P.op("pe", lambda e, pst=pst: e.transpose(out=pst[0:96, 0:128], in_=stg, identity=cx.ident),
                     reads=[bstg, cx.bident], writes=[bT])
                P.op("act", lambda e, qb=qb, pst=pst: e.copy(out=nst[64:96, qb * 128:(qb + 1) * 128], in_=pst[64:96, 0:128]),
                     reads=[bT], writes=[bnst])
            for r in range(4):
                P.op("act", lambda e, r=r: e.copy(out=qa[r][64:96, :], in_=nst[64:96, :]), reads=[bnst], writes=[bqa[r]])
            dbg(cx, 6)
            E = [A.alloc([512], BF16) for _ in range(6)]
            bE = P.bufs(6, "E")
            tmpd = [A.alloc([256], F32) for _ in range(3)]
            btd = P.bufs(3, "tmpd")
            rsw = A.alloc([16], F32)
            brsw = P.bufs(2, "rsw")
            tmpo = [A.alloc([64], F32) for _ in range(2)]
            btmpo = P.bufs(2, "tmpo")
            st = {"S": 0, "E": 0, "D": 0, "O": 0}
            zE = A.alloc([128], BF16)
            bzE = P.buf("zE")
            P.op("dve", lambda e: e.memset(zE, 0.0), writes=[bzE])

            SB5 = [0, 1, 6, 7]
            stages5 = []

            def add_branch(kind, r, h, c, psO, bO, fin):
                kb_lo = 0 if kind == 0 else max(0, 4 * c - 4)
                kbs = list(range(kb_lo, 4 * c + 4))
                vt, bvt = (vs, bvs) if kind == 0 else (vw, bvw)
                for kb in kbs:
                    qlo = max(kb, 4 * c)
                    qhi = 4 * c + 3 if kind == 0 else min(kb + 4, 4 * c + 3)
                    nq = qhi - qlo + 1
                    ncol = nq * 128
                    sb = SB5[st["S"] % 4]
                    st["S"] += 1
                    eb = st["E"] % 6
                    st["E"] += 1
                    d0 = qlo - kb
                    n01 = max(0, min(2, d0 + nq) - d0) if d0 < 2 else 0
                    n4 = 1 if (kind == 1 and qhi - kb == 4) else 0
                    ncst = nq - n01 - n4
                    db1 = st["D"] % 3
                    if n01:
                        st["D"] += 1
                    db4 = st["D"] % 3
                    if n4:
                        st["D"] += 1

                    def front(kind=kind, r=r, h=h, kb=kb, qlo=qlo, ncol=ncol, sb=sb, eb=eb, d0=d0, n01=n01, n4=n4, ncst=ncst,
                              db1=db1, db4=db4):
                        if kind == 0:
                            P.op("pe", lambda e: e.matmul(ps[sb][:, 0:ncol], lhsT=ka[0:96, kb * 128:(kb + 1) * 128],
                                                          rhs=qa[r][0:96, qlo * 128:qlo * 128 + ncol], start=True, stop=True),
                                 reads=[bka, bqa[r]], writes=[pb[sb]])
                        else:
                            P.op("pe", lambda e: e.matmul(ps[sb][:, 0:ncol], lhsT=kw[0:64, kb * 128:(kb + 1) * 128],
                                                          rhs=qa[r][0:64, qlo * 128:qlo * 128 + ncol], start=True, stop=True),
                                 reads=[bkw, bqa[r]], writes=[pb[sb]])
                        col = 0
                        if n01:
                            w = n01 * 128
                            P.op("dve", lambda e: e.scalar_tensor_tensor(
                                out=tmpd[db1][:, 0:w], in0=ps[sb][:, 0:w], scalar=scale,
                                in1=rev_cols(x01[:, r, :], 255 - d0 * 128, w), op0=ALU.mult, op1=ALU.add),
                                 reads=[pb[sb], bx01], writes=[btd[db1]])
                            P.op("act", lambda e: e.activation(out=E[eb][:, 0:w], in_=tmpd[db1][:, 0:w], func=AF.Exp),
                                 reads=[btd[db1]], writes=[bE[eb]])
                            col = w
                        col4 = (n01 + ncst) * 128
                        if n4:
                            P.op("dve", lambda e: e.tensor_tensor(out=tmpd[db4][:, 0:128], in0=ps[sb][:, col4:col4 + 128], in1=m4,
                                                                  op=ALU.add), reads=[pb[sb], bm4], writes=[btd[db4]])
                            P.op("act", lambda e: e.activation(out=E[eb][:, col4:col4 + 128], in_=tmpd[db4][:, 0:128], func=AF.Exp,
                                                               scale=scale, bias=c31[:, h:h + 1]),
                                 reads=[btd[db4], bc31], writes=[bE[eb]])
                        if ncst:
                            w2_ = ncst * 128
                            P.op("act", lambda e: e.activation(out=E[eb][:, col:col + w2_], in_=ps[sb][:, col:col + w2_], func=AF.Exp,
                                                               scale=scale, bias=c31[:, h:h + 1]),
                                 reads=[pb[sb], bc31], writes=[bE[eb]])

                    def back(kb=kb, qlo=qlo, nq=nq, eb=eb, c=c, psO=psO, bO=bO, vt=vt, bvt=bvt, is_first=(kb == kbs[0]),
                             is_last=(kb == kbs[-1]), fin=fin):
                        if is_first:
                            for oi in range(4):
                                P.op("pe", lambda e, oi=oi: e.matmul(psO[:, oi * 65:(oi + 1) * 65], lhsT=zE, rhs=vt[:, 0, :],
                                                                   start=(oi == 0), stop=False, skip_group_check=True),
                                     reads=[bzE, bvt], writes=[bO], inc=(oi == 3))
                        for qi in range(nq):
                            oi = qlo + qi - 4 * c
                            P.op("pe", lambda e, qi=qi, oi=oi: e.matmul(
                                psO[:, oi * 65:(oi + 1) * 65], lhsT=E[eb][:, qi * 128:(qi + 1) * 128], rhs=vt[:, kb, :],
                                start=False, stop=False, skip_group_check=True),
                                 reads=[bE[eb], bvt], writes=[bO], inc=(qi == nq - 1))
                        if is_last and fin is not None:
                            fin()

                    stages5.append((front, back))

            def make_fin(r, h, c, ob, psOs, bOs, psOw, bOw):
                def fin():
                    o3s = psOs[:, 0:260].rearrange("p (q d) -> p q d", d=65)
                    o3w = psOw[:, 0:260].rearrange("p (q d) -> p q d", d=65)
                    rs_ = rsw[:, ob * 8:ob * 8 + 4]
                    rw_ = rsw[:, ob * 8 + 4:ob * 8 + 8]
                    P.op("dve", lambda e: e.reciprocal(out=rs_, in_=o3s[:, :, 64]), reads=[bOs], writes=[brsw[ob]])
                    P.op("dve", lambda e: e.reciprocal(out=rw_, in_=o3w[:, :, 64]), reads=[bOw], writes=[brsw[ob]])
                    P.op("dve", lambda e: e.tensor_tensor(out=rs_, in0=rs_, in1=gates[:, 4 * c:4 * c + 4, 3 * h + 1], op=ALU.mult),
                         reads=[brsw[ob]] + bgates[4 * c:4 * c + 4], writes=[brsw[ob]])
                    P.op("dve", lambda e: e.tensor_tensor(out=rw_, in0=rw_, in1=gates[:, 4 * c:4 * c + 4, 3 * h + 2], op=ALU.mult),
                         reads=[brsw[ob]] + bgates[4 * c:4 * c + 4], writes=[brsw[ob]])
                    for oi in range(4):
                        qb = 4 * c + oi
                        tb = oi % 2
                        P.op("dve", lambda e, oi=oi, qb=qb, tb=tb: e.scalar_tensor_tensor(
                            out=tmpo[tb], in0=psOs[:, oi * 65:oi * 65 + 64], scalar=rsw[:, ob * 8 + oi:ob * 8 + oi + 1],
                            in1=ocmp[:, qb, r, :], op0=ALU.mult, op1=ALU.add),
                             reads=[bOs, brsw[ob], bocmp[qb]], writes=[btmpo[tb]])
                        P.op("dve", lambda e, oi=oi, qb=qb, tb=tb: e.scalar_tensor_tensor(
                            out=o_all[:, qb, h * 64:(h + 1) * 64], in0=psOw[:, oi * 65:oi * 65 + 64],
                            scalar=rsw[:, ob * 8 + 4 + oi:ob * 8 + 4 + oi + 1], in1=tmpo[tb], op0=ALU.mult, op1=ALU.add),
                             reads=[bOw, brsw[ob], btmpo[tb]], writes=[bo_all[qb]])
                return fin

            for r in range(4):
                h = 4 * g + r
                for c in range(4):
                    ob = st["O"] % 2
                    st["O"] += 1
                    psOs, bOs = ps[2 + ob], pb[2 + ob]
                    psOw, bOw = ps[4 + ob], pb[4 + ob]
                    add_branch(0, r, h, c, psOs, bOs, None)
                    add_branch(1, r, h, c, psOw, bOw, make_fin(r, h, c, ob, psOs, bOs, psOw, bOw))
            run_pipeline(stages5, 3)
            P.barrier()
            A.release()
            dbg(cx, 7)
        A.mark()
        wo = A.alloc([8, 1024], BF16)
        bwo = P.buf("wo")
        P.dma("pool", [(wo, cx.inp["nsa_w_o"][j].rearrange("(kc p) c -> p kc c", p=128))], writes=[bwo])
        oT = [A.alloc([8, 128], BF16) for _ in range(2)]
        boT = P.bufs(2, "oT")
        hr = [A.alloc([D], F32) for _ in range(2)]
        bhr = P.bufs(2, "hr")
        for qb in range(16):
            out_proj_tile(cx, o_all[:, qb, :], bo_all[qb], wo, bwo, src, dst, tok0 + qb * 128, oT, boT, hr, bhr, qb % 2,
                          psT, bT, ps[0:2], pb[0:2])
        P.barrier()
        A.release()
    A.release()


def final_norm_phase(cx, src, dst):
    P, A = cx.P, cx.A
    A.mark()
    gain, bgain = load_gain(cx, cx.inp["final_norm"], D)
    hn = [A.alloc([D], F32) for _ in range(2)]
    bhn = P.bufs(2, "fhn")
    yo = [A.alloc([D], F32) for _ in range(2)]
    byo = P.bufs(2, "fyo")
    junk = A.alloc([D], BF16)
    bjunk = P.buf("junk")
    ss = A.alloc([8], F32)
    bss = P.bufs(8, "ss")
    for i in range(T // 128):
        sl = i % 2
        c = i % 8
        P.dma("sp", [(hn[sl], src[i * 128:(i + 1) * 128, :])], writes=[bhn[sl]], key="fhn%d" % sl)
        rms_rstd(cx, hn[sl], bhn[sl], D, ss, bss[c], c, junk, bjunk)
        P.op("dve", lambda e, sl=sl, c=c: e.scalar_tensor_tensor(out=yo[sl], in0=hn[sl], scalar=ss[:, c:c + 1], in1=gain,
                                                                 op0=ALU.mult, op1=ALU.mult),
             reads=[bhn[sl], bss[c], bgain], writes=[byo[sl]])
        P.dma("sp", [(dst[i * 128:(i + 1) * 128, :], yo[sl])], reads=[byo[sl]], key="fyo%d" % sl)
    P.barrier()
    A.release()


INPUT_SHAPES = {
    "ffn_norm_a": (DEPTH, D), "ffn_a_w_gate": (DEPTH, D, FF), "ffn_a_w_up": (DEPTH, D, FF), "ffn_a_w_down": (DEPTH, FF, D),
    "mix_norm": (DEPTH, D), "ffn_norm_b": (DEPTH, D), "ffn_b_w_gate": (DEPTH, D, FF), "ffn_b_w_up": (DEPTH, D, FF),
    "ffn_b_w_down": (DEPTH, FF, D), "final_norm": (D,), "rel_bias": (32, 16),
    "mla_w_in": (2, D, 672), "mla_q_norm": (2, 384), "mla_kv_norm": (2, 256), "mla_w_uq": (2, 384, 1536),
    "mla_w_ukv": (2, 256, 2048), "mla_w_o": (2, 1024, 1024),
    "nsa_w_in": (2, D, 2608), "nsa_cmp_pos_k": (2, 32, 64), "nsa_cmp_w1_k": (2, 2048, 128), "nsa_cmp_w2_k": (2, 128, 64),
    "nsa_cmp_pos_v": (2, 32, 64), "nsa_cmp_w1_v": (2, 2048, 128), "nsa_cmp_w2_v": (2, 128, 64), "nsa_w_o": (2, 1024, 1024),
}
ARENA_WORDS = 50944


def host_consts():
    c = {}
    c["c_ident"] = np.eye(128, dtype=np.float32).astype(ml_dtypes.bfloat16)
    half = 16
    inv = (np.float32(10000.0) ** (-np.arange(half, dtype=np.float32) * np.float32(2.0) / np.float32(32))).astype(np.float32)
    ang = np.arange(S, dtype=np.float32)[:, None] * inv[None, :]
    cos, sin = np.cos(ang).astype(np.float32), np.sin(ang).astype(np.float32)
    c["c_rope_cos"] = np.ascontiguousarray(np.concatenate([cos.T, cos.T], axis=0))
    c["c_rope_sin"] = np.ascontiguousarray(np.concatenate([sin.T, sin.T], axis=0))
    k = np.arange(128)[:, None]
    t = np.arange(128)[None, :]
    c["c_mdiag"] = np.where(t >= k, 0.0, NEGM).astype(np.float32)
    c["c_m4"] = np.where(t < k, 0.0, NEGM).astype(np.float32)
    def bucket(dist):
        n = np.maximum(dist, 0)
        nf = np.maximum(n, 1).astype(np.float32)
        large = 16 + (np.log(nf / np.float32(16)) / np.float32(math.log(8.0)) * np.float32(16)).astype(np.int32)
        return np.where(n < 16, n, np.minimum(large, 31))
    i = np.arange(4096)
    dist = 2047 - i
    oh = np.zeros((33, 4096), np.float32)
    bk = bucket(dist)
    oh[bk[dist >= 0], i[dist >= 0]] = 1.0
    oh[32, i[dist < 0]] = 1.0
    c["c_oh"] = oh
    kk = np.arange(S)
    c["c_bexp"] = (kk[None, :] // 64 == np.arange(32)[:, None]).astype(np.float32).astype(ml_dtypes.bfloat16)
    cs_ = np.arange(127) * 16
    ce_ = cs_ + 32
    ss_ = np.arange(32) * 64
    se_ = ss_ + 64
    ov = np.minimum(ce_[:, None], se_[None, :]) - np.maximum(cs_[:, None], ss_[None, :])
    c["c_ov"] = (np.clip(ov, 0, None) / 32.0).astype(np.float32).astype(ml_dtypes.bfloat16)
    tl = np.arange(128)[:, None, None]
    qb_ = np.arange(16)[None, :, None]
    jj = np.arange(32)[None, None, :]
    blk = (qb_ * 128 + tl) // 64
    forced = (jj == 0) | (jj == blk) | (jj == blk - 1)
    causal = jj <= blk
    c["c_cm"] = (causal & ~forced).astype(np.float32)
    c["c_add"] = np.where(forced, 1e6, np.where(causal, 0.0, -1e6)).astype(np.float32)
    return c


def default_phases():
    ph = []
    for i in range(DEPTH):
        ph.append(("ffn", i, "a"))
        ph.append(("mla", i // 2) if i % 2 == 0 else ("nsa", i // 2))
        ph.append(("ffn", i, "b"))
    ph.append(("final",))
    return ph


def build_program(phases, dbg_stop=None):
    nc = bass.Bass("TRN2", target_bir_lowering=False)
    cx = Ctx()
    cx.dbg_stop = dbg_stop
    cx.nc = nc
    cx.inp = {}
    cx.cd = {}
    x_in = nc.dram_tensor("x", [T, D], F32, kind="ExternalInput").ap()
    for name, shp in INPUT_SHAPES.items():
        cx.inp[name] = nc.dram_tensor(name, list(shp), F32, kind="ExternalInput").ap()
    consts = host_consts()
    cd = cx.cd
    for name, arr in consts.items():
        dt = BF16 if arr.dtype == ml_dtypes.bfloat16 else F32
        cd[name] = nc.dram_tensor(name, list(arr.shape), dt, kind="ExternalInput").ap()
    y = nc.dram_tensor("y", [T, D], F32, kind="ExternalOutput").ap()
    hbuf = nc.dram_tensor("hbuf", [T, D], F32).ap()
    with ExitStack() as es:
        big = es.enter_context(nc.sbuf_tensor("big", [128, ARENA_WORDS], F32))
        cx.ps = [es.enter_context(nc.psum_tensor("ps%d" % i, [128, 512], F32))[:, :] for i in range(8)]
        P = Prog(nc, es)
        cx.P = P
        cx.pb = P.bufs(8, "psb")
        cx.A = Arena(big, ARENA_WORDS)
        cx.ident = cx.A.alloc([128], BF16)
        cx.bident = P.buf("ident")
        P.dma("sp", [(cx.ident, cd["c_ident"])], writes=[cx.bident])
        cur = x_in
        try:
          for ph in phases:
            if ph[0] == "ffn":
                ffn_phase(cx, ph[1], ph[2], cur, hbuf)
                cur = hbuf
            elif ph[0] == "mla":
                mla_phase(cx, ph[1], 2 * ph[1], cur, hbuf)
                cur = hbuf
            elif ph[0] == "nsa":
                if not getattr(cx, "nsa_ready", False):
                    nsa_setup(cx)
                    cx.nsa_ready = True
                nsa_phase(cx, ph[1], 2 * ph[1] + 1, cur, hbuf)
                cur = hbuf
            elif ph[0] == "final":
                final_norm_phase(cx, cur, y)
                cur = y
            else:
                raise NotImplementedError(ph)
        except StopBuild:
            cur = x_in
        if cur is not y:
            P.dma("sp", [(y, cur)], key="ycopy")
            P.barrier()
        P.run_block()
    cx.consts = consts
    return nc, cx


_CACHE = {}


def kernel(**inputs):
    phases = default_phases()
    key = "full"
    if key not in _CACHE:
        _CACHE[key] = build_program(phases)
    nc, cx = _CACHE[key]
    x = np.ascontiguousarray(np.asarray(inputs["x"], dtype=np.float32)).reshape(N_CORES, T, D)
    shared = {k: np.ascontiguousarray(np.asarray(inputs[k], dtype=np.float32)) for k in INPUT_SHAPES}
    shared.update(cx.consts)
    in_maps = []
    for c in range(N_CORES):
        m = dict(shared)
        m["x"] = x[c]
        in_maps.append(m)
    res = run_bass_kernel_spmd(nc, in_maps, core_ids=list(range(N_CORES)))
    out = np.stack([np.asarray(r["y"], dtype=np.float32) for r in res.results], axis=0)
    return out.reshape(16, S, D)
```

```python
import math
from contextlib import ExitStack

import numpy as np
import ml_dtypes

import concourse.bass as bass
import concourse.mybir as mybir
from concourse.bass_utils import run_bass_kernel_spmd

F32 = mybir.dt.float32
BF16 = mybir.dt.bfloat16
AF = mybir.ActivationFunctionType
ALU = mybir.AluOpType

N_CORES = 8
D = 1024
S = 2048
NSEQ = 2
T = NSEQ * S
FF = 2816
NHC = FF // 128
DEPTH = 4
EPS = 1e-6
NEGM = -16384.0

ENGS = ["pe", "act", "dve", "pool", "sp"]
SELF_SYNC = {"pe": False, "act": True, "dve": True, "pool": True, "sp": False}


class Buf:
    __slots__ = ("name", "base", "w", "r")

    def __init__(self, name, base):
        self.name = name
        self.base = base
        self.w = []
        self.r = []


class Prog:
    def __init__(self, nc, es):
        self.nc = nc
        self.es = es
        self.q = {e: [] for e in ENGS}
        self.cnt = {e: 0 for e in ENGS}
        self.sems = {e: es.enter_context(nc.semaphore("s_" + e)) for e in ENGS if e != "sp"}
        self.dcnt = {}
        self.nbuf = 0
        self.ninst = 0

    def buf(self, name=None):
        self.nbuf += 1
        return Buf("%s_%d" % (name or "b", self.nbuf), name or "b")

    def bufs(self, n, name="b"):
        return [self.buf("%s%d" % (name, i)) for i in range(n)]

    def _deps(self, reads, writes):
        waits = []
        for b in reads:
            waits += b.w
        for b in writes:
            waits += b.w
            waits += b.r
        return waits

    def _commit(self, ev, reads, writes):
        for b in reads:
            b.r.append(ev)
        for b in writes:
            b.w = [ev]
            b.r = []

    def op(self, eng, fn, reads=(), writes=(), inc=True):
        waits = self._deps(reads, writes)
        idx = self.cnt[eng] + 1
        if inc:
            self.cnt[eng] = idx
        self._commit((eng, idx), reads, writes)
        self.q[eng].append((fn, waits, (eng, 1) if inc else None))
        self.ninst += 1

    def dma(self, eng, pairs, reads=(), writes=(), key=None, **kw):
        key = "d_" + (key or writes[0].base)
        if key not in self.sems:
            self.sems[key] = self.es.enter_context(self.nc.semaphore(key))
            self.dcnt[key] = 0
        waits = self._deps(reads, writes)
        self.dcnt[key] += 16 * len(pairs)
        self._commit((key, self.dcnt[key]), reads, writes)
        for i, (o, a) in enumerate(pairs):
            self.q[eng].append(((lambda e, o=o, a=a: e.dma_start(out=o, in_=a, **kw)),
                                waits if i == 0 else [], (key, 16)))
            self.ninst += 1

    def barrier(self):
        evs = [(e, self.cnt[e]) for e in ENGS if e != "sp" and self.cnt[e] > 0]
        evs += [(k, v) for k, v in self.dcnt.items() if v > 0]
        for e in ENGS:
            self.q[e].append((None, list(evs), None))

    def replay(self, eng, e):
        waited = {}
        for fn, waits, inc in self.q[eng]:
            need = {}
            for k, v in waits:
                if k == eng and not SELF_SYNC[eng]:
                    continue
                if v > need.get(k, 0):
                    need[k] = v
            for k, v in need.items():
                if waited.get(k, 0) < v:
                    e.wait_ge(self.sems[k], v)
                    waited[k] = v
            if fn is not None:
                ins = fn(e)
                if inc is not None:
                    ins.then_inc(self.sems[inc[0]], inc[1])

    def run_block(self):
        for e in ENGS:
            assert self.cnt[e] < 65000, (e, self.cnt[e])
        with self.nc.Block() as block:
            @block.sync
            def _(e):
                self.replay("sp", e)

            @block.tensor
            def _(e):
                self.replay("pe", e)

            @block.scalar
            def _(e):
                self.replay("act", e)

            @block.vector
            def _(e):
                self.replay("dve", e)

            @block.gpsimd
            def _(e):
                self.replay("pool", e)


class Arena:
    def __init__(self, big, words):
        self.big = big
        self.words = words
        self.top = 0
        self.marks = []
        self.peak = 0

    def alloc(self, shape_free, dtype, parts=128):
        n = int(np.prod(shape_free))
        nb = 2 if dtype == BF16 else 4
        w = (n * nb + 3) // 4
        w = (w + 7) // 8 * 8
        assert self.top + w <= self.words, ("SBUF arena overflow", self.top, w, self.words)
        ap = self.big[0:parts, self.top:self.top + w]
        self.top += w
        self.peak = max(self.peak, self.top)
        if dtype != F32:
            ap = ap.bitcast(dtype)
        ap = ap[:, 0:n]
        if len(shape_free) > 1:
            names = " ".join("d%d" % i for i in range(len(shape_free)))
            kw = {"d%d" % i: int(s) for i, s in enumerate(shape_free)}
            ap = ap.rearrange("p (%s) -> p %s" % (names, names), **kw)
        return ap

    def mark(self):
        self.marks.append(self.top)

    def release(self):
        self.top = self.marks.pop()


class Ctx:
    dbg_stop = None


class StopBuild(Exception):
    pass


def dbg(cx, level):
    if cx.dbg_stop is not None and cx.dbg_stop == level:
        cx.P.barrier()
        raise StopBuild()


def rms_rstd(cx, x_ap, bx, n, ss, bss, col, junk, bjunk):
    P = cx.P
    c = ss[:, col:col + 1]
    P.op("act", lambda e: e.activation(out=junk[:, 0:n], in_=x_ap, func=AF.Square, accum_out=c),
         reads=[bx], writes=[bjunk, bss])
    P.op("dve", lambda e: e.tensor_scalar(out=c, in0=c, scalar1=1.0 / n, scalar2=EPS, op0=ALU.mult, op1=ALU.add),
         reads=[bss], writes=[bss])
    P.op("act", lambda e: e.activation(out=c, in_=c, func=AF.Ln), reads=[bss], writes=[bss])
    P.op("act", lambda e: e.activation(out=c, in_=c, func=AF.Exp, scale=-0.5), reads=[bss], writes=[bss])


def load_gain(cx, vec_ap, n, name="gain"):
    P, A = cx.P, cx.A
    g = A.alloc([n], F32)
    bg = P.buf(name)
    P.dma("sp", [(g, vec_ap.partition_broadcast(128))], writes=[bg])
    return g, bg


def ffn_phase(cx, li, which, src, dst):
    P, A, ps, pb = cx.P, cx.A, cx.ps, cx.pb
    A.mark()
    Wg_d = cx.inp["ffn_%s_w_gate" % which][li]
    Wu_d = cx.inp["ffn_%s_w_up" % which][li]
    Wd_d = cx.inp["ffn_%s_w_down" % which][li]
    gn_d = cx.inp["ffn_norm_%s" % which][li]
    wg = A.alloc([8, FF], BF16)
    wu = A.alloc([8, FF], BF16)
    wd = A.alloc([NHC, D], BF16)
    CG = [(0, 768), (768, 1536), (1536, 2304), (2304, 2816)]
    bwg, bwu, bwd = P.bufs(4, "wg"), P.bufs(4, "wu"), P.bufs(4, "wd")
    for gi, (c0, c1) in enumerate(CG):
        P.dma("pool", [(wg[:, :, c0:c1], Wg_d[:, c0:c1].rearrange("(kc p) c -> p kc c", p=128))], writes=[bwg[gi]],
              key="wg%d" % gi)
        P.dma("pool", [(wu[:, :, c0:c1], Wu_d[:, c0:c1].rearrange("(kc p) c -> p kc c", p=128))], writes=[bwu[gi]],
              key="wu%d" % gi)
    for gi, (c0, c1) in enumerate(CG):
        P.dma("pool", [(wd[:, c0 // 128:c1 // 128, :], Wd_d[c0:c1, :].rearrange("(hc p) c -> p hc c", p=128))],
              writes=[bwd[gi]], key="wd%d" % gi)
    gain, bgain = load_gain(cx, gn_d, D)
    hn = [A.alloc([D], F32) for _ in range(2)]
    bhn = P.bufs(2, "hn")
    junk = A.alloc([D], BF16)
    bjunk = P.buf("junk")
    ss = A.alloc([8], F32)
    bss = P.bufs(8, "ss")
    xn = A.alloc([4, D], BF16)
    bxn = P.bufs(4, "xn")
    xT = A.alloc([8, 512], BF16)
    bxT = P.buf("xT")
    sg = [A.alloc([512], BF16) for _ in range(2)]
    bsg = P.bufs(2, "sg")
    hT = A.alloc([NHC, 512], BF16)
    bhT = P.bufs(NHC, "hT")
    hr = [A.alloc([D], F32) for _ in range(2)]
    bhr = P.bufs(2, "hr")
    psA, psB, psD, psT = ps[0:2], ps[2:4], ps[4:6], ps[6]
    bA, bB, bD, bT = pb[0:2], pb[2:4], pb[4:6], pb[6]
    ident = cx.ident
    cnt = {"n": 0, "g": 0, "d": 0}

    def norm_a(tt):
        for j in range(4):
            n = cnt["n"]
            cnt["n"] += 1
            sl = n % 2
            r0 = tt * 512 + j * 128
            P.dma("sp", [(hn[sl], src[r0:r0 + 128, :])], writes=[bhn[sl]], key="hn%d" % sl)
            c = n % 8
            rms_rstd(cx, hn[sl], bhn[sl], D, ss, bss[c], c, junk, bjunk)
            P.op("dve", lambda e, sl=sl, c=c, j=j: e.scalar_tensor_tensor(
                out=xn[:, j, :], in0=hn[sl], scalar=ss[:, c:c + 1], in1=gain, op0=ALU.mult, op1=ALU.mult),
                 reads=[bhn[sl], bss[c], bgain], writes=[bxn[j]])

    def norm_b(tt):
        pst = psT.bitcast(BF16)
        for j in range(4):
            for kc in range(8):
                P.op("pe", lambda e, j=j, kc=kc: e.transpose(out=pst[:, kc * 128:(kc + 1) * 128],
                                                               in_=xn[:, j, kc * 128:(kc + 1) * 128], identity=ident),
                     reads=[bxn[j], cx.bident], writes=[bT], inc=(kc == 7))
            P.op("act", lambda e, j=j: e.copy(out=xT[:, :, j * 128:(j + 1) * 128],
                                              in_=pst.rearrange("p (k t) -> p k t", k=8)),
                 reads=[bT], writes=[bxT])

    def gateup(tt):
        for hc in range(NHC):
            g = cnt["g"]
            cnt["g"] += 1
            s2 = g % 2
            gi = min(hc // 6, 3)
            for kc in range(8):
                P.op("pe", lambda e, hc=hc, kc=kc, s2=s2: e.matmul(psA[s2], lhsT=wg[:, kc, hc * 128:(hc + 1) * 128],
                                                                    rhs=xT[:, kc, :], start=(kc == 0), stop=(kc == 7)),
                     reads=[bxT, bwg[gi]], writes=[bA[s2]], inc=(kc == 7))
            for kc in range(8):
                P.op("pe", lambda e, hc=hc, kc=kc, s2=s2: e.matmul(psB[s2], lhsT=wu[:, kc, hc * 128:(hc + 1) * 128],
                                                                    rhs=xT[:, kc, :], start=(kc == 0), stop=(kc == 7)),
                     reads=[bxT, bwu[gi]], writes=[bB[s2]], inc=(kc == 7))
            P.op("act", lambda e, s2=s2: e.activation(out=sg[s2], in_=psA[s2], func=AF.Silu),
                 reads=[bA[s2]], writes=[bsg[s2]])
            P.op("dve", lambda e, s2=s2, hc=hc: e.tensor_tensor(out=hT[:, hc, :], in0=sg[s2], in1=psB[s2], op=ALU.mult),
                 reads=[bsg[s2], bB[s2]], writes=[bhT[hc]])

    def down(tt):
        for j in range(4):
            r0 = tt * 512 + j * 128
            sl = (tt * 4 + j) % 2
            P.dma("sp", [(hr[sl], src[r0:r0 + 128, :])], writes=[bhr[sl]], key="hr%d" % sl)
            for half in range(2):
                d = cnt["d"]
                cnt["d"] += 1
                s2 = d % 2
                for hc in range(NHC):
                    gi = min(hc // 6, 3)
                    P.op("pe", lambda e, hc=hc, j=j, half=half, s2=s2: e.matmul(
                        psD[s2], lhsT=hT[:, hc, j * 128:(j + 1) * 128], rhs=wd[:, hc, half * 512:(half + 1) * 512],
                        start=(hc == 0), stop=(hc == NHC - 1)),
                         reads=[bhT[hc], bwd[gi]], writes=[bD[s2]], inc=(hc == NHC - 1))
                P.op("dve", lambda e, sl=sl, half=half, s2=s2: e.scalar_tensor_tensor(
                    out=hr[sl][:, half * 512:(half + 1) * 512], in0=psD[s2], scalar=0.5,
                    in1=hr[sl][:, half * 512:(half + 1) * 512], op0=ALU.mult, op1=ALU.add),
                     reads=[bD[s2], bhr[sl]], writes=[bhr[sl]])
            P.dma("sp", [(dst[r0:r0 + 128, :], hr[sl])], reads=[bhr[sl]], key="hrst%d" % sl)

    NT = T // 512
    norm_a(0)
    norm_b(0)
    for tt in range(NT):
        gateup(tt)
        if tt + 1 < NT:
            norm_a(tt + 1)
        down(tt)
        if tt + 1 < NT:
            norm_b(tt + 1)
    P.barrier()
    A.release()


def run_pipeline(stages, L):
    n = len(stages)
    for i in range(n + L):
        if i < n:
            stages[i][0]()
        if i >= L:
            stages[i - L][1]()


def norm_T_tile(cx, src, r0, gain, bgain, hn, bhn, sl, ss, bss, c, junk, bjunk, xn, bxn, psT, bT, dstT, bdstT, col0,
                part=None):
    P = cx.P
    if part in (None, 0):
        norm_T_front(cx, src, r0, gain, bgain, hn, bhn, sl, ss, bss, c, junk, bjunk, xn, bxn)
    if part in (None, 1):
        norm_T_back(cx, sl, xn, bxn, psT, bT, dstT, bdstT, col0)


def norm_T_front(cx, src, r0, gain, bgain, hn, bhn, sl, ss, bss, c, junk, bjunk, xn, bxn):
    P = cx.P
    P.dma("sp", [(hn[sl], src[r0:r0 + 128, :])], writes=[bhn[sl]], key="mhn%d" % sl)
    rms_rstd(cx, hn[sl], bhn[sl], D, ss, bss[c], c, junk, bjunk)
    P.op("dve", lambda e: e.scalar_tensor_tensor(out=xn[sl], in0=hn[sl], scalar=ss[:, c:c + 1], in1=gain,
                                                 op0=ALU.mult, op1=ALU.mult),
         reads=[bhn[sl], bss[c], bgain], writes=[bxn[sl]])


def norm_T_back(cx, sl, xn, bxn, psT, bT, dstT, bdstT, col0):
    P = cx.P
    pst = psT.bitcast(BF16)
    for kc in range(8):
        P.op("pe", lambda e, kc=kc: e.transpose(out=pst[:, kc * 128:(kc + 1) * 128], in_=xn[sl][:, kc * 128:(kc + 1) * 128],
                                                identity=cx.ident),
             reads=[bxn[sl], cx.bident], writes=[bT], inc=(kc == 7))
    P.op("act", lambda e: e.copy(out=dstT[:, 0:8, col0:col0 + 128], in_=pst.rearrange("p (k t) -> p k t", k=8)),
         reads=[bT], writes=[bdstT])


def out_proj_tile(cx, o_tile, bo, wo, bwo, src, dst, r0, oT, boT, hr, bhr, sl, psT, bT, psW, bW):
    P = cx.P
    pst = psT.bitcast(BF16)
    for kc in range(8):
        P.op("pe", lambda e, kc=kc: e.transpose(out=pst[:, kc * 128:(kc + 1) * 128], in_=o_tile[:, kc * 128:(kc + 1) * 128],
                                                identity=cx.ident),
             reads=[bo, cx.bident], writes=[bT], inc=(kc == 7))
    P.op("act", lambda e: e.copy(out=oT[sl], in_=pst.rearrange("p (k t) -> p k t", k=8)), reads=[bT], writes=[boT[sl]])
    P.dma("sp", [(hr[sl], src[r0:r0 + 128, :])], writes=[bhr[sl]], key="mhr%d" % sl)
    for half in range(2):
        for kc in range(8):
            P.op("pe", lambda e, kc=kc, half=half: e.matmul(psW[half], lhsT=oT[sl][:, kc, :],
                                                            rhs=wo[:, kc, half * 512:(half + 1) * 512],
                                                            start=(kc == 0), stop=(kc == 7)),
                 reads=[boT[sl], bwo], writes=[bW[half]], inc=(kc == 7))
        P.op("dve", lambda e, half=half: e.tensor_tensor(out=hr[sl][:, half * 512:(half + 1) * 512], in0=psW[half],
                                                         in1=hr[sl][:, half * 512:(half + 1) * 512], op=ALU.add),
             reads=[bW[half], bhr[sl]], writes=[bhr[sl]])
    P.dma("sp", [(dst[r0:r0 + 128, :], hr[sl])], reads=[bhr[sl]], key="mhrst%d" % sl)


def mla_phase(cx, j, li, src, dst):
    P, A, ps, pb = cx.P, cx.A, cx.ps, cx.pb
    A.mark()
    scale = 96.0 ** -0.5
    Win_d, Wuq_d, Wukv_d, Wo_d = cx.inp["mla_w_in"][j], cx.inp["mla_w_uq"][j], cx.inp["mla_w_ukv"][j], cx.inp["mla_w_o"][j]
    win = A.alloc([8, 672], BF16)
    wkr = A.alloc([8, 96], BF16)
    wq = A.alloc([3, 1536], BF16)
    wqs = A.alloc([3, 16, 96], BF16)
    wkn = A.alloc([2, 16, 64], BF16)
    wv = A.alloc([2, 16, 64], BF16)
    wo = A.alloc([8, 1024], BF16)
    bwin, bwkr, bwq, bwqs, bwkn, bwv, bwo = (P.buf(n) for n in ["win", "wkr", "wq", "wqs", "wkn", "wv", "wo"])
    P.dma("pool", [(win, Win_d.rearrange("(kc p) c -> p kc c", p=128))], writes=[bwin])
    w_in_r = Win_d.rearrange("(kc p) c -> p kc c", p=128)
    P.op("dve", lambda e: e.memset(wkr, 0.0), writes=[bwkr])
    P.dma("pool", [(wkr[:, :, 64:80], w_in_r[:, :, 656:672]), (wkr[:, :, 80:96], w_in_r[:, :, 640:656])], writes=[bwkr])
    P.op("act", lambda e: e.mul(out=wkr[:, :, 64:80], in_=wkr[:, :, 64:80], mul=-1.0), reads=[bwkr], writes=[bwkr])
    wuq_r = Wuq_d.rearrange("(kc p) (h d) -> p kc h d", p=128, d=96)
    P.dma("pool", [(wq, Wuq_d.rearrange("(kc p) c -> p kc c", p=128))], writes=[bwq])
    P.op("dve", lambda e: e.memset(wqs, 0.0), writes=[bwqs])
    for kc in range(3):
        P.dma("pool", [(wqs[:, kc, :, 64:80], wuq_r[:, kc, :, 80:96]), (wqs[:, kc, :, 80:96], wuq_r[:, kc, :, 64:80])],
              writes=[bwqs], key="wqs")
    P.op("act", lambda e: e.mul(out=wqs[:, :, :, 64:80], in_=wqs[:, :, :, 64:80], mul=-1.0), reads=[bwqs], writes=[bwqs])
    wukv_r = Wukv_d.rearrange("(kc p) (h d) -> p kc h d", p=128, d=128)
    for kc in range(2):
        P.dma("pool", [(wkn[:, kc, :, :], wukv_r[:, kc, :, 0:64])], writes=[bwkn], key="wkn")
        P.dma("pool", [(wv[:, kc, :, :], wukv_r[:, kc, :, 64:128])], writes=[bwv], key="wv")
    P.dma("pool", [(wo, Wo_d.rearrange("(kc p) c -> p kc c", p=128))], writes=[bwo])
    gain, bgain = load_gain(cx, cx.inp["mix_norm"][li], D)
    gq, bgq = load_gain(cx, cx.inp["mla_q_norm"][j], 384, "gainq")
    gkv, bgkv = load_gain(cx, cx.inp["mla_kv_norm"][j], 256, "gainkv")
    CC = A.alloc([S], F32)
    SS = A.alloc([S], F32)
    bcs = P.buf("cs")
    P.dma("sp", [(CC[64:96, :], cx.cd["c_rope_cos"]), (SS[64:96, :], cx.cd["c_rope_sin"])], writes=[bcs])
    mdiag = A.alloc([128], F32)
    bmd = P.buf("mdiag")
    P.dma("sp", [(mdiag, cx.cd["c_mdiag"])], writes=[bmd])
    junk = A.alloc([D], BF16)
    bjunk = P.buf("junk")
    ss = A.alloc([8], F32)
    bss = P.bufs(8, "ss")
    ssq = A.alloc([8], F32)
    bssq = P.bufs(8, "ssq")
    cqnT = A.alloc([3, S], BF16)
    ckvnT = A.alloc([2, S], BF16)
    bcqnT, bckvnT = P.buf("cqnT"), P.buf("ckvnT")
    kT = [A.alloc([S], BF16) for _ in range(2)]
    bkT = P.bufs(2, "kT")
    for b2 in range(2):
        P.op("dve", lambda e, b2=b2: e.memset(kT[b2][96:128, :], 0.0), writes=[bkT[b2]])
    o_all = A.alloc([16, D], BF16)
    bo_all = P.bufs(16, "o_all")
    psT, bT = ps[6], pb[6]

    for sq in range(NSEQ):
        tok0 = sq * S
        A.mark()
        hn = [A.alloc([D], F32) for _ in range(3)]
        bhn = P.bufs(3, "hn")
        xn = [A.alloc([D], BF16) for _ in range(3)]
        bxn = P.bufs(3, "xn")
        mT = A.alloc([8, 512], BF16)
        bmT = P.buf("mT")
        cqn = [A.alloc([384], BF16) for _ in range(2)]
        ckvn = [A.alloc([256], BF16) for _ in range(2)]
        bcqn, bckvn = P.bufs(2, "cqn"), P.bufs(2, "ckvn")
        tmpa = A.alloc([512], F32)
        tmpb = A.alloc([512], F32)
        btmpa, btmpb = P.buf("tmpa"), P.buf("tmpb")
        n = 0
        for c in range(4):
            stA = []
            for jj in range(4):
                def frontA(jj=jj, n=n, c=c):
                    norm_T_tile(cx, src, tok0 + c * 512 + jj * 128, gain, bgain, hn, bhn, n % 3, ss, bss, n % 8, junk, bjunk,
                                xn, bxn, psT, bT, mT, bmT, jj * 128, part=0)

                def backA(jj=jj, n=n, c=c):
                    norm_T_tile(cx, src, tok0 + c * 512 + jj * 128, gain, bgain, hn, bhn, n % 3, ss, bss, n % 8, junk, bjunk,
                                xn, bxn, psT, bT, mT, bmT, jj * 128, part=1)
                stA.append((frontA, backA))
                n += 1
            run_pipeline(stA, 2)
            for kc in range(8):
                P.op("pe", lambda e, kc=kc: e.matmul(ps[4][0:96, :], lhsT=win[:, kc, 576:672], rhs=mT[:, kc, :],
                                                     start=(kc == 0), stop=(kc == 7)),
                     reads=[bwin, bmT], writes=[pb[4]], inc=(kc == 7))
            for kc in range(8):
                P.op("pe", lambda e, kc=kc: e.matmul(ps[5][0:96, :], lhsT=wkr[:, kc, :], rhs=mT[:, kc, :],
                                                     start=(kc == 0), stop=(kc == 7)),
                     reads=[bwkr, bmT], writes=[pb[5]], inc=(kc == 7))
            cs = slice(c * 512, (c + 1) * 512)
            P.op("dve", lambda e, cs=cs: e.tensor_tensor(out=tmpa[64:96, :], in0=ps[4][64:96, :], in1=CC[64:96, cs], op=ALU.mult),
                 reads=[pb[4], bcs], writes=[btmpa])
            P.op("dve", lambda e, cs=cs: e.tensor_tensor(out=tmpb[64:96, :], in0=ps[5][64:96, :], in1=SS[64:96, cs], op=ALU.mult),
                 reads=[pb[5], bcs], writes=[btmpb])
            for b2 in range(2):
                P.op("dve", lambda e, cs=cs, b2=b2: e.tensor_tensor(out=kT[b2][64:96, cs], in0=tmpa[64:96, :], in1=tmpb[64:96, :],
                                                                    op=ALU.add),
                     reads=[btmpa, btmpb], writes=[bkT[b2]])
            for jj in range(4):
                tsl = slice(jj * 128, (jj + 1) * 128)
                s2 = jj % 2
                for kc in range(8):
                    P.op("pe", lambda e, kc=kc, tsl=tsl, s2=s2: e.matmul(ps[s2][:, 0:384], lhsT=mT[:, kc, tsl],
                                                                        rhs=win[:, kc, 0:384], start=(kc == 0), stop=(kc == 7)),
                         reads=[bwin, bmT], writes=[pb[s2]], inc=(kc == 7))
                for kc in range(8):
                    P.op("pe", lambda e, kc=kc, tsl=tsl, s2=s2: e.matmul(ps[2 + s2][:, 0:256], lhsT=mT[:, kc, tsl],
                                                                        rhs=win[:, kc, 384:640], start=(kc == 0), stop=(kc == 7)),
                         reads=[bwin, bmT], writes=[pb[2 + s2]], inc=(kc == 7))
                cq = (c * 4 + jj) % 8
                rms_rstd(cx, ps[s2][:, 0:384], pb[s2], 384, ssq, bssq[cq], cq, junk, bjunk)
                P.op("dve", lambda e, s2=s2, cq=cq: e.scalar_tensor_tensor(out=cqn[s2], in0=ps[s2][:, 0:384],
                                                                           scalar=ssq[:, cq:cq + 1], in1=gq,
                                                                           op0=ALU.mult, op1=ALU.mult),
                     reads=[pb[s2], bssq[cq], bgq], writes=[bcqn[s2]])
                rms_rstd(cx, ps[2 + s2][:, 0:256], pb[2 + s2], 256, ss, bss[cq], cq, junk, bjunk)
                P.op("dve", lambda e, s2=s2, cq=cq: e.scalar_tensor_tensor(out=ckvn[s2], in0=ps[2 + s2][:, 0:256],
                                                                           scalar=ss[:, cq:cq + 1], in1=gkv,
                                                                           op0=ALU.mult, op1=ALU.mult),
                     reads=[pb[2 + s2], bss[cq], bgkv], writes=[bckvn[s2]])
                pst = psT.bitcast(BF16)
                for kc in range(3):
                    P.op("pe", lambda e, kc=kc, s2=s2: e.transpose(out=pst[:, kc * 128:(kc + 1) * 128],
                                                                   in_=cqn[s2][:, kc * 128:(kc + 1) * 128], identity=cx.ident),
                         reads=[bcqn[s2], cx.bident], writes=[bT], inc=False)
                for kc in range(2):
                    P.op("pe", lambda e, kc=kc, s2=s2: e.transpose(out=pst[:, (3 + kc) * 128:(4 + kc) * 128],
                                                                   in_=ckvn[s2][:, kc * 128:(kc + 1) * 128], identity=cx.ident),
                         reads=[bckvn[s2], cx.bident], writes=[bT], inc=(kc == 1))
                g0 = c * 512 + jj * 128
                P.op("act", lambda e, g0=g0: e.copy(out=cqnT[:, :, g0:g0 + 128],
                                                    in_=pst[:, 0:384].rearrange("p (k t) -> p k t", k=3)),
                     reads=[bT], writes=[bcqnT])
                P.op("act", lambda e, g0=g0: e.copy(out=ckvnT[:, :, g0:g0 + 128],
                                                    in_=pst[:, 384:640].rearrange("p (k t) -> p k t", k=2)),
                     reads=[bT], writes=[bckvnT])
        P.barrier()
        A.release()
        A.mark()
        qT = [A.alloc([S], BF16) for _ in range(2)]
        bqT = P.bufs(2, "qT")
        for b2 in range(2):
            P.op("dve", lambda e, b2=b2: e.memset(qT[b2][96:128, :], 0.0), writes=[bqT[b2]])
        vaug = [A.alloc([16, 65], BF16) for _ in range(2)]
        bva = P.bufs(2, "vaug")
        for b2 in range(2):
            P.op("dve", lambda e, b2=b2: e.memset(vaug[b2][:, :, 64:65], 1.0), writes=[bva[b2]])
        E = [A.alloc([512], BF16) for _ in range(6)]
        bE = P.bufs(6, "E")
        tmpd = [A.alloc([128], F32) for _ in range(2)]
        btd = P.bufs(2, "tmpd")
        tq1 = A.alloc([512], F32)
        tq2 = A.alloc([512], F32)
        btq1, btq2 = P.buf("tq1"), P.buf("tq2")
        rec = A.alloc([8], F32)
        brec = P.bufs(2, "rec")
        nS = 0
        nE = 0
        nD = 0
        def proj(h):
            hb = h % 2
            for c in range(4):
                cs = slice(c * 512, (c + 1) * 512)
                for kc in range(3):
                    P.op("pe", lambda e, kc=kc, cs=cs, h=h: e.matmul(ps[2][0:96, :], lhsT=wq[:, kc, h * 96:(h + 1) * 96], rhs=cqnT[:, kc, cs],
                                                                    start=(kc == 0), stop=(kc == 2)),
                         reads=[bwq, bcqnT], writes=[pb[2]], inc=(kc == 2))
                for kc in range(3):
                    P.op("pe", lambda e, kc=kc, cs=cs, h=h: e.matmul(ps[3][0:96, :], lhsT=wqs[:, kc, h, :], rhs=cqnT[:, kc, cs],
                                                                    start=(kc == 0), stop=(kc == 2)),
                         reads=[bwqs, bcqnT], writes=[pb[3]], inc=(kc == 2))
                for kc in range(2):
                    P.op("pe", lambda e, kc=kc, cs=cs, h=h: e.matmul(ps[4][0:64, :], lhsT=wkn[:, kc, h, :], rhs=ckvnT[:, kc, cs],
                                                                    start=(kc == 0), stop=(kc == 1)),
                         reads=[bwkn, bckvnT], writes=[pb[4]], inc=(kc == 1))
                P.op("act", lambda e, cs=cs, hb=hb: e.copy(out=qT[hb][0:64, cs], in_=ps[2][0:64, :]),
                     reads=[pb[2]], writes=[bqT[hb]])
                P.op("dve", lambda e, cs=cs: e.tensor_tensor(out=tq1[64:96, :], in0=ps[2][64:96, :], in1=CC[64:96, cs], op=ALU.mult),
                     reads=[pb[2], bcs], writes=[btq1])
                P.op("dve", lambda e, cs=cs: e.tensor_tensor(out=tq2[64:96, :], in0=ps[3][64:96, :], in1=SS[64:96, cs], op=ALU.mult),
                     reads=[pb[3], bcs], writes=[btq2])
                P.op("dve", lambda e, cs=cs, hb=hb: e.tensor_tensor(out=qT[hb][64:96, cs], in0=tq1[64:96, :], in1=tq2[64:96, :],
                                                                    op=ALU.add),
                     reads=[btq1, btq2], writes=[bqT[hb]])
                P.op("act", lambda e, cs=cs, hb=hb: e.copy(out=kT[hb][0:64, cs], in_=ps[4][0:64, :]),
                     reads=[pb[4]], writes=[bkT[hb]])
            for g8 in range(2):
                for kk in range(8):
                    kb = g8 * 8 + kk
                    for kc in range(2):
                        P.op("pe", lambda e, kc=kc, kb=kb, kk=kk, h=h: e.matmul(
                            ps[3][:, kk * 64:(kk + 1) * 64], lhsT=ckvnT[:, kc, kb * 128:(kb + 1) * 128], rhs=wv[:, kc, h, :],
                            start=(kc == 0), stop=(kc == 1)),
                             reads=[bwv, bckvnT], writes=[pb[3]], inc=(kc == 1 and kk == 7))
                P.op("act", lambda e, g8=g8, hb=hb: e.copy(out=vaug[hb][:, g8 * 8:(g8 + 1) * 8, 0:64],
                                                           in_=ps[3].rearrange("p (k d) -> p k d", k=8)),
                     reads=[pb[3]], writes=[bva[hb]])

        import os as _os
        ILV = _os.environ.get("MLA_ILV", "1") == "1"
        if ILV:
            proj(0)
        for h in range(16):
            hb = h % 2
            if not ILV:
                proj(h)
            stages = []
            SB = [0, 1, 5]
            for c in range(4):
                ob = c % 2
                psO, bO = ps[6 + ob], pb[6 + ob]
                nkb = 4 * c + 4
                for kb in range(nkb):
                    qlo = max(kb, 4 * c)
                    ncol = (4 * c + 4 - qlo) * 128
                    sb = SB[nS % 3]
                    nS += 1
                    eb = nE % 6
                    nE += 1
                    diag = kb >= 4 * c
                    db = nD % 2
                    if diag:
                        nD += 1

                    def front(kb=kb, qlo=qlo, ncol=ncol, sb=sb, eb=eb, diag=diag, db=db, hb=hb):
                        P.op("pe", lambda e: e.matmul(ps[sb][:, 0:ncol], lhsT=kT[hb][:, kb * 128:(kb + 1) * 128],
                                                      rhs=qT[hb][:, qlo * 128:qlo * 128 + ncol], start=True, stop=True),
                             reads=[bkT[hb], bqT[hb]], writes=[pb[sb]])
                        c0 = 0
                        if diag:
                            P.op("dve", lambda e: e.tensor_tensor(out=tmpd[db], in0=ps[sb][:, 0:128], in1=mdiag, op=ALU.add),
                                 reads=[pb[sb], bmd], writes=[btd[db]])
                            P.op("act", lambda e: e.activation(out=E[eb][:, 0:128], in_=tmpd[db], func=AF.Exp, scale=scale),
                                 reads=[btd[db]], writes=[bE[eb]])
                            c0 = 128
                        if ncol > c0:
                            P.op("act", lambda e: e.activation(out=E[eb][:, c0:ncol], in_=ps[sb][:, c0:ncol], func=AF.Exp,
                                                               scale=scale), reads=[pb[sb]], writes=[bE[eb]])

                    def back(kb=kb, qlo=qlo, eb=eb, c=c, ob=ob, psO=psO, bO=bO, hb=hb, h=h, nkb=nkb):
                        nq = 4 * c + 4 - qlo
                        for qi in range(nq):
                            oi = qlo + qi - 4 * c
                            P.op("pe", lambda e, qi=qi, oi=oi: e.matmul(
                                psO[:, oi * 65:(oi + 1) * 65], lhsT=E[eb][:, qi * 128:(qi + 1) * 128], rhs=vaug[hb][:, kb, :],
                                start=(kb == 0 and qi == 0), stop=False, skip_group_check=True),
                                 reads=[bE[eb], bva[hb]], writes=[bO], inc=(qi == nq - 1))
                        if kb == nkb - 1:
                            rb = brec[ob]
                            P.op("dve", lambda e: e.reciprocal(out=rec[:, ob * 4:(ob + 1) * 4],
                                                               in_=psO[:, 0:260].rearrange("p (q d) -> p q d", d=65)[:, :, 64]),
                                 reads=[bO], writes=[rb])
                            for oi in range(4):
                                qb = 4 * c + oi
                                P.op("dve", lambda e, oi=oi, qb=qb: e.tensor_scalar(
                                    out=o_all[:, qb, h * 64:(h + 1) * 64], in0=psO[:, oi * 65:oi * 65 + 64],
                                    scalar1=rec[:, ob * 4 + oi:ob * 4 + oi + 1], scalar2=None, op0=ALU.mult),
                                     reads=[bO, rb], writes=[bo_all[qb]])

                    stages.append((front, back))
            if ILV and h + 1 < 16:
                mid = len(stages) // 2
                f0, b0 = stages[mid]
                stages[mid] = ((lambda f0=f0, h=h: (proj(h + 1), f0())), b0)
            run_pipeline(stages, 2)
        P.barrier()
        A.release()
        A.mark()
        oT = [A.alloc([8, 128], BF16) for _ in range(2)]
        boT = P.bufs(2, "oT")
        hr = [A.alloc([D], F32) for _ in range(2)]
        bhr = P.bufs(2, "hr")
        for qb in range(16):
            out_proj_tile(cx, o_all[:, qb, :], bo_all[qb], wo, bwo, src, dst, tok0 + qb * 128, oT, boT, hr, bhr, qb % 2,
                          psT, bT, ps[0:2], pb[0:2])
        P.barrier()
        A.release()
    A.release()


def rev_cols(ap, start, n):
    a = [list(x) for x in ap.ap]
    assert len(a) == 2 and a[1][0] == 1, a
    return bass.AP(ap.tensor, ap.offset + start, [a[0], [-1, n]])


def nsa_setup(cx):
    P, A, ps, pb = cx.P, cx.A, cx.ps, cx.pb
    nc = cx.nc
    cx.rtab_t = nc.dram_tensor("rtab", [17, 4096], F32)
    rtab = cx.rtab_t.ap()[0:16, :]
    A.mark()
    tbl = A.alloc([16], F32)
    btbl = P.buf("tbl")
    P.op("dve", lambda e: e.memset(tbl[0:64, :], NEGM), writes=[btbl])
    P.dma("sp", [(tbl[0:32, :], cx.inp["rel_bias"])], writes=[btbl])
    oh = A.alloc([4096], F32)
    boh = P.buf("oh")
    P.dma("sp", [(oh[0:33, :], cx.cd["c_oh"])], writes=[boh])
    rt = A.alloc([4096], F32)
    brt = P.buf("rt")
    for ch in range(8):
        b = ch % 2
        P.op("pe", lambda e, ch=ch, b=b: e.matmul(ps[b][0:16, :], lhsT=tbl[0:33, :], rhs=oh[0:33, ch * 512:(ch + 1) * 512],
                                                  start=True, stop=True),
             reads=[btbl, boh], writes=[pb[b]])
        P.op("act", lambda e, ch=ch, b=b: e.copy(out=rt[0:16, ch * 512:(ch + 1) * 512], in_=ps[b][0:16, :]),
             reads=[pb[b]], writes=[brt])
    P.dma("sp", [(rtab, rt[0:16, :])], reads=[brt], key="rtab_st")
    P.barrier()
    A.release()
    dbg(cx, 0)


def nsa_phase(cx, j, li, src, dst):
    P, A, ps, pb = cx.P, cx.A, cx.ps, cx.pb
    A.mark()
    scale = 0.125
    Win_d = cx.inp["nsa_w_in"][j]
    w_in_r = Win_d.rearrange("(kc p) c -> p kc c", p=128)
    rt = cx.rtab_t
    W1 = [A.alloc([32, 128], BF16) for _ in range(2)]
    bW1 = P.bufs(2, "W1")
    w2 = [A.alloc([64], BF16) for _ in range(2)]
    bw2 = P.bufs(2, "w2")
    for kv, nm in enumerate(["k", "v"]):
        P.dma("pool", [(W1[kv][0:64], cx.inp["nsa_cmp_w1_%s" % nm][j].rearrange("(l d) c -> d l c", d=64))], writes=[bW1[kv]])
        P.dma("pool", [(w2[kv], cx.inp["nsa_cmp_w2_%s" % nm][j])], writes=[bw2[kv]])
    posf = A.alloc([2, 32], F32)
    posb = A.alloc([2, 32], BF16)
    bposf, bposb = P.buf("posf"), P.buf("posb")
    P.dma("sp", [(posf[0:64, 0, :], cx.inp["nsa_cmp_pos_k"][j].rearrange("l d -> d l")),
                 (posf[0:64, 1, :], cx.inp["nsa_cmp_pos_v"][j].rearrange("l d -> d l"))], writes=[bposf],
          allow_slow_non_contiguous=True)
    P.op("act", lambda e: e.copy(out=posb[0:64], in_=posf[0:64]), reads=[bposf], writes=[bposb])
    cpos = A.alloc([2], F32)
    bcpos = P.buf("cpos")
    for kv in range(2):
        for l in range(32):
            P.op("pe", lambda e, kv=kv, l=l: e.matmul(ps[4 + kv][:, 0:1], lhsT=W1[kv][0:64, l, :], rhs=posb[0:64, kv, l:l + 1],
                                                      start=(l == 0), stop=(l == 31)),
                 reads=[bW1[kv], bposb], writes=[pb[4 + kv]], inc=(l == 31))
        P.op("act", lambda e, kv=kv: e.copy(out=cpos[:, kv:kv + 1], in_=ps[4 + kv][:, 0:1]), reads=[pb[4 + kv]], writes=[bcpos])
    gain, bgain = load_gain(cx, cx.inp["mix_norm"][li], D)
    c31 = A.alloc([16], F32)
    bc31 = P.buf("c31")
    P.dma("sp", [(c31, cx.inp["rel_bias"][31].partition_broadcast(128))], writes=[bc31])
    m4 = A.alloc([128], F32)
    bm4 = P.buf("m4")
    P.dma("sp", [(m4, cx.cd["c_m4"])], writes=[bm4])
    selc = A.alloc([2, 16, 32], F32)
    bselc = P.buf("selc")
    P.dma("sp", [(selc[:, 0], cx.cd["c_cm"]), (selc[:, 1], cx.cd["c_add"])], writes=[bselc])
    junk = A.alloc([D], BF16)
    bjunk = P.buf("junk")
    ss = A.alloc([8], F32)
    bss = P.bufs(8, "ss")
    psT, bT = ps[6], pb[6]
    dbg(cx, 1)

    for sq in range(NSEQ):
        tok0 = sq * S
        mT = A.alloc([8, S], BF16) if sq == 0 else mT
        gates = A.alloc([16, 48], F32) if sq == 0 else gates
        o_all = A.alloc([16, D], BF16) if sq == 0 else o_all
        if sq == 0:
            bmT, bgates = P.buf("mT"), P.bufs(16, "gates")
            bo_all = P.bufs(16, "o_all")
        A.mark()
        wgt = A.alloc([8, 48], BF16)
        bwgt = P.buf("wgt")
        P.dma("pool", [(wgt, w_in_r[:, :, 2560:2608])], writes=[bwgt])
        hn = [A.alloc([D], F32) for _ in range(3)]
        bhn = P.bufs(3, "hn")
        xn = [A.alloc([D], BF16) for _ in range(3)]
        bxn = P.bufs(3, "xn")
        stA = []
        for qb in range(16):
            def frontA(qb=qb):
                norm_T_tile(cx, src, tok0 + qb * 128, gain, bgain, hn, bhn, qb % 3, ss, bss, qb % 8, junk, bjunk,
                            xn, bxn, psT, bT, mT, bmT, qb * 128, part=0)

            def backA(qb=qb):
                norm_T_tile(cx, src, tok0 + qb * 128, gain, bgain, hn, bhn, qb % 3, ss, bss, qb % 8, junk, bjunk,
                            xn, bxn, psT, bT, mT, bmT, qb * 128, part=1)
                b = qb % 2
                for kc in range(8):
                    P.op("pe", lambda e, kc=kc: e.matmul(ps[b][:, 0:48], lhsT=mT[:, kc, qb * 128:(qb + 1) * 128],
                                                         rhs=wgt[:, kc, :], start=(kc == 0), stop=(kc == 7)),
                         reads=[bmT, bwgt], writes=[pb[b]], inc=(kc == 7))
                P.op("dve", lambda e: e.tensor_copy(out=gates[:, qb, :], in_=ps[b][:, 0:48]),
                     reads=[pb[b]], writes=[bgates[qb]])
            stA.append((frontA, backA))
        run_pipeline(stA, 1)
        P.op("act", lambda e: e.activation(out=gates, in_=gates, func=AF.Sigmoid), reads=bgates, writes=bgates)
        P.barrier()
        A.release()
        dbg(cx, 2)
        for g in range(4):
            A.mark()
            wg_ = A.alloc([8, 640], BF16)
            bwg_ = P.buf("wing")
            prs = [(wg_[:, :, 0:256], w_in_r[:, :, g * 256:(g + 1) * 256])]
            for i in range(6):
                prs.append((wg_[:, :, 256 + i * 64:320 + i * 64], w_in_r[:, :, 1024 + i * 256 + g * 64:1024 + i * 256 + (g + 1) * 64]))
            P.dma("pool", prs, writes=[bwg_])
            x01 = A.alloc([4, 256], F32)
            bx01 = P.buf("x01")
            P.dma("sp", [(x01, bass.AP(rt, (4 * g) * 4096 + 1792, [[1, 128], [4096, 4], [1, 256]]))], writes=[bx01])
            qa = [A.alloc([S], BF16) for _ in range(4)]
            bqa = P.bufs(4, "qa")
            ka = A.alloc([S], BF16)
            bka = P.buf("ka")
            P.dma("sp", [(ka[64:96, :], cx.cd["c_bexp"])], writes=[bka])
            P.op("dve", lambda e: e.memset(ka[96:128, :], 0.0), writes=[bka])
            for r_ in range(4):
                P.op("dve", lambda e, r_=r_: e.memset(qa[r_][96:128, :], 0.0), writes=[bqa[r_]])
            kw = A.alloc([S], BF16)
            kcf = A.alloc([S], BF16)
            vcf = A.alloc([S], BF16)
            bkw, bkcf, bvcf = P.buf("kw"), P.buf("kcf"), P.buf("vcf")
            vs = A.alloc([16, 65], BF16)
            vw = A.alloc([16, 65], BF16)
            bvs, bvw = P.buf("vs"), P.buf("vw")
            P.op("dve", lambda e: e.memset(vs[:, :, 64:65], 1.0), writes=[bvs])
            P.op("dve", lambda e: e.memset(vw[:, :, 64:65], 1.0), writes=[bvw])
            kcT = A.alloc([128], BF16)
            bkcT = P.buf("kcT")
            vcx = A.alloc([97], BF16)
            bvcx = P.buf("vcx")
            P.op("dve", lambda e: e.memset(vcx[:, 64:65], 1.0), writes=[bvcx])
            P.dma("sp", [(vcx[0:127, 65:97], cx.cd["c_ov"])], writes=[bvcx])
            ocmp = A.alloc([16, 4, 64], BF16)
            bocmp = P.bufs(16, "ocmp")
            imp = A.alloc([16, 32], F32)
            bimp = P.bufs(16, "imp")
            nst = A.alloc([S], BF16)
            bnst = P.buf("nst")
            pi = 0
            fm = [(qa[0], bqa[0], 0), (qa[1], bqa[1], 64), (qa[2], bqa[2], 128), (qa[3], bqa[3], 192),
                  (kcf, bkcf, 256), (vcf, bvcf, 320), (ka, bka, 384), (kw, bkw, 512)]
            for (dt_, bdt, c0) in fm:
                for c in range(4):
                    b = 4 + pi % 2
                    pi += 1
                    cs = slice(c * 512, (c + 1) * 512)
                    for kc in range(8):
                        P.op("pe", lambda e, kc=kc, cs=cs, c0=c0, b=b: e.matmul(ps[b][0:64, :], lhsT=wg_[:, kc, c0:c0 + 64],
                                                                               rhs=mT[:, kc, cs], start=(kc == 0), stop=(kc == 7)),
                             reads=[bwg_, bmT], writes=[pb[b]], inc=(kc == 7))
                    P.op("act", lambda e, dt_=dt_, cs=cs, b=b: e.copy(out=dt_[0:64, cs], in_=ps[b][0:64, :]),
                         reads=[pb[b]], writes=[bdt])
            for (vt, bvt, c0) in [(vs, bvs, 448), (vw, bvw, 576)]:
                for g8 in range(2):
                    b = 4 + pi % 2
                    pi += 1
                    for kk in range(8):
                        kb = g8 * 8 + kk
                        for kc in range(8):
                            P.op("pe", lambda e, kc=kc, kb=kb, kk=kk, c0=c0, b=b: e.matmul(
                                ps[b][:, kk * 64:(kk + 1) * 64], lhsT=mT[:, kc, kb * 128:(kb + 1) * 128], rhs=wg_[:, kc, c0:c0 + 64],
                                start=(kc == 0), stop=(kc == 7)),
                                 reads=[bwg_, bmT], writes=[pb[b]], inc=(kc == 7 and kk == 7))
                    P.op("act", lambda e, vt=vt, g8=g8, b=b: e.copy(out=vt[:, g8 * 8:(g8 + 1) * 8, 0:64],
                                                                  in_=ps[b].rearrange("p (k d) -> p k d", k=8)),
                         reads=[pb[b]], writes=[bvt])
            dbg(cx, 3)
            xh = A.alloc([128], F32)
            x2 = A.alloc([128], F32)
            sgm = A.alloc([128], F32)
            gh = A.alloc([128], BF16)
            bxh, bx2, bsgm, bgh = P.buf("xh"), P.buf("x2"), P.buf("sgm"), P.buf("gh")
            for kv, (cf, bcf) in enumerate([(kcf, bkcf), (vcf, bvcf)]):
                b = 4 + kv
                for l in range(32):
                    P.op("pe", lambda e, kv=kv, l=l, cf=cf, b=b: e.matmul(ps[b][:, 0:127], lhsT=W1[kv][0:64, l, :],
                                                                         rhs=cf[0:64, l:l + 16 * 126 + 1:16],
                                                                         start=(l == 0), stop=(l == 31)),
                         reads=[bW1[kv], bcf], writes=[pb[b]], inc=(l == 31))
                P.op("act", lambda e, kv=kv, b=b: e.activation(out=xh[:, 0:127], in_=ps[b][:, 0:127], func=AF.Identity,
                                                               bias=cpos[:, kv:kv + 1], scale=1.0),
                     reads=[pb[b], bcpos], writes=[bxh])
                P.op("dve", lambda e: e.tensor_tensor(out=x2[:, 0:127], in0=xh[:, 0:127], in1=xh[:, 0:127], op=ALU.mult),
                     reads=[bxh], writes=[bx2])
                P.op("dve", lambda e: e.tensor_scalar(out=x2[:, 0:127], in0=x2[:, 0:127], scalar1=0.044715, scalar2=1.0,
                                                      op0=ALU.mult, op1=ALU.add), reads=[bx2], writes=[bx2])
                P.op("dve", lambda e: e.tensor_tensor(out=x2[:, 0:127], in0=x2[:, 0:127], in1=xh[:, 0:127], op=ALU.mult),
                     reads=[bx2, bxh], writes=[bx2])
                P.op("act", lambda e: e.activation(out=sgm[:, 0:127], in_=x2[:, 0:127], func=AF.Sigmoid, scale=1.5957691216057308),
                     reads=[bx2], writes=[bsgm])
                P.op("dve", lambda e: e.tensor_tensor(out=gh[:, 0:127], in0=xh[:, 0:127], in1=sgm[:, 0:127], op=ALU.mult),
                     reads=[bxh, bsgm], writes=[bgh])
                if kv == 0:
                    P.op("pe", lambda e: e.matmul(ps[6][0:64, 0:127], lhsT=w2[0], rhs=gh[:, 0:127], start=True, stop=True),
                         reads=[bw2[0], bgh], writes=[pb[6]])
                    P.op("act", lambda e: e.copy(out=kcT[0:64, 0:127], in_=ps[6][0:64, 0:127]), reads=[pb[6]], writes=[bkcT])
                else:
                    P.op("pe", lambda e: e.matmul(ps[6][0:127, 0:64], lhsT=gh[:, 0:127], rhs=w2[1], start=True, stop=True),
                         reads=[bw2[1], bgh], writes=[pb[6]])
                    P.op("act", lambda e: e.copy(out=vcx[0:127, 0:64], in_=ps[6][0:127, 0:64]), reads=[pb[6]], writes=[bvcx])
            dbg(cx, 4)
            xb = [A.alloc([S], F32) for _ in range(2)]
            bxb = P.bufs(2, "xb")
            tmpc = [A.alloc([512], F32) for _ in range(2)]
            btmpc = P.bufs(2, "tmpc")
            Ec = [A.alloc([512], BF16) for _ in range(3)]
            bEc = P.bufs(3, "Ec")
            rc = A.alloc([8], F32)
            brc = P.bufs(2, "rc")
            rg = A.alloc([8], F32)
            brg = P.bufs(2, "rg")
            nc_ = 0
            stages3 = []
            SB3 = [0, 1, 7]
            for r in range(4):
                h = 4 * g + r
                xbr = xb[r % 2]
                bxbr = bxb[r % 2]
                for c in range(4):
                    sb = SB3[nc_ % 3]
                    eb = nc_ % 3
                    tb = nc_ % 2
                    ob = nc_ % 2
                    nc_ += 1
                    psC, bC = ps[2 + ob], pb[2 + ob]

                    def front(r=r, h=h, c=c, sb=sb, eb=eb, tb=tb, xbr=xbr, bxbr=bxbr):
                        if c == 0:
                            P.dma("sp", [(xbr, bass.AP(rt, h * 4096 + 31, [[16, 128], [1, 2048]]))], writes=[bxbr],
                                  key="xb%d" % (r % 2))
                        cs = slice(c * 512, (c + 1) * 512)
                        P.op("pe", lambda e: e.matmul(ps[sb][0:127, :], lhsT=kcT[0:64, 0:127], rhs=qa[r][0:64, cs],
                                                      start=True, stop=True),
                             reads=[bkcT, bqa[r]], writes=[pb[sb]])
                        P.op("dve", lambda e: e.scalar_tensor_tensor(
                            out=tmpc[tb][0:127, :], in0=ps[sb][0:127, :], scalar=scale, in1=rev_cols(xbr[0:127, :], 2047 - c * 512, 512),
                            op0=ALU.mult, op1=ALU.add), reads=[pb[sb], bxbr], writes=[btmpc[tb]])
                        P.op("act", lambda e: e.activation(out=Ec[eb][0:127, :], in_=tmpc[tb][0:127, :], func=AF.Exp),
                             reads=[btmpc[tb]], writes=[bEc[eb]])

                    def back(r=r, h=h, c=c, eb=eb, ob=ob, psC=psC, bC=bC):
                        for qi in range(4):
                            P.op("pe", lambda e, qi=qi: e.matmul(
                                psC[:, qi * 97:(qi + 1) * 97], lhsT=Ec[eb][0:127, qi * 128:(qi + 1) * 128], rhs=vcx[0:127, :],
                                start=(qi == 0), stop=False, skip_group_check=True),
                                 reads=[bEc[eb], bvcx], writes=[bC], inc=(qi == 3))
                        pc3 = psC[:, 0:388].rearrange("p (q d) -> p q d", d=97)
                        P.op("dve", lambda e: e.tensor_scalar(out=rc[:, ob * 4:(ob + 1) * 4], in0=pc3[:, :, 64],
                                                              scalar1=1e-30, scalar2=None, op0=ALU.max),
                             reads=[bC], writes=[brc[ob]])
                        P.op("dve", lambda e: e.reciprocal(out=rc[:, ob * 4:(ob + 1) * 4], in_=rc[:, ob * 4:(ob + 1) * 4]),
                             reads=[brc[ob]], writes=[brc[ob]])
                        P.op("dve", lambda e: e.tensor_tensor(out=rg[:, ob * 4:(ob + 1) * 4], in0=rc[:, ob * 4:(ob + 1) * 4],
                                                              in1=gates[:, 4 * c:4 * c + 4, 3 * h], op=ALU.mult),
                             reads=[brc[ob]] + bgates[4 * c:4 * c + 4], writes=[brg[ob]])
                        for qi in range(4):
                            qb = 4 * c + qi
                            P.op("dve", lambda e, qi=qi, qb=qb: e.tensor_scalar(
                                out=ocmp[:, qb, r, :], in0=psC[:, qi * 97:qi * 97 + 64], scalar1=rg[:, ob * 4 + qi:ob * 4 + qi + 1],
                                scalar2=None, op0=ALU.mult), reads=[bC, brg[ob]], writes=[bocmp[qb]])
                            if r == 0:
                                P.op("dve", lambda e, qi=qi, qb=qb: e.tensor_scalar(
                                    out=imp[:, qb, :], in0=psC[:, qi * 97 + 65:qi * 97 + 97], scalar1=rc[:, ob * 4 + qi:ob * 4 + qi + 1],
                                    scalar2=None, op0=ALU.mult), reads=[bC, brc[ob]], writes=[bimp[qb]])
                            else:
                                P.op("dve", lambda e, qi=qi, qb=qb: e.scalar_tensor_tensor(
                                    out=imp[:, qb, :], in0=psC[:, qi * 97 + 65:qi * 97 + 97], scalar=rc[:, ob * 4 + qi:ob * 4 + qi + 1],
                                    in1=imp[:, qb, :], op0=ALU.mult, op1=ALU.add), reads=[bC, brc[ob], bimp[qb]], writes=[bimp[qb]])

                    stages3.append((front, back))
            run_pipeline(stages3, 2)
            dbg(cx, 5)
            sc = A.alloc([32], F32)
            wk = A.alloc([32], F32)
            m8 = A.alloc([16], F32)
            stg = A.alloc([96], BF16)
            bsc, bwk, bm8, bstg = P.buf("sc"), P.buf("wk"), P.buf("m8"), P.buf("stg")
            P.op("dve", lambda e: e.memset(stg, 0.0), writes=[bstg])
            for qb in range(16):
                P.op("dve", lambda e, qb=qb: e.tensor_tensor(out=sc, in0=imp[:, qb, :], in1=selc[:, 0, qb, :], op=ALU.mult),
                     reads=[bimp[qb], bselc], writes=[bsc])
                P.op("dve", lambda e, qb=qb: e.tensor_tensor(out=sc, in0=sc, in1=selc[:, 1, qb, :], op=ALU.add),
                     reads=[bsc, bselc], writes=[bsc])
                P.op("dve", lambda e: e.max(out=m8[:, 0:8], in_=sc), reads=[bsc], writes=[bm8])
                P.op("dve", lambda e: e.match_replace(out=wk, in_to_replace=m8[:, 0:8], in_values=sc, imm_value=-3.0e38),
                     reads=[bsc, bm8], writes=[bwk])
                P.op("dve", lambda e: e.max(out=m8[:, 8:16], in_=wk), reads=[bwk], writes=[bm8])
                P.op("dve", lambda e: e.tensor_scalar(out=wk, in0=sc, scalar1=m8[:, 15:16], scalar2=None, op0=ALU.is_ge),
                     reads=[bsc, bm8], writes=[bwk])
                P.op("dve", lambda e: e.tensor_scalar(out=stg[:, 64:96], in0=wk, scalar1=-NEGM, scalar2=NEGM, op0=ALU.mult, op1=ALU.add),
                     reads=[bwk], writes=[bstg])
                pst = psT.bitcast(BF16)
                P.op("pe", lambda e, pst=pst: e.transpose(out=pst[0:96, 0:128], in_=stg, identity=cx.ident),
                     reads=[bstg, cx.bident], writes=[bT])
                P.op("act", lambda e, qb=qb, pst=pst: e.copy(out=nst[64:96, qb * 128:(qb + 1) * 128], in_=pst[64:96, 0:128]),
                     reads=[bT], writes=[bnst])
            for r in range(4):
                P.op("act", lambda e, r=r: e.copy(out=qa[r][64:96, :], in_=nst[64:96, :]), reads=[bnst], writes=[bqa[r]])
            dbg(cx, 6)
            E = [A.alloc([512], BF16) for _ in range(6)]
            bE = P.bufs(6, "E")
            tmpd = [A.alloc([256], F32) for _ in range(3)]
            btd = P.bufs(3, "tmpd")
            rsw = A.alloc([16], F32)
            brsw = P.bufs(2, "rsw")
            tmpo = [A.alloc([64], F32) for _ in range(2)]
            btmpo = P.bufs(2, "tmpo")
            st = {"S": 0, "E": 0, "D": 0, "O": 0}
            zE = A.alloc([128], BF16)
            bzE = P.buf("zE")
            P.op("dve", lambda e: e.memset(zE, 0.0), writes=[bzE])

            SB5 = [0, 1, 6, 7]
            stages5 = []

            def add_branch(kind, r, h, c, psO, bO, fin):
                kb_lo = 0 if kind == 0 else max(0, 4 * c - 4)
                kbs = list(range(kb_lo, 4 * c + 4))
                vt, bvt = (vs, bvs) if kind == 0 else (vw, bvw)
                for kb in kbs:
                    qlo = max(kb, 4 * c)
                    qhi = 4 * c + 3 if kind == 0 else min(kb + 4, 4 * c + 3)
                    nq = qhi - qlo + 1
                    ncol = nq * 128
                    sb = SB5[st["S"] % 4]
                    st["S"] += 1
                    eb = st["E"] % 6
                    st["E"] += 1
                    d0 = qlo - kb
                    n01 = max(0, min(2, d0 + nq) - d0) if d0 < 2 else 0
                    n4 = 1 if (kind == 1 and qhi - kb == 4) else 0
                    ncst = nq - n01 - n4
                    db1 = st["D"] % 3
                    if n01:
                        st["D"] += 1
                    db4 = st["D"] % 3
                    if n4:
                        st["D"] += 1

                    def front(kind=kind, r=r, h=h, kb=kb, qlo=qlo, ncol=ncol, sb=sb, eb=eb, d0=d0, n01=n01, n4=n4, ncst=ncst,
                              db1=db1, db4=db4):
                        if kind == 0:
                            P.op("pe", lambda e: e.matmul(ps[sb][:, 0:ncol], lhsT=ka[:, kb * 128:(kb + 1) * 128],
                                                          rhs=qa[r][:, qlo * 128:qlo * 128 + ncol], start=True, stop=True),
                                 reads=[bka, bqa[r]], writes=[pb[sb]])
                        else:
                            P.op("pe", lambda e: e.matmul(ps[sb][:, 0:ncol], lhsT=kw[0:64, kb * 128:(kb + 1) * 128],
                                                          rhs=qa[r][0:64, qlo * 128:qlo * 128 + ncol], start=True, stop=True),
                                 reads=[bkw, bqa[r]], writes=[pb[sb]])
                        col = 0
                        if n01:
                            w = n01 * 128
                            P.op("dve", lambda e: e.scalar_tensor_tensor(
                                out=tmpd[db1][:, 0:w], in0=ps[sb][:, 0:w], scalar=scale,
                                in1=rev_cols(x01[:, r, :], 255 - d0 * 128, w), op0=ALU.mult, op1=ALU.add),
                                 reads=[pb[sb], bx01], writes=[btd[db1]])
                            P.op("act", lambda e: e.activation(out=E[eb][:, 0:w], in_=tmpd[db1][:, 0:w], func=AF.Exp),
                                 reads=[btd[db1]], writes=[bE[eb]])
                            col = w
                        col4 = (n01 + ncst) * 128
                        if n4:
                            P.op("dve", lambda e: e.tensor_tensor(out=tmpd[db4][:, 0:128], in0=ps[sb][:, col4:col4 + 128], in1=m4,
                                                                  op=ALU.add), reads=[pb[sb], bm4], writes=[btd[db4]])
                            P.op("act", lambda e: e.activation(out=E[eb][:, col4:col4 + 128], in_=tmpd[db4][:, 0:128], func=AF.Exp,
                                                               scale=scale, bias=c31[:, h:h + 1]),
                                 reads=[btd[db4], bc31], writes=[bE[eb]])
                        if ncst:
                            w2_ = ncst * 128
                            P.op("act", lambda e: e.activation(out=E[eb][:, col:col + w2_], in_=ps[sb][:, col:col + w2_], func=AF.Exp,
                                                               scale=scale, bias=c31[:, h:h + 1]),
                                 reads=[pb[sb], bc31], writes=[bE[eb]])

                    def back(kb=kb, qlo=qlo, nq=nq, eb=eb, c=c, psO=psO, bO=bO, vt=vt, bvt=bvt, is_first=(kb == kbs[0]),
                             is_last=(kb == kbs[-1]), fin=fin):
                        if is_first:
                            for oi in range(4):
                                P.op("pe", lambda e, oi=oi: e.matmul(psO[:, oi * 65:(oi + 1) * 65], lhsT=zE, rhs=vt[:, 0, :],
                                                                   start=(oi == 0), stop=False, skip_group_check=True),
                                     reads=[bzE, bvt], writes=[bO], inc=(oi == 3))
                        for qi in range(nq):
                            oi = qlo + qi - 4 * c
                            P.op("pe", lambda e, qi=qi, oi=oi: e.matmul(
                                psO[:, oi * 65:(oi + 1) * 65], lhsT=E[eb][:, qi * 128:(qi + 1) * 128], rhs=vt[:, kb, :],
                                start=False, stop=False, skip_group_check=True),
                                 reads=[bE[eb], bvt], writes=[bO], inc=(qi == nq - 1))
                        if is_last and fin is not None:
                            fin()

                    stages5.append((front, back))

            def make_fin(r, h, c, ob, psOs, bOs, psOw, bOw):
                def fin():
                    o3s = psOs[:, 0:260].rearrange("p (q d) -> p q d", d=65)
                    o3w = psOw[:, 0:260].rearrange("p (q d) -> p q d", d=65)
                    rs_ = rsw[:, ob * 8:ob * 8 + 4]
                    rw_ = rsw[:, ob * 8 + 4:ob * 8 + 8]
                    P.op("dve", lambda e: e.reciprocal(out=rs_, in_=o3s[:, :, 64]), reads=[bOs], writes=[brsw[ob]])
                    P.op("dve", lambda e: e.reciprocal(out=rw_, in_=o3w[:, :, 64]), reads=[bOw], writes=[brsw[ob]])
                    P.op("dve", lambda e: e.tensor_tensor(out=rs_, in0=rs_, in1=gates[:, 4 * c:4 * c + 4, 3 * h + 1], op=ALU.mult),
                         reads=[brsw[ob]] + bgates[4 * c:4 * c + 4], writes=[brsw[ob]])
                    P.op("dve", lambda e: e.tensor_tensor(out=rw_, in0=rw_, in1=gates[:, 4 * c:4 * c + 4, 3 * h + 2], op=ALU.mult),
                         reads=[brsw[ob]] + bgates[4 * c:4 * c + 4], writes=[brsw[ob]])
                    for oi in range(4):
                        qb = 4 * c + oi
                        tb = oi % 2
                        P.op("dve", lambda e, oi=oi, qb=qb, tb=tb: e.scalar_tensor_tensor(
                            out=tmpo[tb], in0=psOs[:, oi * 65:oi * 65 + 64], scalar=rsw[:, ob * 8 + oi:ob * 8 + oi + 1],
                            in1=ocmp[:, qb, r, :], op0=ALU.mult, op1=ALU.add),
                             reads=[bOs, brsw[ob], bocmp[qb]], writes=[btmpo[tb]])
                        P.op("dve", lambda e, oi=oi, qb=qb, tb=tb: e.scalar_tensor_tensor(
                            out=o_all[:, qb, h * 64:(h + 1) * 64], in0=psOw[:, oi * 65:oi * 65 + 64],
                            scalar=rsw[:, ob * 8 + 4 + oi:ob * 8 + 4 + oi + 1], in1=tmpo[tb], op0=ALU.mult, op1=ALU.add),
                             reads=[bOw, brsw[ob], btmpo[tb]], writes=[bo_all[qb]])
                return fin

            for r in range(4):
                h = 4 * g + r
                for c in range(4):
                    ob = st["O"] % 2
                    st["O"] += 1
                    psOs, bOs = ps[2 + ob], pb[2 + ob]
                    psOw, bOw = ps[4 + ob], pb[4 + ob]
                    add_branch(0, r, h, c, psOs, bOs, None)
                    add_branch(1, r, h, c, psOw, bOw, make_fin(r, h, c, ob, psOs, bOs, psOw, bOw))
            run_pipeline(stages5, 3)
            P.barrier()
            A.release()
            dbg(cx, 7)
        A.mark()
        wo = A.alloc([8, 1024], BF16)
        bwo = P.buf("wo")
        P.dma("pool", [(wo, cx.inp["nsa_w_o"][j].rearrange("(kc p) c -> p kc c", p=128))], writes=[bwo])
        oT = [A.alloc([8, 128], BF16) for _ in range(2)]
        boT = P.bufs(2, "oT")
        hr = [A.alloc([D], F32) for _ in range(2)]
        bhr = P.bufs(2, "hr")
        for qb in range(16):
            out_proj_tile(cx, o_all[:, qb, :], bo_all[qb], wo, bwo, src, dst, tok0 + qb * 128, oT, boT, hr, bhr, qb % 2,
                          psT, bT, ps[0:2], pb[0:2])
        P.barrier()
        A.release()
    A.release()


def final_norm_phase(cx, src, dst):
    P, A = cx.P, cx.A
    A.mark()
    gain, bgain = load_gain(cx, cx.inp["final_norm"], D)
    hn = [A.alloc([D], F32) for _ in range(2)]
    bhn = P.bufs(2, "fhn")
    yo = [A.alloc([D], F32) for _ in range(2)]
    byo = P.bufs(2, "fyo")
    junk = A.alloc([D], BF16)
    bjunk = P.buf("junk")
    ss = A.alloc([8], F32)
    bss = P.bufs(8, "ss")
    for i in range(T // 128):
        sl = i % 2
        c = i % 8
        P.dma("sp", [(hn[sl], src[i * 128:(i + 1) * 128, :])], writes=[bhn[sl]], key="fhn%d" % sl)
        rms_rstd(cx, hn[sl], bhn[sl], D, ss, bss[c], c, junk, bjunk)
        P.op("dve", lambda e, sl=sl, c=c: e.scalar_tensor_tensor(out=yo[sl], in0=hn[sl], scalar=ss[:, c:c + 1], in1=gain,
                                                                 op0=ALU.mult, op1=ALU.mult),
             reads=[bhn[sl], bss[c], bgain], writes=[byo[sl]])
        P.dma("sp", [(dst[i * 128:(i + 1) * 128, :], yo[sl])], reads=[byo[sl]], key="fyo%d" % sl)
    P.barrier()
    A.release()


INPUT_SHAPES = {
    "ffn_norm_a": (DEPTH, D), "ffn_a_w_gate": (DEPTH, D, FF), "ffn_a_w_up": (DEPTH, D, FF), "ffn_a_w_down": (DEPTH, FF, D),
    "mix_norm": (DEPTH, D), "ffn_norm_b": (DEPTH, D), "ffn_b_w_gate": (DEPTH, D, FF), "ffn_b_w_up": (DEPTH, D, FF),
    "ffn_b_w_down": (DEPTH, FF, D), "final_norm": (D,), "rel_bias": (32, 16),
    "mla_w_in": (2, D, 672), "mla_q_norm": (2, 384), "mla_kv_norm": (2, 256), "mla_w_uq": (2, 384, 1536),
    "mla_w_ukv": (2, 256, 2048), "mla_w_o": (2, 1024, 1024),
    "nsa_w_in": (2, D, 2608), "nsa_cmp_pos_k": (2, 32, 64), "nsa_cmp_w1_k": (2, 2048, 128), "nsa_cmp_w2_k": (2, 128, 64),
    "nsa_cmp_pos_v": (2, 32, 64), "nsa_cmp_w1_v": (2, 2048, 128), "nsa_cmp_w2_v": (2, 128, 64), "nsa_w_o": (2, 1024, 1024),
}
ARENA_WORDS = 50944


def host_consts():
    c = {}
    c["c_ident"] = np.eye(128, dtype=np.float32).astype(ml_dtypes.bfloat16)
    half = 16
    inv = (np.float32(10000.0) ** (-np.arange(half, dtype=np.float32) * np.float32(2.0) / np.float32(32))).astype(np.float32)
    ang = np.arange(S, dtype=np.float32)[:, None] * inv[None, :]
    cos, sin = np.cos(ang).astype(np.float32), np.sin(ang).astype(np.float32)
    c["c_rope_cos"] = np.ascontiguousarray(np.concatenate([cos.T, cos.T], axis=0))
    c["c_rope_sin"] = np.ascontiguousarray(np.concatenate([sin.T, sin.T], axis=0))
    k = np.arange(128)[:, None]
    t = np.arange(128)[None, :]
    c["c_mdiag"] = np.where(t >= k, 0.0, NEGM).astype(np.float32)
    c["c_m4"] = np.where(t < k, 0.0, NEGM).astype(np.float32)
    def bucket(dist):
        n = np.maximum(dist, 0)
        nf = np.maximum(n, 1).astype(np.float32)
        large = 16 + (np.log(nf / np.float32(16)) / np.float32(math.log(8.0)) * np.float32(16)).astype(np.int32)
        return np.where(n < 16, n, np.minimum(large, 31))
    i = np.arange(4096)
    dist = 2047 - i
    oh = np.zeros((33, 4096), np.float32)
    bk = bucket(dist)
    oh[bk[dist >= 0], i[dist >= 0]] = 1.0
    oh[32, i[dist < 0]] = 1.0
    c["c_oh"] = oh
    kk = np.arange(S)
    c["c_bexp"] = (kk[None, :] // 64 == np.arange(32)[:, None]).astype(np.float32).astype(ml_dtypes.bfloat16)
    cs_ = np.arange(127) * 16
    ce_ = cs_ + 32
    ss_ = np.arange(32) * 64
    se_ = ss_ + 64
    ov = np.minimum(ce_[:, None], se_[None, :]) - np.maximum(cs_[:, None], ss_[None, :])
    c["c_ov"] = (np.clip(ov, 0, None) / 32.0).astype(np.float32).astype(ml_dtypes.bfloat16)
    tl = np.arange(128)[:, None, None]
    qb_ = np.arange(16)[None, :, None]
    jj = np.arange(32)[None, None, :]
    blk = (qb_ * 128 + tl) // 64
    forced = (jj == 0) | (jj == blk) | (jj == blk - 1)
    causal = jj <= blk
    c["c_cm"] = (causal & ~forced).astype(np.float32)
    c["c_add"] = np.where(forced, 1e6, np.where(causal, 0.0, -1e6)).astype(np.float32)
    return c


def default_phases():
    ph = []
    for i in range(DEPTH):
        ph.append(("ffn", i, "a"))
        ph.append(("mla", i // 2) if i % 2 == 0 else ("nsa", i // 2))
        ph.append(("ffn", i, "b"))
    ph.append(("final",))
    return ph


def build_program(phases, dbg_stop=None):
    nc = bass.Bass("TRN2", target_bir_lowering=False)
    cx = Ctx()
    cx.dbg_stop = dbg_stop
    cx.nc = nc
    cx.inp = {}
    cx.cd = {}
    x_in = nc.dram_tensor("x", [T, D], F32, kind="ExternalInput").ap()
    for name, shp in INPUT_SHAPES.items():
        cx.inp[name] = nc.dram_tensor(name, list(shp), F32, kind="ExternalInput").ap()
    consts = host_consts()
    cd = cx.cd
    for name, arr in consts.items():
        dt = BF16 if arr.dtype == ml_dtypes.bfloat16 else F32
        cd[name] = nc.dram_tensor(name, list(arr.shape), dt, kind="ExternalInput").ap()
    y = nc.dram_tensor("y", [T, D], F32, kind="ExternalOutput").ap()
    hbuf = nc.dram_tensor("hbuf", [T, D], F32).ap()
    with ExitStack() as es:
        big = es.enter_context(nc.sbuf_tensor("big", [128, ARENA_WORDS], F32))
        cx.ps = [es.enter_context(nc.psum_tensor("ps%d" % i, [128, 512], F32))[:, :] for i in range(8)]
        P = Prog(nc, es)
        cx.P = P
        cx.pb = P.bufs(8, "psb")
        cx.A = Arena(big, ARENA_WORDS)
        cx.ident = cx.A.alloc([128], BF16)
        cx.bident = P.buf("ident")
        P.dma("sp", [(cx.ident, cd["c_ident"])], writes=[cx.bident])
        cur = x_in
        try:
          for ph in phases:
            if ph[0] == "ffn":
                ffn_phase(cx, ph[1], ph[2], cur, hbuf)
                cur = hbuf
            elif ph[0] == "mla":
                mla_phase(cx, ph[1], 2 * ph[1], cur, hbuf)
                cur = hbuf
            elif ph[0] == "nsa":
                if not getattr(cx, "nsa_ready", False):
                    nsa_setup(cx)
                    cx.nsa_ready = True
                nsa_phase(cx, ph[1], 2 * ph[1] + 1, cur, hbuf)
                cur = hbuf
            elif ph[0] == "final":
                final_norm_phase(cx, cur, y)
                cur = y
            else:
                raise NotImplementedError(ph)
        except StopBuild:
            cur = x_in
        if cur is not y:
            P.dma("sp", [(y, cur)], key="ycopy")
            P.barrier()
        P.run_block()
    cx.consts = consts
    return nc, cx


_CACHE = {}


def kernel(**inputs):
    phases = default_phases()
    key = "full"
    if key not in _CACHE:
        _CACHE[key] = build_program(phases)
    nc, cx = _CACHE[key]
    x = np.ascontiguousarray(np.asarray(inputs["x"], dtype=np.float32)).reshape(N_CORES, T, D)
    shared = {k: np.ascontiguousarray(np.asarray(inputs[k], dtype=np.float32)) for k in INPUT_SHAPES}
    shared.update(cx.consts)
    in_maps = []
    for c in range(N_CORES):
        m = dict(shared)
        m["x"] = x[c]
        in_maps.append(m)
    res = run_bass_kernel_spmd(nc, in_maps, core_ids=list(range(N_CORES)))
    out = np.stack([np.asarray(r["y"], dtype=np.float32) for r in res.results], axis=0)
    return out.reshape(16, S, D)
```

```python
import math
from contextlib import ExitStack

import numpy as np
import ml_dtypes

import concourse.bass as bass
import concourse.mybir as mybir
from concourse.bass_utils import run_bass_kernel_spmd

F32 = mybir.dt.float32
BF16 = mybir.dt.bfloat16
AF = mybir.ActivationFunctionType
ALU = mybir.AluOpType

N_CORES = 8
D = 1024
S = 2048
NSEQ = 2
T = NSEQ * S
FF = 2816
NHC = FF // 128
DEPTH = 4
EPS = 1e-6
NEGM = -16384.0

ENGS = ["pe", "act", "dve", "pool", "sp"]
SELF_SYNC = {"pe": False, "act": True, "dve": True, "pool": True, "sp": False}


class Buf:
    __slots__ = ("name", "base", "w", "r")

    def __init__(self, name, base):
        self.name = name
        self.base = base
        self.w = []
        self.r = []


class Prog:
    def __init__(self, nc, es):
        self.nc = nc
        self.es = es
        self.q = {e: [] for e in ENGS}
        self.cnt = {e: 0 for e in ENGS}
        self.sems = {e: es.enter_context(nc.semaphore("s_" + e)) for e in ENGS if e != "sp"}
        self.dcnt = {}
        self.nbuf = 0
        self.ninst = 0

    def buf(self, name=None):
        self.nbuf += 1
        return Buf("%s_%d" % (name or "b", self.nbuf), name or "b")

    def bufs(self, n, name="b"):
        return [self.buf("%s%d" % (name, i)) for i in range(n)]

    def _deps(self, reads, writes):
        waits = []
        for b in reads:
            waits += b.w
        for b in writes:
            waits += b.w
            waits += b.r
        return waits

    def _commit(self, ev, reads, writes):
        for b in reads:
            b.r.append(ev)
        for b in writes:
            b.w = [ev]
            b.r = []

    def op(self, eng, fn, reads=(), writes=(), inc=True):
        waits = self._deps(reads, writes)
        idx = self.cnt[eng] + 1
        if inc:
            self.cnt[eng] = idx
        self._commit((eng, idx), reads, writes)
        self.q[eng].append((fn, waits, (eng, 1) if inc else None))
        self.ninst += 1

    def dma(self, eng, pairs, reads=(), writes=(), key=None, **kw):
        key = "d_" + (key or writes[0].base)
        if key not in self.sems:
            self.sems[key] = self.es.enter_context(self.nc.semaphore(key))
            self.dcnt[key] = 0
        waits = self._deps(reads, writes)
        self.dcnt[key] += 16 * len(pairs)
        self._commit((key, self.dcnt[key]), reads, writes)
        for i, (o, a) in enumerate(pairs):
            self.q[eng].append(((lambda e, o=o, a=a: e.dma_start(out=o, in_=a, **kw)),
                                waits if i == 0 else [], (key, 16)))
            self.ninst += 1

    def barrier(self):
        evs = [(e, self.cnt[e]) for e in ENGS if e != "sp" and self.cnt[e] > 0]
        evs += [(k, v) for k, v in self.dcnt.items() if v > 0]
        for e in ENGS:
            self.q[e].append((None, list(evs), None))

    def replay(self, eng, e):
        waited = {}
        for fn, waits, inc in self.q[eng]:
            need = {}
            for k, v in waits:
                if k == eng and not SELF_SYNC[eng]:
                    continue
                if v > need.get(k, 0):
                    need[k] = v
            for k, v in need.items():
                if waited.get(k, 0) < v:
                    e.wait_ge(self.sems[k], v)
                    waited[k] = v
            if fn is not None:
                ins = fn(e)
                if inc is not None:
                    ins.then_inc(self.sems[inc[0]], inc[1])

    def run_block(self):
        for e in ENGS:
            assert self.cnt[e] < 65000, (e, self.cnt[e])
        with self.nc.Block() as block:
            @block.sync
            def _(e):
                self.replay("sp", e)

            @block.tensor
            def _(e):
                self.replay("pe", e)

            @block.scalar
            def _(e):
                self.replay("act", e)

            @block.vector
            def _(e):
                self.replay("dve", e)

            @block.gpsimd
            def _(e):
                self.replay("pool", e)


class Arena:
    def __init__(self, big, words):
        self.big = big
        self.words = words
        self.top = 0
        self.marks = []
        self.peak = 0

    def alloc(self, shape_free, dtype, parts=128):
        n = int(np.prod(shape_free))
        nb = 2 if dtype == BF16 else 4
        w = (n * nb + 3) // 4
        w = (w + 7) // 8 * 8
        assert self.top + w <= self.words, ("SBUF arena overflow", self.top, w, self.words)
        ap = self.big[0:parts, self.top:self.top + w]
        self.top += w
        self.peak = max(self.peak, self.top)
        if dtype != F32:
            ap = ap.bitcast(dtype)
        ap = ap[:, 0:n]
        if len(shape_free) > 1:
            names = " ".join("d%d" % i for i in range(len(shape_free)))
            kw = {"d%d" % i: int(s) for i, s in enumerate(shape_free)}
            ap = ap.rearrange("p (%s) -> p %s" % (names, names), **kw)
        return ap

    def mark(self):
        self.marks.append(self.top)

    def release(self):
        self.top = self.marks.pop()


class Ctx:
    dbg_stop = None


class StopBuild(Exception):
    pass


def dbg(cx, level):
    if cx.dbg_stop is not None and cx.dbg_stop == level:
        cx.P.barrier()
        raise StopBuild()


def rms_rstd(cx, x_ap, bx, n, ss, bss, col, junk, bjunk):
    P = cx.P
    c = ss[:, col:col + 1]
    P.op("act", lambda e: e.activation(out=junk[:, 0:n], in_=x_ap, func=AF.Square, accum_out=c),
         reads=[bx], writes=[bjunk, bss])
    P.op("dve", lambda e: e.tensor_scalar(out=c, in0=c, scalar1=1.0 / n, scalar2=EPS, op0=ALU.mult, op1=ALU.add),
         reads=[bss], writes=[bss])
    P.op("act", lambda e: e.activation(out=c, in_=c, func=AF.Ln), reads=[bss], writes=[bss])
    P.op("act", lambda e: e.activation(out=c, in_=c, func=AF.Exp, scale=-0.5), reads=[bss], writes=[bss])


def load_gain(cx, vec_ap, n, name="gain"):
    P, A = cx.P, cx.A
    g = A.alloc([n], F32)
    bg = P.buf(name)
    P.dma("sp", [(g, vec_ap.partition_broadcast(128))], writes=[bg])
    return g, bg


def ffn_phase(cx, li, which, src, dst):
    P, A, ps, pb = cx.P, cx.A, cx.ps, cx.pb
    A.mark()
    Wg_d = cx.inp["ffn_%s_w_gate" % which][li]
    Wu_d = cx.inp["ffn_%s_w_up" % which][li]
    Wd_d = cx.inp["ffn_%s_w_down" % which][li]
    gn_d = cx.inp["ffn_norm_%s" % which][li]
    wg = A.alloc([8, FF], BF16)
    wu = A.alloc([8, FF], BF16)
    wd = A.alloc([NHC, D], BF16)
    CG = [(0, 768), (768, 1536), (1536, 2304), (2304, 2816)]
    bwg, bwu, bwd = P.bufs(4, "wg"), P.bufs(4, "wu"), P.bufs(4, "wd")
    for gi, (c0, c1) in enumerate(CG):
        P.dma("pool", [(wg[:, :, c0:c1], Wg_d[:, c0:c1].rearrange("(kc p) c -> p kc c", p=128))], writes=[bwg[gi]],
              key="wg%d" % gi)
        P.dma("pool", [(wu[:, :, c0:c1], Wu_d[:, c0:c1].rearrange("(kc p) c -> p kc c", p=128))], writes=[bwu[gi]],
              key="wu%d" % gi)
    for gi, (c0, c1) in enumerate(CG):
        P.dma("pool", [(wd[:, c0 // 128:c1 // 128, :], Wd_d[c0:c1, :].rearrange("(hc p) c -> p hc c", p=128))],
              writes=[bwd[gi]], key="wd%d" % gi)
    gain, bgain = load_gain(cx, gn_d, D)
    hn = [A.alloc([D], F32) for _ in range(2)]
    bhn = P.bufs(2, "hn")
    junk = A.alloc([D], BF16)
    bjunk = P.buf("junk")
    ss = A.alloc([8], F32)
    bss = P.bufs(8, "ss")
    xn = A.alloc([4, D], BF16)
    bxn = P.bufs(4, "xn")
    xT = A.alloc([8, 512], BF16)
    bxT = P.buf("xT")
    sg = [A.alloc([512], BF16) for _ in range(2)]
    bsg = P.bufs(2, "sg")
    hT = A.alloc([NHC, 512], BF16)
    bhT = P.bufs(NHC, "hT")
    hr = [A.alloc([D], F32) for _ in range(2)]
    bhr = P.bufs(2, "hr")
    psA, psB, psD, psT = ps[0:2], ps[2:4], ps[4:6], ps[6]
    bA, bB, bD, bT = pb[0:2], pb[2:4], pb[4:6], pb[6]
    ident = cx.ident
    cnt = {"n": 0, "g": 0, "d": 0}

    def norm_a(tt):
        for j in range(4):
            n = cnt["n"]
            cnt["n"] += 1
            sl = n % 2
            r0 = tt * 512 + j * 128
            P.dma("sp", [(hn[sl], src[r0:r0 + 128, :])], writes=[bhn[sl]], key="hn%d" % sl)
            c = n % 8
            rms_rstd(cx, hn[sl], bhn[sl], D, ss, bss[c], c, junk, bjunk)
            P.op("dve", lambda e, sl=sl, c=c, j=j: e.scalar_tensor_tensor(
                out=xn[:, j, :], in0=hn[sl], scalar=ss[:, c:c + 1], in1=gain, op0=ALU.mult, op1=ALU.mult),
                 reads=[bhn[sl], bss[c], bgain], writes=[bxn[j]])

    def norm_b(tt):
        pst = psT.bitcast(BF16)
        for j in range(4):
            for kc in range(8):
                P.op("pe", lambda e, j=j, kc=kc: e.transpose(out=pst[:, kc * 128:(kc + 1) * 128],
                                                               in_=xn[:, j, kc * 128:(kc + 1) * 128], identity=ident),
                     reads=[bxn[j], cx.bident], writes=[bT], inc=(kc == 7))
            P.op("act", lambda e, j=j: e.copy(out=xT[:, :, j * 128:(j + 1) * 128],
                                              in_=pst.rearrange("p (k t) -> p k t", k=8)),
                 reads=[bT], writes=[bxT])

    def gateup(tt):
        for hc in range(NHC):
            g = cnt["g"]
            cnt["g"] += 1
            s2 = g % 2
            gi = min(hc // 6, 3)
            for kc in range(8):
                P.op("pe", lambda e, hc=hc, kc=kc, s2=s2: e.matmul(psA[s2], lhsT=wg[:, kc, hc * 128:(hc + 1) * 128],
                                                                    rhs=xT[:, kc, :], start=(kc == 0), stop=(kc == 7)),
                     reads=[bxT, bwg[gi]], writes=[bA[s2]], inc=(kc == 7))
            for kc in range(8):
                P.op("pe", lambda e, hc=hc, kc=kc, s2=s2: e.matmul(psB[s2], lhsT=wu[:, kc, hc * 128:(hc + 1) * 128],
                                                                    rhs=xT[:, kc, :], start=(kc == 0), stop=(kc == 7)),
                     reads=[bxT, bwu[gi]], writes=[bB[s2]], inc=(kc == 7))
            P.op("act", lambda e, s2=s2: e.activation(out=sg[s2], in_=psA[s2], func=AF.Silu),
                 reads=[bA[s2]], writes=[bsg[s2]])
            P.op("dve", lambda e, s2=s2, hc=hc: e.tensor_tensor(out=hT[:, hc, :], in0=sg[s2], in1=psB[s2], op=ALU.mult),
                 reads=[bsg[s2], bB[s2]], writes=[bhT[hc]])

    def down(tt):
        for j in range(4):
            r0 = tt * 512 + j * 128
            sl = (tt * 4 + j) % 2
            P.dma("sp", [(hr[sl], src[r0:r0 + 128, :])], writes=[bhr[sl]], key="hr%d" % sl)
            for half in range(2):
                d = cnt["d"]
                cnt["d"] += 1
                s2 = d % 2
                for hc in range(NHC):
                    gi = min(hc // 6, 3)
                    P.op("pe", lambda e, hc=hc, j=j, half=half, s2=s2: e.matmul(
                        psD[s2], lhsT=hT[:, hc, j * 128:(j + 1) * 128], rhs=wd[:, hc, half * 512:(half + 1) * 512],
                        start=(hc == 0), stop=(hc == NHC - 1)),
                         reads=[bhT[hc], bwd[gi]], writes=[bD[s2]], inc=(hc == NHC - 1))
                P.op("dve", lambda e, sl=sl, half=half, s2=s2: e.scalar_tensor_tensor(
                    out=hr[sl][:, half * 512:(half + 1) * 512], in0=psD[s2], scalar=0.5,
                    in1=hr[sl][:, half * 512:(half + 1) * 512], op0=ALU.mult, op1=ALU.add),
                     reads=[bD[s2], bhr[sl]], writes=[bhr[sl]])
            P.dma("sp", [(dst[r0:r0 + 128, :], hr[sl])], reads=[bhr[sl]], key="hrst%d" % sl)

    NT = T // 512
    norm_a(0)
    norm_b(0)
    for tt in range(NT):
        gateup(tt)
        if tt + 1 < NT:
            norm_a(tt + 1)
        down(tt)
        if tt + 1 < NT:
            norm_b(tt + 1)
    P.barrier()
    A.release()


def run_pipeline(stages, L):
    n = len(stages)
    for i in range(n + L):
        if i < n:
            stages[i][0]()
        if i >= L:
            stages[i - L][1]()


def norm_T_tile(cx, src, r0, gain, bgain, hn, bhn, sl, ss, bss, c, junk, bjunk, xn, bxn, psT, bT, dstT, bdstT, col0,
                part=None):
    P = cx.P
    if part in (None, 0):
        norm_T_front(cx, src, r0, gain, bgain, hn, bhn, sl, ss, bss, c, junk, bjunk, xn, bxn)
    if part in (None, 1):
        norm_T_back(cx, sl, xn, bxn, psT, bT, dstT, bdstT, col0)


def norm_T_front(cx, src, r0, gain, bgain, hn, bhn, sl, ss, bss, c, junk, bjunk, xn, bxn):
    P = cx.P
    P.dma("sp", [(hn[sl], src[r0:r0 + 128, :])], writes=[bhn[sl]], key="mhn%d" % sl)
    rms_rstd(cx, hn[sl], bhn[sl], D, ss, bss[c], c, junk, bjunk)
    P.op("dve", lambda e: e.scalar_tensor_tensor(out=xn[sl], in0=hn[sl], scalar=ss[:, c:c + 1], in1=gain,
                                                 op0=ALU.mult, op1=ALU.mult),
         reads=[bhn[sl], bss[c], bgain], writes=[bxn[sl]])


def norm_T_back(cx, sl, xn, bxn, psT, bT, dstT, bdstT, col0):
    P = cx.P
    pst = psT.bitcast(BF16)
    for kc in range(8):
        P.op("pe", lambda e, kc=kc: e.transpose(out=pst[:, kc * 128:(kc + 1) * 128], in_=xn[sl][:, kc * 128:(kc + 1) * 128],
                                                identity=cx.ident),
             reads=[bxn[sl], cx.bident], writes=[bT], inc=(kc == 7))
    P.op("act", lambda e: e.copy(out=dstT[:, 0:8, col0:col0 + 128], in_=pst.rearrange("p (k t) -> p k t", k=8)),
         reads=[bT], writes=[bdstT])


def out_proj_tile(cx, o_tile, bo, wo, bwo, src, dst, r0, oT, boT, hr, bhr, sl, psT, bT, psW, bW, part=None):
    P = cx.P
    if part in (None, 0):
        pst = psT.bitcast(BF16)
        for kc in range(8):
            P.op("pe", lambda e, kc=kc: e.transpose(out=pst[:, kc * 128:(kc + 1) * 128], in_=o_tile[:, kc * 128:(kc + 1) * 128],
                                                    identity=cx.ident),
                 reads=[bo, cx.bident], writes=[bT], inc=(kc == 7))
        P.op("act", lambda e: e.copy(out=oT[sl], in_=pst.rearrange("p (k t) -> p k t", k=8)), reads=[bT], writes=[boT[sl]])
        P.dma("sp", [(hr[sl], src[r0:r0 + 128, :])], writes=[bhr[sl]], key="mhr%d" % sl)
    if part == 0:
        return
    for half in range(2):
        for kc in range(8):
            P.op("pe", lambda e, kc=kc, half=half: e.matmul(psW[half], lhsT=oT[sl][:, kc, :],
                                                            rhs=wo[:, kc, half * 512:(half + 1) * 512],
                                                            start=(kc == 0), stop=(kc == 7)),
                 reads=[boT[sl], bwo], writes=[bW[half]], inc=(kc == 7))
        P.op("dve", lambda e, half=half: e.tensor_tensor(out=hr[sl][:, half * 512:(half + 1) * 512], in0=psW[half],
                                                         in1=hr[sl][:, half * 512:(half + 1) * 512], op=ALU.add),
             reads=[bW[half], bhr[sl]], writes=[bhr[sl]])
    P.dma("sp", [(dst[r0:r0 + 128, :], hr[sl])], reads=[bhr[sl]], key="mhrst%d" % sl)


def mla_phase(cx, j, li, src, dst):
    P, A, ps, pb = cx.P, cx.A, cx.ps, cx.pb
    A.mark()
    scale = 96.0 ** -0.5
    Win_d, Wuq_d, Wukv_d, Wo_d = cx.inp["mla_w_in"][j], cx.inp["mla_w_uq"][j], cx.inp["mla_w_ukv"][j], cx.inp["mla_w_o"][j]
    win = A.alloc([8, 672], BF16)
    wkr = A.alloc([8, 96], BF16)
    wq = A.alloc([3, 1536], BF16)
    wqs = A.alloc([3, 16, 96], BF16)
    wkn = A.alloc([2, 16, 64], BF16)
    wv = A.alloc([2, 16, 64], BF16)
    wo = A.alloc([8, 1024], BF16)
    bwin, bwkr, bwq, bwqs, bwkn, bwv, bwo = (P.buf(n) for n in ["win", "wkr", "wq", "wqs", "wkn", "wv", "wo"])
    P.dma("pool", [(win, Win_d.rearrange("(kc p) c -> p kc c", p=128))], writes=[bwin])
    w_in_r = Win_d.rearrange("(kc p) c -> p kc c", p=128)
    P.op("dve", lambda e: e.memset(wkr, 0.0), writes=[bwkr])
    P.dma("pool", [(wkr[:, :, 64:80], w_in_r[:, :, 656:672]), (wkr[:, :, 80:96], w_in_r[:, :, 640:656])], writes=[bwkr])
    P.op("act", lambda e: e.mul(out=wkr[:, :, 64:80], in_=wkr[:, :, 64:80], mul=-1.0), reads=[bwkr], writes=[bwkr])
    wuq_r = Wuq_d.rearrange("(kc p) (h d) -> p kc h d", p=128, d=96)
    P.dma("pool", [(wq, Wuq_d.rearrange("(kc p) c -> p kc c", p=128))], writes=[bwq])
    P.op("dve", lambda e: e.memset(wqs, 0.0), writes=[bwqs])
    for kc in range(3):
        P.dma("pool", [(wqs[:, kc, :, 64:80], wuq_r[:, kc, :, 80:96]), (wqs[:, kc, :, 80:96], wuq_r[:, kc, :, 64:80])],
              writes=[bwqs], key="wqs")
    P.op("act", lambda e: e.mul(out=wqs[:, :, :, 64:80], in_=wqs[:, :, :, 64:80], mul=-1.0), reads=[bwqs], writes=[bwqs])
    wukv_r = Wukv_d.rearrange("(kc p) (h d) -> p kc h d", p=128, d=128)
    for kc in range(2):
        P.dma("pool", [(wkn[:, kc, :, :], wukv_r[:, kc, :, 0:64])], writes=[bwkn], key="wkn")
        P.dma("pool", [(wv[:, kc, :, :], wukv_r[:, kc, :, 64:128])], writes=[bwv], key="wv")
    P.dma("pool", [(wo, Wo_d.rearrange("(kc p) c -> p kc c", p=128))], writes=[bwo])
    gain, bgain = load_gain(cx, cx.inp["mix_norm"][li], D)
    gq, bgq = load_gain(cx, cx.inp["mla_q_norm"][j], 384, "gainq")
    gkv, bgkv = load_gain(cx, cx.inp["mla_kv_norm"][j], 256, "gainkv")
    CC = A.alloc([S], F32)
    SS = A.alloc([S], F32)
    bcs = P.buf("cs")
    P.dma("sp", [(CC[64:96, :], cx.cd["c_rope_cos"]), (SS[64:96, :], cx.cd["c_rope_sin"])], writes=[bcs])
    mdiag = A.alloc([128], F32)
    bmd = P.buf("mdiag")
    P.dma("sp", [(mdiag, cx.cd["c_mdiag"])], writes=[bmd])
    junk = A.alloc([D], BF16)
    bjunk = P.buf("junk")
    ss = A.alloc([8], F32)
    bss = P.bufs(8, "ss")
    ssq = A.alloc([8], F32)
    bssq = P.bufs(8, "ssq")
    cqnT = A.alloc([3, S], BF16)
    ckvnT = A.alloc([2, S], BF16)
    bcqnT, bckvnT = P.buf("cqnT"), P.buf("ckvnT")
    kT = [A.alloc([S], BF16) for _ in range(2)]
    bkT = P.bufs(2, "kT")
    for b2 in range(2):
        P.op("dve", lambda e, b2=b2: e.memset(kT[b2][96:128, :], 0.0), writes=[bkT[b2]])
    o_all = A.alloc([16, D], BF16)
    bo_all = P.bufs(16, "o_all")
    psT, bT = ps[6], pb[6]

    for sq in range(NSEQ):
        tok0 = sq * S
        A.mark()
        hn = [A.alloc([D], F32) for _ in range(3)]
        bhn = P.bufs(3, "hn")
        xn = [A.alloc([D], BF16) for _ in range(3)]
        bxn = P.bufs(3, "xn")
        mT = A.alloc([8, 512], BF16)
        bmT = P.buf("mT")
        cqn = [A.alloc([384], BF16) for _ in range(2)]
        ckvn = [A.alloc([256], BF16) for _ in range(2)]
        bcqn, bckvn = P.bufs(2, "cqn"), P.bufs(2, "ckvn")
        tmpa = A.alloc([512], F32)
        tmpb = A.alloc([512], F32)
        btmpa, btmpb = P.buf("tmpa"), P.buf("tmpb")
        n = 0
        for c in range(4):
            stA = []
            for jj in range(4):
                def frontA(jj=jj, n=n, c=c):
                    norm_T_tile(cx, src, tok0 + c * 512 + jj * 128, gain, bgain, hn, bhn, n % 3, ss, bss, n % 8, junk, bjunk,
                                xn, bxn, psT, bT, mT, bmT, jj * 128, part=0)

                def backA(jj=jj, n=n, c=c):
                    norm_T_tile(cx, src, tok0 + c * 512 + jj * 128, gain, bgain, hn, bhn, n % 3, ss, bss, n % 8, junk, bjunk,
                                xn, bxn, psT, bT, mT, bmT, jj * 128, part=1)
                stA.append((frontA, backA))
                n += 1
            run_pipeline(stA, 2)
            for kc in range(8):
                P.op("pe", lambda e, kc=kc: e.matmul(ps[4][0:96, :], lhsT=win[:, kc, 576:672], rhs=mT[:, kc, :],
                                                     start=(kc == 0), stop=(kc == 7)),
                     reads=[bwin, bmT], writes=[pb[4]], inc=(kc == 7))
            for kc in range(8):
                P.op("pe", lambda e, kc=kc: e.matmul(ps[5][0:96, :], lhsT=wkr[:, kc, :], rhs=mT[:, kc, :],
                                                     start=(kc == 0), stop=(kc == 7)),
                     reads=[bwkr, bmT], writes=[pb[5]], inc=(kc == 7))
            cs = slice(c * 512, (c + 1) * 512)
            P.op("dve", lambda e, cs=cs: e.tensor_tensor(out=tmpa[64:96, :], in0=ps[4][64:96, :], in1=CC[64:96, cs], op=ALU.mult),
                 reads=[pb[4], bcs], writes=[btmpa])
            P.op("dve", lambda e, cs=cs: e.tensor_tensor(out=tmpb[64:96, :], in0=ps[5][64:96, :], in1=SS[64:96, cs], op=ALU.mult),
                 reads=[pb[5], bcs], writes=[btmpb])
            for b2 in range(2):
                P.op("dve", lambda e, cs=cs, b2=b2: e.tensor_tensor(out=kT[b2][64:96, cs], in0=tmpa[64:96, :], in1=tmpb[64:96, :],
                                                                    op=ALU.add),
                     reads=[btmpa, btmpb], writes=[bkT[b2]])
            for jj in range(4):
                tsl = slice(jj * 128, (jj + 1) * 128)
                s2 = jj % 2
                for kc in range(8):
                    P.op("pe", lambda e, kc=kc, tsl=tsl, s2=s2: e.matmul(ps[s2][:, 0:384], lhsT=mT[:, kc, tsl],
                                                                        rhs=win[:, kc, 0:384], start=(kc == 0), stop=(kc == 7)),
                         reads=[bwin, bmT], writes=[pb[s2]], inc=(kc == 7))
                for kc in range(8):
                    P.op("pe", lambda e, kc=kc, tsl=tsl, s2=s2: e.matmul(ps[2 + s2][:, 0:256], lhsT=mT[:, kc, tsl],
                                                                        rhs=win[:, kc, 384:640], start=(kc == 0), stop=(kc == 7)),
                         reads=[bwin, bmT], writes=[pb[2 + s2]], inc=(kc == 7))
                cq = (c * 4 + jj) % 8
                rms_rstd(cx, ps[s2][:, 0:384], pb[s2], 384, ssq, bssq[cq], cq, junk, bjunk)
                P.op("dve", lambda e, s2=s2, cq=cq: e.scalar_tensor_tensor(out=cqn[s2], in0=ps[s2][:, 0:384],
                                                                           scalar=ssq[:, cq:cq + 1], in1=gq,
                                                                           op0=ALU.mult, op1=ALU.mult),
                     reads=[pb[s2], bssq[cq], bgq], writes=[bcqn[s2]])
                rms_rstd(cx, ps[2 + s2][:, 0:256], pb[2 + s2], 256, ss, bss[cq], cq, junk, bjunk)
                P.op("dve", lambda e, s2=s2, cq=cq: e.scalar_tensor_tensor(out=ckvn[s2], in0=ps[2 + s2][:, 0:256],
                                                                           scalar=ss[:, cq:cq + 1], in1=gkv,
                                                                           op0=ALU.mult, op1=ALU.mult),
                     reads=[pb[2 + s2], bss[cq], bgkv], writes=[bckvn[s2]])
                pst = psT.bitcast(BF16)
                for kc in range(3):
                    P.op("pe", lambda e, kc=kc, s2=s2: e.transpose(out=pst[:, kc * 128:(kc + 1) * 128],
                                                                   in_=cqn[s2][:, kc * 128:(kc + 1) * 128], identity=cx.ident),
                         reads=[bcqn[s2], cx.bident], writes=[bT], inc=False)
                for kc in range(2):
                    P.op("pe", lambda e, kc=kc, s2=s2: e.transpose(out=pst[:, (3 + kc) * 128:(4 + kc) * 128],
                                                                   in_=ckvn[s2][:, kc * 128:(kc + 1) * 128], identity=cx.ident),
                         reads=[bckvn[s2], cx.bident], writes=[bT], inc=(kc == 1))
                g0 = c * 512 + jj * 128
                P.op("act", lambda e, g0=g0: e.copy(out=cqnT[:, :, g0:g0 + 128],
                                                    in_=pst[:, 0:384].rearrange("p (k t) -> p k t", k=3)),
                     reads=[bT], writes=[bcqnT])
                P.op("act", lambda e, g0=g0: e.copy(out=ckvnT[:, :, g0:g0 + 128],
                                                    in_=pst[:, 384:640].rearrange("p (k t) -> p k t", k=2)),
                     reads=[bT], writes=[bckvnT])
        P.barrier()
        A.release()
        A.mark()
        qT = [A.alloc([S], BF16) for _ in range(2)]
        bqT = P.bufs(2, "qT")
        for b2 in range(2):
            P.op("dve", lambda e, b2=b2: e.memset(qT[b2][96:128, :], 0.0), writes=[bqT[b2]])
        vaug = [A.alloc([16, 65], BF16) for _ in range(2)]
        bva = P.bufs(2, "vaug")
        for b2 in range(2):
            P.op("dve", lambda e, b2=b2: e.memset(vaug[b2][:, :, 64:65], 1.0), writes=[bva[b2]])
        E = [A.alloc([512], BF16) for _ in range(6)]
        bE = P.bufs(6, "E")
        tmpd = [A.alloc([128], F32) for _ in range(2)]
        btd = P.bufs(2, "tmpd")
        tq1 = A.alloc([512], F32)
        tq2 = A.alloc([512], F32)
        btq1, btq2 = P.buf("tq1"), P.buf("tq2")
        rec = A.alloc([8], F32)
        brec = P.bufs(2, "rec")
        nS = 0
        nE = 0
        nD = 0
        def proj(h):
            hb = h % 2
            for c in range(4):
                cs = slice(c * 512, (c + 1) * 512)
                for kc in range(3):
                    P.op("pe", lambda e, kc=kc, cs=cs, h=h: e.matmul(ps[2][0:96, :], lhsT=wq[:, kc, h * 96:(h + 1) * 96], rhs=cqnT[:, kc, cs],
                                                                    start=(kc == 0), stop=(kc == 2)),
                         reads=[bwq, bcqnT], writes=[pb[2]], inc=(kc == 2))
                for kc in range(3):
                    P.op("pe", lambda e, kc=kc, cs=cs, h=h: e.matmul(ps[3][0:96, :], lhsT=wqs[:, kc, h, :], rhs=cqnT[:, kc, cs],
                                                                    start=(kc == 0), stop=(kc == 2)),
                         reads=[bwqs, bcqnT], writes=[pb[3]], inc=(kc == 2))
                for kc in range(2):
                    P.op("pe", lambda e, kc=kc, cs=cs, h=h: e.matmul(ps[4][0:64, :], lhsT=wkn[:, kc, h, :], rhs=ckvnT[:, kc, cs],
                                                                    start=(kc == 0), stop=(kc == 1)),
                         reads=[bwkn, bckvnT], writes=[pb[4]], inc=(kc == 1))
                P.op("act", lambda e, cs=cs, hb=hb: e.copy(out=qT[hb][0:64, cs], in_=ps[2][0:64, :]),
                     reads=[pb[2]], writes=[bqT[hb]])
                P.op("dve", lambda e, cs=cs: e.tensor_tensor(out=tq1[64:96, :], in0=ps[2][64:96, :], in1=CC[64:96, cs], op=ALU.mult),
                     reads=[pb[2], bcs], writes=[btq1])
                P.op("dve", lambda e, cs=cs: e.tensor_tensor(out=tq2[64:96, :], in0=ps[3][64:96, :], in1=SS[64:96, cs], op=ALU.mult),
                     reads=[pb[3], bcs], writes=[btq2])
                P.op("dve", lambda e, cs=cs, hb=hb: e.tensor_tensor(out=qT[hb][64:96, cs], in0=tq1[64:96, :], in1=tq2[64:96, :],
                                                                    op=ALU.add),
                     reads=[btq1, btq2], writes=[bqT[hb]])
                P.op("act", lambda e, cs=cs, hb=hb: e.copy(out=kT[hb][0:64, cs], in_=ps[4][0:64, :]),
                     reads=[pb[4]], writes=[bkT[hb]])
            for g8 in range(2):
                for kk in range(8):
                    kb = g8 * 8 + kk
                    for kc in range(2):
                        P.op("pe", lambda e, kc=kc, kb=kb, kk=kk, h=h: e.matmul(
                            ps[3][:, kk * 64:(kk + 1) * 64], lhsT=ckvnT[:, kc, kb * 128:(kb + 1) * 128], rhs=wv[:, kc, h, :],
                            start=(kc == 0), stop=(kc == 1)),
                             reads=[bwv, bckvnT], writes=[pb[3]], inc=(kc == 1 and kk == 7))
                P.op("act", lambda e, g8=g8, hb=hb: e.copy(out=vaug[hb][:, g8 * 8:(g8 + 1) * 8, 0:64],
                                                           in_=ps[3].rearrange("p (k d) -> p k d", k=8)),
                     reads=[pb[3]], writes=[bva[hb]])

        import os as _os
        ILV = _os.environ.get("MLA_ILV", "1") == "1"
        if ILV:
            proj(0)
        for h in range(16):
            hb = h % 2
            if not ILV:
                proj(h)
            stages = []
            SB = [0, 1, 5]
            for c in range(4):
                ob = c % 2
                psO, bO = ps[6 + ob], pb[6 + ob]
                nkb = 4 * c + 4
                for kb in range(nkb):
                    qlo = max(kb, 4 * c)
                    ncol = (4 * c + 4 - qlo) * 128
                    sb = SB[nS % 3]
                    nS += 1
                    eb = nE % 6
                    nE += 1
                    diag = kb >= 4 * c
                    db = nD % 2
                    if diag:
                        nD += 1

                    def front(kb=kb, qlo=qlo, ncol=ncol, sb=sb, eb=eb, diag=diag, db=db, hb=hb):
                        P.op("pe", lambda e: e.matmul(ps[sb][:, 0:ncol], lhsT=kT[hb][:, kb * 128:(kb + 1) * 128],
                                                      rhs=qT[hb][:, qlo * 128:qlo * 128 + ncol], start=True, stop=True),
                             reads=[bkT[hb], bqT[hb]], writes=[pb[sb]])
                        c0 = 0
                        if diag:
                            P.op("dve", lambda e: e.tensor_tensor(out=tmpd[db], in0=ps[sb][:, 0:128], in1=mdiag, op=ALU.add),
                                 reads=[pb[sb], bmd], writes=[btd[db]])
                            P.op("act", lambda e: e.activation(out=E[eb][:, 0:128], in_=tmpd[db], func=AF.Exp, scale=scale),
                                 reads=[btd[db]], writes=[bE[eb]])
                            c0 = 128
                        if ncol > c0:
                            P.op("act", lambda e: e.activation(out=E[eb][:, c0:ncol], in_=ps[sb][:, c0:ncol], func=AF.Exp,
                                                               scale=scale), reads=[pb[sb]], writes=[bE[eb]])

                    def back(kb=kb, qlo=qlo, eb=eb, c=c, ob=ob, psO=psO, bO=bO, hb=hb, h=h, nkb=nkb):
                        nq = 4 * c + 4 - qlo
                        for qi in range(nq):
                            oi = qlo + qi - 4 * c
                            P.op("pe", lambda e, qi=qi, oi=oi: e.matmul(
                                psO[:, oi * 65:(oi + 1) * 65], lhsT=E[eb][:, qi * 128:(qi + 1) * 128], rhs=vaug[hb][:, kb, :],
                                start=(kb == 0 and qi == 0), stop=False, skip_group_check=True),
                                 reads=[bE[eb], bva[hb]], writes=[bO], inc=(qi == nq - 1))
                        if kb == nkb - 1:
                            rb = brec[ob]
                            P.op("dve", lambda e: e.reciprocal(out=rec[:, ob * 4:(ob + 1) * 4],
                                                               in_=psO[:, 0:260].rearrange("p (q d) -> p q d", d=65)[:, :, 64]),
                                 reads=[bO], writes=[rb])
                            for oi in range(4):
                                qb = 4 * c + oi
                                P.op("dve", lambda e, oi=oi, qb=qb: e.tensor_scalar(
                                    out=o_all[:, qb, h * 64:(h + 1) * 64], in0=psO[:, oi * 65:oi * 65 + 64],
                                    scalar1=rec[:, ob * 4 + oi:ob * 4 + oi + 1], scalar2=None, op0=ALU.mult),
                                     reads=[bO, rb], writes=[bo_all[qb]])

                    stages.append((front, back))
            if ILV and h + 1 < 16:
                mid = len(stages) // 2
                f0, b0 = stages[mid]
                stages[mid] = ((lambda f0=f0, h=h: (proj(h + 1), f0())), b0)
            run_pipeline(stages, 2)
        P.barrier()
        A.release()
        A.mark()
        oT = [A.alloc([8, 128], BF16) for _ in range(3)]
        boT = P.bufs(3, "oT")
        hr = [A.alloc([D], F32) for _ in range(3)]
        bhr = P.bufs(3, "hr")
        stC = []
        for qb in range(16):
            def frontC(qb=qb):
                out_proj_tile(cx, o_all[:, qb, :], bo_all[qb], wo, bwo, src, dst, tok0 + qb * 128, oT, boT, hr, bhr, qb % 3,
                              ps[6 + qb % 2], pb[6 + qb % 2], ps[0:2], pb[0:2], part=0)

            def backC(qb=qb):
                out_proj_tile(cx, o_all[:, qb, :], bo_all[qb], wo, bwo, src, dst, tok0 + qb * 128, oT, boT, hr, bhr, qb % 3,
                              ps[6 + qb % 2], pb[6 + qb % 2], ps[0:2], pb[0:2], part=1)
            stC.append((frontC, backC))
        run_pipeline(stC, 1)
        P.barrier()
        A.release()
    A.release()


def rev_cols(ap, start, n):
    a = [list(x) for x in ap.ap]
    assert len(a) == 2 and a[1][0] == 1, a
    return bass.AP(ap.tensor, ap.offset + start, [a[0], [-1, n]])


def nsa_setup(cx):
    P, A, ps, pb = cx.P, cx.A, cx.ps, cx.pb
    nc = cx.nc
    cx.rtab_t = nc.dram_tensor("rtab", [17, 4096], F32)
    rtab = cx.rtab_t.ap()[0:16, :]
    A.mark()
    tbl = A.alloc([16], F32)
    btbl = P.buf("tbl")
    P.op("dve", lambda e: e.memset(tbl[0:64, :], NEGM), writes=[btbl])
    P.dma("sp", [(tbl[0:32, :], cx.inp["rel_bias"])], writes=[btbl])
    oh = A.alloc([4096], F32)
    boh = P.buf("oh")
    P.dma("sp", [(oh[0:33, :], cx.cd["c_oh"])], writes=[boh])
    rt = A.alloc([4096], F32)
    brt = P.buf("rt")
    for ch in range(8):
        b = ch % 2
        P.op("pe", lambda e, ch=ch, b=b: e.matmul(ps[b][0:16, :], lhsT=tbl[0:33, :], rhs=oh[0:33, ch * 512:(ch + 1) * 512],
                                                  start=True, stop=True),
             reads=[btbl, boh], writes=[pb[b]])
        P.op("act", lambda e, ch=ch, b=b: e.copy(out=rt[0:16, ch * 512:(ch + 1) * 512], in_=ps[b][0:16, :]),
             reads=[pb[b]], writes=[brt])
    P.dma("sp", [(rtab, rt[0:16, :])], reads=[brt], key="rtab_st")
    P.barrier()
    A.release()
    dbg(cx, 0)


def nsa_phase(cx, j, li, src, dst):
    P, A, ps, pb = cx.P, cx.A, cx.ps, cx.pb
    A.mark()
    scale = 0.125
    Win_d = cx.inp["nsa_w_in"][j]
    w_in_r = Win_d.rearrange("(kc p) c -> p kc c", p=128)
    rt = cx.rtab_t
    W1 = [A.alloc([32, 128], BF16) for _ in range(2)]
    bW1 = P.bufs(2, "W1")
    w2 = [A.alloc([64], BF16) for _ in range(2)]
    bw2 = P.bufs(2, "w2")
    for kv, nm in enumerate(["k", "v"]):
        P.dma("pool", [(W1[kv][0:64], cx.inp["nsa_cmp_w1_%s" % nm][j].rearrange("(l d) c -> d l c", d=64))], writes=[bW1[kv]])
        P.dma("pool", [(w2[kv], cx.inp["nsa_cmp_w2_%s" % nm][j])], writes=[bw2[kv]])
    posf = A.alloc([2, 32], F32)
    posb = A.alloc([2, 32], BF16)
    bposf, bposb = P.buf("posf"), P.buf("posb")
    P.dma("sp", [(posf[0:64, 0, :], cx.inp["nsa_cmp_pos_k"][j].rearrange("l d -> d l")),
                 (posf[0:64, 1, :], cx.inp["nsa_cmp_pos_v"][j].rearrange("l d -> d l"))], writes=[bposf],
          allow_slow_non_contiguous=True)
    P.op("act", lambda e: e.copy(out=posb[0:64], in_=posf[0:64]), reads=[bposf], writes=[bposb])
    cpos = A.alloc([2], F32)
    bcpos = P.buf("cpos")
    for kv in range(2):
        for l in range(32):
            P.op("pe", lambda e, kv=kv, l=l: e.matmul(ps[4 + kv][:, 0:1], lhsT=W1[kv][0:64, l, :], rhs=posb[0:64, kv, l:l + 1],
                                                      start=(l == 0), stop=(l == 31)),
                 reads=[bW1[kv], bposb], writes=[pb[4 + kv]], inc=(l == 31))
        P.op("act", lambda e, kv=kv: e.copy(out=cpos[:, kv:kv + 1], in_=ps[4 + kv][:, 0:1]), reads=[pb[4 + kv]], writes=[bcpos])
    gain, bgain = load_gain(cx, cx.inp["mix_norm"][li], D)
    c31 = A.alloc([16], F32)
    bc31 = P.buf("c31")
    P.dma("sp", [(c31, cx.inp["rel_bias"][31].partition_broadcast(128))], writes=[bc31])
    m4 = A.alloc([128], F32)
    bm4 = P.buf("m4")
    P.dma("sp", [(m4, cx.cd["c_m4"])], writes=[bm4])
    selc = A.alloc([2, 16, 32], F32)
    bselc = P.buf("selc")
    P.dma("sp", [(selc[:, 0], cx.cd["c_cm"]), (selc[:, 1], cx.cd["c_add"])], writes=[bselc])
    junk = A.alloc([D], BF16)
    bjunk = P.buf("junk")
    ss = A.alloc([8], F32)
    bss = P.bufs(8, "ss")
    psT, bT = ps[6], pb[6]
    dbg(cx, 1)

    for sq in range(NSEQ):
        tok0 = sq * S
        mT = A.alloc([8, S], BF16) if sq == 0 else mT
        gates = A.alloc([16, 48], F32) if sq == 0 else gates
        o_all = A.alloc([16, D], BF16) if sq == 0 else o_all
        if sq == 0:
            bmT, bgates = P.buf("mT"), P.bufs(16, "gates")
            bo_all = P.bufs(16, "o_all")
        A.mark()
        wgt = A.alloc([8, 48], BF16)
        bwgt = P.buf("wgt")
        P.dma("pool", [(wgt, w_in_r[:, :, 2560:2608])], writes=[bwgt])
        hn = [A.alloc([D], F32) for _ in range(3)]
        bhn = P.bufs(3, "hn")
        xn = [A.alloc([D], BF16) for _ in range(3)]
        bxn = P.bufs(3, "xn")
        stA = []
        for qb in range(16):
            def frontA(qb=qb):
                norm_T_tile(cx, src, tok0 + qb * 128, gain, bgain, hn, bhn, qb % 3, ss, bss, qb % 8, junk, bjunk,
                            xn, bxn, psT, bT, mT, bmT, qb * 128, part=0)

            def backA(qb=qb):
                norm_T_tile(cx, src, tok0 + qb * 128, gain, bgain, hn, bhn, qb % 3, ss, bss, qb % 8, junk, bjunk,
                            xn, bxn, psT, bT, mT, bmT, qb * 128, part=1)
                b = qb % 2
                for kc in range(8):
                    P.op("pe", lambda e, kc=kc: e.matmul(ps[b][:, 0:48], lhsT=mT[:, kc, qb * 128:(qb + 1) * 128],
                                                         rhs=wgt[:, kc, :], start=(kc == 0), stop=(kc == 7)),
                         reads=[bmT, bwgt], writes=[pb[b]], inc=(kc == 7))
                P.op("dve", lambda e: e.tensor_copy(out=gates[:, qb, :], in_=ps[b][:, 0:48]),
                     reads=[pb[b]], writes=[bgates[qb]])
            stA.append((frontA, backA))
        run_pipeline(stA, 1)
        P.op("act", lambda e: e.activation(out=gates, in_=gates, func=AF.Sigmoid), reads=bgates, writes=bgates)
        P.barrier()
        A.release()
        dbg(cx, 2)
        for g in range(4):
            A.mark()
            wg_ = A.alloc([8, 640], BF16)
            bwg_ = P.buf("wing")
            prs = [(wg_[:, :, 0:256], w_in_r[:, :, g * 256:(g + 1) * 256])]
            for i in range(6):
                prs.append((wg_[:, :, 256 + i * 64:320 + i * 64], w_in_r[:, :, 1024 + i * 256 + g * 64:1024 + i * 256 + (g + 1) * 64]))
            P.dma("pool", prs, writes=[bwg_])
            x01 = A.alloc([4, 256], F32)
            bx01 = P.buf("x01")
            P.dma("sp", [(x01, bass.AP(rt, (4 * g) * 4096 + 1792, [[1, 128], [4096, 4], [1, 256]]))], writes=[bx01])
            qa = [A.alloc([S], BF16) for _ in range(4)]
            bqa = P.bufs(4, "qa")
            ka = A.alloc([S], BF16)
            bka = P.buf("ka")
            P.dma("sp", [(ka[64:96, :], cx.cd["c_bexp"])], writes=[bka])
            P.op("dve", lambda e: e.memset(ka[96:128, :], 0.0), writes=[bka])
            for r_ in range(4):
                P.op("dve", lambda e, r_=r_: e.memset(qa[r_][96:128, :], 0.0), writes=[bqa[r_]])
            kw = A.alloc([S], BF16)
            kcf = A.alloc([S], BF16)
            vcf = A.alloc([S], BF16)
            bkw, bkcf, bvcf = P.buf("kw"), P.buf("kcf"), P.buf("vcf")
            vs = A.alloc([16, 65], BF16)
            vw = A.alloc([16, 65], BF16)
            bvs, bvw = P.buf("vs"), P.buf("vw")
            P.op("dve", lambda e: e.memset(vs[:, :, 64:65], 1.0), writes=[bvs])
            P.op("dve", lambda e: e.memset(vw[:, :, 64:65], 1.0), writes=[bvw])
            kcT = A.alloc([128], BF16)
            bkcT = P.buf("kcT")
            vcx = A.alloc([97], BF16)
            bvcx = P.buf("vcx")
            P.op("dve", lambda e: e.memset(vcx[:, 64:65], 1.0), writes=[bvcx])
            P.dma("sp", [(vcx[0:127, 65:97], cx.cd["c_ov"])], writes=[bvcx])
            ocmp = A.alloc([16, 4, 64], BF16)
            bocmp = P.bufs(16, "ocmp")
            imp = A.alloc([16, 32], F32)
            bimp = P.bufs(16, "imp")
            nst = A.alloc([S], BF16)
            bnst = P.buf("nst")
            pi = 0
            fm = [(qa[0], bqa[0], 0), (qa[1], bqa[1], 64), (qa[2], bqa[2], 128), (qa[3], bqa[3], 192),
                  (kcf, bkcf, 256), (vcf, bvcf, 320), (ka, bka, 384), (kw, bkw, 512)]
            for (dt_, bdt, c0) in fm:
                for c in range(4):
                    b = 4 + pi % 2
                    pi += 1
                    cs = slice(c * 512, (c + 1) * 512)
                    for kc in range(8):
                        P.op("pe", lambda e, kc=kc, cs=cs, c0=c0, b=b: e.matmul(ps[b][0:64, :], lhsT=wg_[:, kc, c0:c0 + 64],
                                                                               rhs=mT[:, kc, cs], start=(kc == 0), stop=(kc == 7)),
                             reads=[bwg_, bmT], writes=[pb[b]], inc=(kc == 7))
                    P.op("act", lambda e, dt_=dt_, cs=cs, b=b: e.copy(out=dt_[0:64, cs], in_=ps[b][0:64, :]),
                         reads=[pb[b]], writes=[bdt])
            for (vt, bvt, c0) in [(vs, bvs, 448), (vw, bvw, 576)]:
                for g8 in range(2):
                    b = 4 + pi % 2
                    pi += 1
                    for kk in range(8):
                        kb = g8 * 8 + kk
                        for kc in range(8):
                            P.op("pe", lambda e, kc=kc, kb=kb, kk=kk, c0=c0, b=b: e.matmul(
                                ps[b][:, kk * 64:(kk + 1) * 64], lhsT=mT[:, kc, kb * 128:(kb + 1) * 128], rhs=wg_[:, kc, c0:c0 + 64],
                                start=(kc == 0), stop=(kc == 7)),
                                 reads=[bwg_, bmT], writes=[pb[b]], inc=(kc == 7 and kk == 7))
                    P.op("act", lambda e, vt=vt, g8=g8, b=b: e.copy(out=vt[:, g8 * 8:(g8 + 1) * 8, 0:64],
                                                                  in_=ps[b].rearrange("p (k d) -> p k d", k=8)),
                         reads=[pb[b]], writes=[bvt])
            dbg(cx, 3)
            xh = A.alloc([128], F32)
            x2 = A.alloc([128], F32)
            sgm = A.alloc([128], F32)
            gh = A.alloc([128], BF16)
            bxh, bx2, bsgm, bgh = P.buf("xh"), P.buf("x2"), P.buf("sgm"), P.buf("gh")
            for kv, (cf, bcf) in enumerate([(kcf, bkcf), (vcf, bvcf)]):
                b = 4 + kv
                for l in range(32):
                    P.op("pe", lambda e, kv=kv, l=l, cf=cf, b=b: e.matmul(ps[b][:, 0:127], lhsT=W1[kv][0:64, l, :],
                                                                         rhs=cf[0:64, l:l + 16 * 126 + 1:16],
                                                                         start=(l == 0), stop=(l == 31)),
                         reads=[bW1[kv], bcf], writes=[pb[b]], inc=(l == 31))
                P.op("act", lambda e, kv=kv, b=b: e.activation(out=xh[:, 0:127], in_=ps[b][:, 0:127], func=AF.Identity,
                                                               bias=cpos[:, kv:kv + 1], scale=1.0),
                     reads=[pb[b], bcpos], writes=[bxh])
                P.op("dve", lambda e: e.tensor_tensor(out=x2[:, 0:127], in0=xh[:, 0:127], in1=xh[:, 0:127], op=ALU.mult),
                     reads=[bxh], writes=[bx2])
                P.op("dve", lambda e: e.tensor_scalar(out=x2[:, 0:127], in0=x2[:, 0:127], scalar1=0.044715, scalar2=1.0,
                                                      op0=ALU.mult, op1=ALU.add), reads=[bx2], writes=[bx2])
                P.op("dve", lambda e: e.tensor_tensor(out=x2[:, 0:127], in0=x2[:, 0:127], in1=xh[:, 0:127], op=ALU.mult),
                     reads=[bx2, bxh], writes=[bx2])
                P.op("act", lambda e: e.activation(out=sgm[:, 0:127], in_=x2[:, 0:127], func=AF.Sigmoid, scale=1.5957691216057308),
                     reads=[bx2], writes=[bsgm])
                P.op("dve", lambda e: e.tensor_tensor(out=gh[:, 0:127], in0=xh[:, 0:127], in1=sgm[:, 0:127], op=ALU.mult),
                     reads=[bxh, bsgm], writes=[bgh])
                if kv == 0:
                    P.op("pe", lambda e: e.matmul(ps[6][0:64, 0:127], lhsT=w2[0], rhs=gh[:, 0:127], start=True, stop=True),
                         reads=[bw2[0], bgh], writes=[pb[6]])
                    P.op("act", lambda e: e.copy(out=kcT[0:64, 0:127], in_=ps[6][0:64, 0:127]), reads=[pb[6]], writes=[bkcT])
                else:
                    P.op("pe", lambda e: e.matmul(ps[6][0:127, 0:64], lhsT=gh[:, 0:127], rhs=w2[1], start=True, stop=True),
                         reads=[bw2[1], bgh], writes=[pb[6]])
                    P.op("act", lambda e: e.copy(out=vcx[0:127, 0:64], in_=ps[6][0:127, 0:64]), reads=[pb[6]], writes=[bvcx])
            dbg(cx, 4)
            xb = [A.alloc([S], F32) for _ in range(2)]
            bxb = P.bufs(2, "xb")
            tmpc = [A.alloc([512], F32) for _ in range(2)]
            btmpc = P.bufs(2, "tmpc")
            Ec = [A.alloc([512], BF16) for _ in range(3)]
            bEc = P.bufs(3, "Ec")
            rc = A.alloc([8], F32)
            brc = P.bufs(2, "rc")
            rg = A.alloc([8], F32)
            brg = P.bufs(2, "rg")
            nc_ = 0
            stages3 = []
            SB3 = [0, 1, 7]
            for r in range(4):
                h = 4 * g + r
                xbr = xb[r % 2]
                bxbr = bxb[r % 2]
                for c in range(4):
                    sb = SB3[nc_ % 3]
                    eb = nc_ % 3
                    tb = nc_ % 2
                    ob = nc_ % 2
                    nc_ += 1
                    psC, bC = ps[2 + ob], pb[2 + ob]

                    def front(r=r, h=h, c=c, sb=sb, eb=eb, tb=tb, xbr=xbr, bxbr=bxbr):
                        if c == 0:
                            P.dma("sp", [(xbr, bass.AP(rt, h * 4096 + 31, [[16, 128], [1, 2048]]))], writes=[bxbr],
                                  key="xb%d" % (r % 2))
                        cs = slice(c * 512, (c + 1) * 512)
                        P.op("pe", lambda e: e.matmul(ps[sb][0:127, :], lhsT=kcT[0:64, 0:127], rhs=qa[r][0:64, cs],
                                                      start=True, stop=True),
                             reads=[bkcT, bqa[r]], writes=[pb[sb]])
                        P.op("dve", lambda e: e.scalar_tensor_tensor(
                            out=tmpc[tb][0:127, :], in0=ps[sb][0:127, :], scalar=scale, in1=rev_cols(xbr[0:127, :], 2047 - c * 512, 512),
                            op0=ALU.mult, op1=ALU.add), reads=[pb[sb], bxbr], writes=[btmpc[tb]])
                        P.op("act", lambda e: e.activation(out=Ec[eb][0:127, :], in_=tmpc[tb][0:127, :], func=AF.Exp),
                             reads=[btmpc[tb]], writes=[bEc[eb]])

                    def back(r=r, h=h, c=c, eb=eb, ob=ob, psC=psC, bC=bC):
                        for qi in range(4):
                            P.op("pe", lambda e, qi=qi: e.matmul(
                                psC[:, qi * 97:(qi + 1) * 97], lhsT=Ec[eb][0:127, qi * 128:(qi + 1) * 128], rhs=vcx[0:127, :],
                                start=(qi == 0), stop=False, skip_group_check=True),
                                 reads=[bEc[eb], bvcx], writes=[bC], inc=(qi == 3))
                        pc3 = psC[:, 0:388].rearrange("p (q d) -> p q d", d=97)
                        P.op("dve", lambda e: e.tensor_scalar(out=rc[:, ob * 4:(ob + 1) * 4], in0=pc3[:, :, 64],
                                                              scalar1=1e-30, scalar2=None, op0=ALU.max),
                             reads=[bC], writes=[brc[ob]])
                        P.op("dve", lambda e: e.reciprocal(out=rc[:, ob * 4:(ob + 1) * 4], in_=rc[:, ob * 4:(ob + 1) * 4]),
                             reads=[brc[ob]], writes=[brc[ob]])
                        P.op("dve", lambda e: e.tensor_tensor(out=rg[:, ob * 4:(ob + 1) * 4], in0=rc[:, ob * 4:(ob + 1) * 4],
                                                              in1=gates[:, 4 * c:4 * c + 4, 3 * h], op=ALU.mult),
                             reads=[brc[ob]] + bgates[4 * c:4 * c + 4], writes=[brg[ob]])
                        for qi in range(4):
                            qb = 4 * c + qi
                            P.op("dve", lambda e, qi=qi, qb=qb: e.tensor_scalar(
                                out=ocmp[:, qb, r, :], in0=psC[:, qi * 97:qi * 97 + 64], scalar1=rg[:, ob * 4 + qi:ob * 4 + qi + 1],
                                scalar2=None, op0=ALU.mult), reads=[bC, brg[ob]], writes=[bocmp[qb]])
                            if r == 0:
                                P.op("dve", lambda e, qi=qi, qb=qb: e.tensor_scalar(
                                    out=imp[:, qb, :], in0=psC[:, qi * 97 + 65:qi * 97 + 97], scalar1=rc[:, ob * 4 + qi:ob * 4 + qi + 1],
                                    scalar2=None, op0=ALU.mult), reads=[bC, brc[ob]], writes=[bimp[qb]])
                            else:
                                P.op("dve", lambda e, qi=qi, qb=qb: e.scalar_tensor_tensor(
                                    out=imp[:, qb, :], in0=psC[:, qi * 97 + 65:qi * 97 + 97], scalar=rc[:, ob * 4 + qi:ob * 4 + qi + 1],
                                    in1=imp[:, qb, :], op0=ALU.mult, op1=ALU.add), reads=[bC, brc[ob], bimp[qb]], writes=[bimp[qb]])

                    stages3.append((front, back))
            run_pipeline(stages3, 2)
            dbg(cx, 5)
            sc = A.alloc([32], F32)
            wk = A.alloc([32], F32)
            m8 = A.alloc([16], F32)
            stg = A.alloc([96], BF16)
            bsc, bwk, bm8, bstg = P.buf("sc"), P.buf("wk"), P.buf("m8"), P.buf("stg")
            P.op("dve", lambda e: e.memset(stg, 0.0), writes=[bstg])
            for qb in range(16):
                P.op("dve", lambda e, qb=qb: e.tensor_tensor(out=sc, in0=imp[:, qb, :], in1=selc[:, 0, qb, :], op=ALU.mult),
                     reads=[bimp[qb], bselc], writes=[bsc])
                P.op("dve", lambda e, qb=qb: e.tensor_tensor(out=sc, in0=sc, in1=selc[:, 1, qb, :], op=ALU.add),
                     reads=[bsc, bselc], writes=[bsc])
                P.op("dve", lambda e: e.max(out=m8[:, 0:8], in_=sc), reads=[bsc], writes=[bm8])
                P.op("dve", lambda e: e.match_replace(out=wk, in_to_replace=m8[:, 0:8], in_values=sc, imm_value=-3.0e38),
                     reads=[bsc, bm8], writes=[bwk])
                P.op("dve", lambda e: e.max(out=m8[:, 8:16], in_=wk), reads=[bwk], writes=[bm8])
                P.op("dve", lambda e: e.tensor_scalar(out=wk, in0=sc, scalar1=m8[:, 15:16], scalar2=None, op0=ALU.is_ge),
                     reads=[bsc, bm8], writes=[bwk])
                P.op("dve", lambda e: e.tensor_scalar(out=stg[:, 64:96], in0=wk, scalar1=-NEGM, scalar2=NEGM, op0=ALU.mult, op1=ALU.add),
                     reads=[bwk], writes=[bstg])
                pst = psT.bitcast(BF16)
                P.op("pe", lambda e, pst=pst: e.transpose(out=pst[0:96, 0:128], in_=stg, identity=cx.ident),
                     reads=[bstg, cx.bident], writes=[bT])
                P.op("act", lambda e, qb=qb, pst=pst: e.copy(out=nst[64:96, qb * 128:(qb + 1) * 128], in_=pst[64:96, 0:128]),
                     reads=[bT], writes=[bnst])
            for r in range(4):
                P.op("act", lambda e, r=r: e.copy(out=qa[r][64:96, :], in_=nst[64:96, :]), reads=[bnst], writes=[bqa[r]])
            dbg(cx, 6)
            E = [A.alloc([512], BF16) for _ in range(6)]
            bE = P.bufs(6, "E")
            tmpd = [A.alloc([256], F32) for _ in range(3)]
            btd = P.bufs(3, "tmpd")
            rsw = A.alloc([16], F32)
            brsw = P.bufs(2, "rsw")
            tmpo = [A.alloc([64], F32) for _ in range(2)]
            btmpo = P.bufs(2, "tmpo")
            st = {"S": 0, "E": 0, "D": 0, "O": 0}
            zE = A.alloc([128], BF16)
            bzE = P.buf("zE")
            P.op("dve", lambda e: e.memset(zE, 0.0), writes=[bzE])

            SB5 = [0, 1, 6, 7]
            stages5 = []

            def add_branch(kind, r, h, c, psO, bO, fin):
                kb_lo = 0 if kind == 0 else max(0, 4 * c - 4)
                kbs = list(range(kb_lo, 4 * c + 4))
                vt, bvt = (vs, bvs) if kind == 0 else (vw, bvw)
                for kb in kbs:
                    qlo = max(kb, 4 * c)
                    qhi = 4 * c + 3 if kind == 0 else min(kb + 4, 4 * c + 3)
                    nq = qhi - qlo + 1
                    ncol = nq * 128
                    sb = SB5[st["S"] % 4]
                    st["S"] += 1
                    eb = st["E"] % 6
                    st["E"] += 1
                    d0 = qlo - kb
                    n01 = max(0, min(2, d0 + nq) - d0) if d0 < 2 else 0
                    n4 = 1 if (kind == 1 and qhi - kb == 4) else 0
                    ncst = nq - n01 - n4
                    db1 = st["D"] % 3
                    if n01:
                        st["D"] += 1
                    db4 = st["D"] % 3
                    if n4:
                        st["D"] += 1

                    def front(kind=kind, r=r, h=h, kb=kb, qlo=qlo, ncol=ncol, sb=sb, eb=eb, d0=d0, n01=n01, n4=n4, ncst=ncst,
                              db1=db1, db4=db4):
                        if kind == 0:
                            P.op("pe", lambda e: e.matmul(ps[sb][:, 0:ncol], lhsT=ka[:, kb * 128:(kb + 1) * 128],
                                                          rhs=qa[r][:, qlo * 128:qlo * 128 + ncol], start=True, stop=True),
                                 reads=[bka, bqa[r]], writes=[pb[sb]])
                        else:
                            P.op("pe", lambda e: e.matmul(ps[sb][:, 0:ncol], lhsT=kw[0:64, kb * 128:(kb + 1) * 128],
                                                          rhs=qa[r][0:64, qlo * 128:qlo * 128 + ncol], start=True, stop=True),
                                 reads=[bkw, bqa[r]], writes=[pb[sb]])
                        col = 0
                        if n01:
                            w = n01 * 128
                            P.op("dve", lambda e: e.scalar_tensor_tensor(
                                out=tmpd[db1][:, 0:w], in0=ps[sb][:, 0:w], scalar=scale,
                                in1=rev_cols(x01[:, r, :], 255 - d0 * 128, w), op0=ALU.mult, op1=ALU.add),
                                 reads=[pb[sb], bx01], writes=[btd[db1]])
                            P.op("act", lambda e: e.activation(out=E[eb][:, 0:w], in_=tmpd[db1][:, 0:w], func=AF.Exp),
                                 reads=[btd[db1]], writes=[bE[eb]])
                            col = w
                        col4 = (n01 + ncst) * 128
                        if n4:
                            P.op("dve", lambda e: e.tensor_tensor(out=tmpd[db4][:, 0:128], in0=ps[sb][:, col4:col4 + 128], in1=m4,
                                                                  op=ALU.add), reads=[pb[sb], bm4], writes=[btd[db4]])
                            P.op("act", lambda e: e.activation(out=E[eb][:, col4:col4 + 128], in_=tmpd[db4][:, 0:128], func=AF.Exp,
                                                               scale=scale, bias=c31[:, h:h + 1]),
                                 reads=[btd[db4], bc31], writes=[bE[eb]])
                        if ncst:
                            w2_ = ncst * 128
                            P.op("act", lambda e: e.activation(out=E[eb][:, col:col + w2_], in_=ps[sb][:, col:col + w2_], func=AF.Exp,
                                                               scale=scale, bias=c31[:, h:h + 1]),
                                 reads=[pb[sb], bc31], writes=[bE[eb]])

                    def back(kb=kb, qlo=qlo, nq=nq, eb=eb, c=c, psO=psO, bO=bO, vt=vt, bvt=bvt, is_first=(kb == kbs[0]),
                             is_last=(kb == kbs[-1]), fin=fin):
                        if is_first:
                            for oi in range(4):
                                P.op("pe", lambda e, oi=oi: e.matmul(psO[:, oi * 65:(oi + 1) * 65], lhsT=zE, rhs=vt[:, 0, :],
                                                                   start=(oi == 0), stop=False, skip_group_check=True),
                                     reads=[bzE, bvt], writes=[bO], inc=(oi == 3))
                        for qi in range(nq):
                            oi = qlo + qi - 4 * c
                            P.op("pe", lambda e, qi=qi, oi=oi: e.matmul(
                                psO[:, oi * 65:(oi + 1) * 65], lhsT=E[eb][:, qi * 128:(qi + 1) * 128], rhs=vt[:, kb, :],
                                start=False, stop=False, skip_group_check=True),
                                 reads=[bE[eb], bvt], writes=[bO], inc=(qi == nq - 1))
                        if is_last and fin is not None:
                            fin()

                    stages5.append((front, back))

            def make_fin(r, h, c, ob, psOs, bOs, psOw, bOw):
                def fin():
                    o3s = psOs[:, 0:260].rearrange("p (q d) -> p q d", d=65)
                    o3w = psOw[:, 0:260].rearrange("p (q d) -> p q d", d=65)
                    rs_ = rsw[:, ob * 8:ob * 8 + 4]
                    rw_ = rsw[:, ob * 8 + 4:ob * 8 + 8]
                    P.op("dve", lambda e: e.reciprocal(out=rs_, in_=o3s[:, :, 64]), reads=[bOs], writes=[brsw[ob]])
                    P.op("dve", lambda e: e.reciprocal(out=rw_, in_=o3w[:, :, 64]), reads=[bOw], writes=[brsw[ob]])
                    P.op("dve", lambda e: e.tensor_tensor(out=rs_, in0=rs_, in1=gates[:, 4 * c:4 * c + 4, 3 * h + 1], op=ALU.mult),
                         reads=[brsw[ob]] + bgates[4 * c:4 * c + 4], writes=[brsw[ob]])
                    P.op("dve", lambda e: e.tensor_tensor(out=rw_, in0=rw_, in1=gates[:, 4 * c:4 * c + 4, 3 * h + 2], op=ALU.mult),
                         reads=[brsw[ob]] + bgates[4 * c:4 * c + 4], writes=[brsw[ob]])
                    for oi in range(4):
                        qb = 4 * c + oi
                        tb = oi % 2
                        P.op("dve", lambda e, oi=oi, qb=qb, tb=tb: e.scalar_tensor_tensor(
                            out=tmpo[tb], in0=psOs[:, oi * 65:oi * 65 + 64], scalar=rsw[:, ob * 8 + oi:ob * 8 + oi + 1],
                            in1=ocmp[:, qb, r, :], op0=ALU.mult, op1=ALU.add),
                             reads=[bOs, brsw[ob], bocmp[qb]], writes=[btmpo[tb]])
                        P.op("dve", lambda e, oi=oi, qb=qb, tb=tb: e.scalar_tensor_tensor(
                            out=o_all[:, qb, h * 64:(h + 1) * 64], in0=psOw[:, oi * 65:oi * 65 + 64],
                            scalar=rsw[:, ob * 8 + 4 + oi:ob * 8 + 4 + oi + 1], in1=tmpo[tb], op0=ALU.mult, op1=ALU.add),
                             reads=[bOw, brsw[ob], btmpo[tb]], writes=[bo_all[qb]])
                return fin

            for r in range(4):
                h = 4 * g + r
                for c in range(4):
                    ob = st["O"] % 2
                    st["O"] += 1
                    psOs, bOs = ps[2 + ob], pb[2 + ob]
                    psOw, bOw = ps[4 + ob], pb[4 + ob]
                    add_branch(0, r, h, c, psOs, bOs, None)
                    add_branch(1, r, h, c, psOw, bOw, make_fin(r, h, c, ob, psOs, bOs, psOw, bOw))
            run_pipeline(stages5, 3)
            P.barrier()
            A.release()
            dbg(cx, 7)
        A.mark()
        wo = A.alloc([8, 1024], BF16)
        bwo = P.buf("wo")
        P.dma("pool", [(wo, cx.inp["nsa_w_o"][j].rearrange("(kc p) c -> p kc c", p=128))], writes=[bwo])
        oT = [A.alloc([8, 128], BF16) for _ in range(3)]
        boT = P.bufs(3, "oT")
        hr = [A.alloc([D], F32) for _ in range(3)]
        bhr = P.bufs(3, "hr")
        stC = []
        for qb in range(16):
            def frontC(qb=qb):
                out_proj_tile(cx, o_all[:, qb, :], bo_all[qb], wo, bwo, src, dst, tok0 + qb * 128, oT, boT, hr, bhr, qb % 3,
                              ps[6 + qb % 2], pb[6 + qb % 2], ps[0:2], pb[0:2], part=0)

            def backC(qb=qb):
                out_proj_tile(cx, o_all[:, qb, :], bo_all[qb], wo, bwo, src, dst, tok0 + qb * 128, oT, boT, hr, bhr, qb % 3,
                              ps[6 + qb % 2], pb[6 + qb % 2], ps[0:2], pb[0:2], part=1)
            stC.append((frontC, backC))
        run_pipeline(stC, 1)
        P.barrier()
        A.release()
    A.release()


def final_norm_phase(cx, src, dst):
    P, A = cx.P, cx.A
    A.mark()
    gain, bgain = load_gain(cx, cx.inp["final_norm"], D)
    hn = [A.alloc([D], F32) for _ in range(2)]
    bhn = P.bufs(2, "fhn")
    yo = [A.alloc([D], F32) for _ in range(2)]
    byo = P.bufs(2, "fyo")
    junk = A.alloc([D], BF16)
    bjunk = P.buf("junk")
    ss = A.alloc([8], F32)
    bss = P.bufs(8, "ss")
    for i in range(T // 128):
        sl = i % 2
        c = i % 8
        P.dma("sp", [(hn[sl], src[i * 128:(i + 1) * 128, :])], writes=[bhn[sl]], key="fhn%d" % sl)
        rms_rstd(cx, hn[sl], bhn[sl], D, ss, bss[c], c, junk, bjunk)
        P.op("dve", lambda e, sl=sl, c=c: e.scalar_tensor_tensor(out=yo[sl], in0=hn[sl], scalar=ss[:, c:c + 1], in1=gain,
                                                                 op0=ALU.mult, op1=ALU.mult),
             reads=[bhn[sl], bss[c], bgain], writes=[byo[sl]])
        P.dma("sp", [(dst[i * 128:(i + 1) * 128, :], yo[sl])], reads=[byo[sl]], key="fyo%d" % sl)
    P.barrier()
    A.release()


INPUT_SHAPES = {
    "ffn_norm_a": (DEPTH, D), "ffn_a_w_gate": (DEPTH, D, FF), "ffn_a_w_up": (DEPTH, D, FF), "ffn_a_w_down": (DEPTH, FF, D),
    "mix_norm": (DEPTH, D), "ffn_norm_b": (DEPTH, D), "ffn_b_w_gate": (DEPTH, D, FF), "ffn_b_w_up": (DEPTH, D, FF),
    "ffn_b_w_down": (DEPTH, FF, D), "final_norm": (D,), "rel_bias": (32, 16),
    "mla_w_in": (2, D, 672), "mla_q_norm": (2, 384), "mla_kv_norm": (2, 256), "mla_w_uq": (2, 384, 1536),
    "mla_w_ukv": (2, 256, 2048), "mla_w_o": (2, 1024, 1024),
    "nsa_w_in": (2, D, 2608), "nsa_cmp_pos_k": (2, 32, 64), "nsa_cmp_w1_k": (2, 2048, 128), "nsa_cmp_w2_k": (2, 128, 64),
    "nsa_cmp_pos_v": (2, 32, 64), "nsa_cmp_w1_v": (2, 2048, 128), "nsa_cmp_w2_v": (2, 128, 64), "nsa_w_o": (2, 1024, 1024),
}
ARENA_WORDS = 50944


def host_consts():
    c = {}
    c["c_ident"] = np.eye(128, dtype=np.float32).astype(ml_dtypes.bfloat16)
    half = 16
    inv = (np.float32(10000.0) ** (-np.arange(half, dtype=np.float32) * np.float32(2.0) / np.float32(32))).astype(np.float32)
    ang = np.arange(S, dtype=np.float32)[:, None] * inv[None, :]
    cos, sin = np.cos(ang).astype(np.float32), np.sin(ang).astype(np.float32)
    c["c_rope_cos"] = np.ascontiguousarray(np.concatenate([cos.T, cos.T], axis=0))
    c["c_rope_sin"] = np.ascontiguousarray(np.concatenate([sin.T, sin.T], axis=0))
    k = np.arange(128)[:, None]
    t = np.arange(128)[None, :]
    c["c_mdiag"] = np.where(t >= k, 0.0, NEGM).astype(np.float32)
    c["c_m4"] = np.where(t < k, 0.0, NEGM).astype(np.float32)
    def bucket(dist):
        n = np.maximum(dist, 0)
        nf = np.maximum(n, 1).astype(np.float32)
        large = 16 + (np.log(nf / np.float32(16)) / np.float32(math.log(8.0)) * np.float32(16)).astype(np.int32)
        return np.where(n < 16, n, np.minimum(large, 31))
    i = np.arange(4096)
    dist = 2047 - i
    oh = np.zeros((33, 4096), np.float32)
    bk = bucket(dist)
    oh[bk[dist >= 0], i[dist >= 0]] = 1.0
    oh[32, i[dist < 0]] = 1.0
    c["c_oh"] = oh
    kk = np.arange(S)
    c["c_bexp"] = (kk[None, :] // 64 == np.arange(32)[:, None]).astype(np.float32).astype(ml_dtypes.bfloat16)
    cs_ = np.arange(127) * 16
    ce_ = cs_ + 32
    ss_ = np.arange(32) * 64
    se_ = ss_ + 64
    ov = np.minimum(ce_[:, None], se_[None, :]) - np.maximum(cs_[:, None], ss_[None, :])
    c["c_ov"] = (np.clip(ov, 0, None) / 32.0).astype(np.float32).astype(ml_dtypes.bfloat16)
    tl = np.arange(128)[:, None, None]
    qb_ = np.arange(16)[None, :, None]
    jj = np.arange(32)[None, None, :]
    blk = (qb_ * 128 + tl) // 64
    forced = (jj == 0) | (jj == blk) | (jj == blk - 1)
    causal = jj <= blk
    c["c_cm"] = (causal & ~forced).astype(np.float32)
    c["c_add"] = np.where(forced, 1e6, np.where(causal, 0.0, -1e6)).astype(np.float32)
    return c


def default_phases():
    ph = []
    for i in range(DEPTH):
        ph.append(("ffn", i, "a"))
        ph.append(("mla", i // 2) if i % 2 == 0 else ("nsa", i // 2))
        ph.append(("ffn", i, "b"))
    ph.append(("final",))
    return ph


def build_program(phases, dbg_stop=None):
    nc = bass.Bass("TRN2", target_bir_lowering=False)
    cx = Ctx()
    cx.dbg_stop = dbg_stop
    cx.nc = nc
    cx.inp = {}
    cx.cd = {}
    x_in = nc.dram_tensor("x", [T, D], F32, kind="ExternalInput").ap()
    for name, shp in INPUT_SHAPES.items():
        cx.inp[name] = nc.dram_tensor(name, list(shp), F32, kind="ExternalInput").ap()
    consts = host_consts()
    cd = cx.cd
    for name, arr in consts.items():
        dt = BF16 if arr.dtype == ml_dtypes.bfloat16 else F32
        cd[name] = nc.dram_tensor(name, list(arr.shape), dt, kind="ExternalInput").ap()
    y = nc.dram_tensor("y", [T, D], F32, kind="ExternalOutput").ap()
    hbuf = nc.dram_tensor("hbuf", [T, D], F32).ap()
    with ExitStack() as es:
        big = es.enter_context(nc.sbuf_tensor("big", [128, ARENA_WORDS], F32))
        cx.ps = [es.enter_context(nc.psum_tensor("ps%d" % i, [128, 512], F32))[:, :] for i in range(8)]
        P = Prog(nc, es)
        cx.P = P
        cx.pb = P.bufs(8, "psb")
        cx.A = Arena(big, ARENA_WORDS)
        cx.ident = cx.A.alloc([128], BF16)
        cx.bident = P.buf("ident")
        P.dma("sp", [(cx.ident, cd["c_ident"])], writes=[cx.bident])
        cur = x_in
        try:
          for ph in phases:
            if ph[0] == "ffn":
                ffn_phase(cx, ph[1], ph[2], cur, hbuf)
                cur = hbuf
            elif ph[0] == "mla":
                mla_phase(cx, ph[1], 2 * ph[1], cur, hbuf)
                cur = hbuf
            elif ph[0] == "nsa":
                if not getattr(cx, "nsa_ready", False):
                    nsa_setup(cx)
                    cx.nsa_ready = True
                nsa_phase(cx, ph[1], 2 * ph[1] + 1, cur, hbuf)
                cur = hbuf
            elif ph[0] == "final":
                final_norm_phase(cx, cur, y)
                cur = y
            else:
                raise NotImplementedError(ph)
        except StopBuild:
            cur = x_in
        if cur is not y:
            P.dma("sp", [(y, cur)], key="ycopy")
            P.barrier()
        P.run_block()
    cx.consts = consts
    return nc, cx


_CACHE = {}


def kernel(**inputs):
    phases = default_phases()
    key = "full"
    if key not in _CACHE:
        _CACHE[key] = build_program(phases)
    nc, cx = _CACHE[key]
    x = np.ascontiguousarray(np.asarray(inputs["x"], dtype=np.float32)).reshape(N_CORES, T, D)
    shared = {k: np.ascontiguousarray(np.asarray(inputs[k], dtype=np.float32)) for k in INPUT_SHAPES}
    shared.update(cx.consts)
    in_maps = []
    for c in range(N_CORES):
        m = dict(shared)
        m["x"] = x[c]
        in_maps.append(m)
    res = run_bass_kernel_spmd(nc, in_maps, core_ids=list(range(N_CORES)))
    out = np.stack([np.asarray(r["y"], dtype=np.float32) for r in res.results], axis=0)
    return out.reshape(16, S, D)
```
